# Optimizing a Trainium2 kernel written in Bass

```python
import math
import jax
import jax.numpy as jnp
from jax import lax
import numpy as np

D_MODEL = 2048
BATCH = 1
SEQ = 8192
DEPTH = 4

GRID_W = 64
CTX_LEN = 256
N_MIXERS = 3
EPS = 1e-6
HALF_STEP = 0.5
N_MOD = 9
D_FF = 5632

NA_HEADS = 16
NA_HEAD_DIM = 128
NA_WIN_ROWS = 8
NA_WIN_COLS = 16

SSM_D_INNER = 2 * D_MODEL
SSM_HEAD_DIM = 64
SSM_HEADS = SSM_D_INNER // SSM_HEAD_DIM
SSM_GROUPS = 8
SSM_STATE = 128
SSM_CONV_W = 5
SSM_CHUNK = 128
SSM_GN = SSM_GROUPS * SSM_STATE
SSM_CONV_DIM = SSM_D_INNER + 2 * SSM_GN
SSM_IN_DIM = SSM_D_INNER + SSM_CONV_DIM + 2 * SSM_HEADS

GQA_HEADS = 16
GQA_KV_HEADS = 4
GQA_HEAD_DIM = 128
ROPE_THETA = 10000.0
Q_BLOCK = 128

N_A = (DEPTH + 2) // 3
N_B = (DEPTH + 1) // 3
N_C = DEPTH // 3

kernel_name = "hybrid_interleaved_na_ssd_gqa_macaron_prefix"


def rmsnorm(x, g):
    xf = x.astype(jnp.float32)
    var = jnp.mean(xf * xf, axis=-1, keepdims=True)
    return (xf * lax.rsqrt(var + EPS)).astype(x.dtype) * g


def modulate(x, shift, scale):
    return x * (1 + scale) + shift


def swiglu(x, w_in, w_out):
    a, b = jnp.split(x @ w_in, 2, axis=-1)
    return (jax.nn.silu(a) * b) @ w_out


def axial_rope(n_tok, head_dim):
    t = jnp.arange(n_tok, dtype=jnp.int32)
    row = (t // GRID_W).astype(jnp.float32)
    col = (t % GRID_W).astype(jnp.float32)
    n_freq = head_dim // 4
    inv_freq = ROPE_THETA ** (-jnp.arange(n_freq, dtype=jnp.float32) / n_freq)
    ang = jnp.concatenate([row[:, None] * inv_freq, col[:, None] * inv_freq], axis=-1)
    return jnp.cos(ang), jnp.sin(ang)


def apply_rope(x, cos, sin):
    half = x.shape[-1] // 2
    shape = (1, x.shape[1]) + (1,) * (x.ndim - 3) + (half,)
    cos = cos.reshape(shape).astype(x.dtype)
    sin = sin.reshape(shape).astype(x.dtype)
    x1, x2 = x[..., :half], x[..., half:]
    return jnp.concatenate([x1 * cos - x2 * sin, x1 * sin + x2 * cos], axis=-1)


def attend(q, k, v):
    scale = q.shape[-1] ** -0.5
    s = jnp.einsum('bqkgd,bskd->bkgqs', q, k, preferred_element_type=jnp.float32) * scale
    p = jax.nn.softmax(s, axis=-1).astype(v.dtype)
    return jnp.einsum('bkgqs,bskd->bqkgd', p, v)


def neighbourhood_attention(u, uc, w_qkv, rpb, w_o, with_ctx_out):
    bsz, n_tok, _ = u.shape
    n_ctx = uc.shape[1]
    rows = n_tok // GRID_W
    wr = min(NA_WIN_ROWS, rows)
    wc = NA_WIN_COLS
    n_win = wr * wc
    scale = NA_HEAD_DIM ** -0.5
    qkv = (u @ w_qkv).reshape(bsz, n_tok, 3, NA_HEADS, NA_HEAD_DIM)
    qkv_c = (uc @ w_qkv).reshape(bsz, n_ctx, 3, NA_HEADS, NA_HEAD_DIM)
    qc, kc, vc = qkv_c[:, :, 0], qkv_c[:, :, 1], qkv_c[:, :, 2]
    grid = (bsz, rows, GRID_W, NA_HEADS, NA_HEAD_DIM)
    q_grid = qkv[:, :, 0].reshape(grid)
    k_grid = qkv[:, :, 1].reshape(grid)
    v_grid = qkv[:, :, 2].reshape(grid)
    row_start = jnp.clip(jnp.arange(rows) - wr // 2, 0, rows - wr)
    col_start = jnp.clip(jnp.arange(GRID_W) - wc // 2, 0, GRID_W - wc)
    col_idx = col_start[:, None] + jnp.arange(wc)[None, :]
    dcol = col_idx - jnp.arange(GRID_W)[:, None]
    bias_cols = rpb[:, :, dcol + wc - 1]

    def row_block(r):
        rs = row_start[r]
        q_r = q_grid[:, r]
        k_band = lax.dynamic_slice_in_dim(k_grid, rs, wr, axis=1)
        v_band = lax.dynamic_slice_in_dim(v_grid, rs, wr, axis=1)
        win_shape = (bsz, GRID_W, n_win, NA_HEADS, NA_HEAD_DIM)
        k_win = k_band[:, :, col_idx].transpose(0, 2, 1, 3, 4, 5).reshape(win_shape)
        v_win = v_band[:, :, col_idx].transpose(0, 2, 1, 3, 4, 5).reshape(win_shape)
        drow = rs + jnp.arange(wr) - r
        bias = bias_cols[:, drow + NA_WIN_ROWS - 1].transpose(0, 2, 1, 3).reshape(NA_HEADS, GRID_W, n_win)
        s_win = jnp.einsum('bqhd,bqkhd->bhqk', q_r, k_win, preferred_element_type=jnp.float32) * scale + bias
        s_ctx = jnp.einsum('bqhd,bkhd->bhqk', q_r, kc, preferred_element_type=jnp.float32) * scale
        p = jax.nn.softmax(jnp.concatenate([s_win, s_ctx], axis=-1), axis=-1).astype(v_win.dtype)
        return (jnp.einsum('bhqk,bqkhd->bqhd', p[..., :n_win], v_win)
                + jnp.einsum('bhqk,bkhd->bqhd', p[..., n_win:], vc))

    o = lax.map(row_block, jnp.arange(rows))
    y = jnp.moveaxis(o, 0, 1).reshape(bsz, n_tok, NA_HEADS * NA_HEAD_DIM) @ w_o
    yc = None
    if with_ctx_out:
        oc = attend(qc[:, :, :, None], kc, vc)
        yc = oc.reshape(bsz, n_ctx, NA_HEADS * NA_HEAD_DIM) @ w_o
    return y, yc


def centred_dwconv(x, w, b):
    pad = (SSM_CONV_W - 1) // 2
    y = lax.conv_general_dilated(x, w[:, None, :], window_strides=(1,), padding=[(pad, pad)],
                                 dimension_numbers=('NWC', 'WIO', 'NWC'), feature_group_count=x.shape[-1])
    return y + b


def segsum(x):
    t = x.shape[-1]
    cs = jnp.cumsum(x, axis=-1)
    seg = cs[..., :, None] - cs[..., None, :]
    return jnp.where(jnp.tril(jnp.ones((t, t), dtype=bool)), seg, -jnp.inf)


def ssd_chunked(x, dt, a, bm, cm, init_state):
    bsz, length, n_heads, p = x.shape
    g, n = bm.shape[2], bm.shape[3]
    r = n_heads // g
    q = SSM_CHUNK
    nc = length // q
    f32 = jnp.float32
    dt = dt.astype(f32)
    xdt = (x.astype(f32) * dt[..., None]).reshape(bsz, nc, q, g, r, p)
    da = (dt * a.astype(f32)).reshape(bsz, nc, q, g, r).transpose(0, 3, 4, 1, 2)
    bc = bm.astype(f32).reshape(bsz, nc, q, g, n)
    cc = cm.astype(f32).reshape(bsz, nc, q, g, n)
    a_cs = jnp.cumsum(da, axis=-1)
    decay_in = jnp.exp(segsum(da))
    cb = jnp.einsum('bclgn,bcsgn->bgcls', cc, bc)
    y_diag = jnp.einsum('bgcls,bgrcls,bcsgrp->bclgrp', cb, decay_in, xdt)
    decay_to_end = jnp.exp(a_cs[..., -1:] - a_cs)
    chunk_states = jnp.einsum('bcsgn,bgrcs,bcsgrp->bcgrpn', bc, decay_to_end, xdt)
    chunk_states = jnp.concatenate([init_state.astype(f32).reshape(bsz, 1, g, r, p, n), chunk_states], axis=1)
    chunk_decay = jnp.exp(segsum(jnp.pad(a_cs[..., -1], ((0, 0), (0, 0), (0, 0), (1, 0)))))
    states = jnp.einsum('bgrzc,bcgrpn->bzgrpn', chunk_decay, chunk_states)
    y_off = jnp.einsum('bclgn,bcgrpn,bgrcl->bclgrp', cc, states[:, :-1], jnp.exp(a_cs))
    y = (y_diag + y_off).reshape(bsz, length, n_heads, p)
    return y, states[:, -1].reshape(bsz, n_heads, p, n)


def mamba2_bidirectional(u, uc, w_in, conv_w, conv_b, a_log, dt_bias, d_skip, norm_g, w_out):
    bsz, n_tok, _ = u.shape

    def project(h):
        length = h.shape[1]
        zxbcdt = h @ w_in
        z = zxbcdt[..., :SSM_D_INNER]
        xbc = jax.nn.silu(centred_dwconv(zxbcdt[..., SSM_D_INNER:SSM_D_INNER + SSM_CONV_DIM], conv_w, conv_b))
        dt = zxbcdt[..., SSM_D_INNER + SSM_CONV_DIM:].reshape(bsz, length, 2, SSM_HEADS)
        xs = xbc[..., :SSM_D_INNER].reshape(bsz, length, SSM_HEADS, SSM_HEAD_DIM)
        bm = xbc[..., SSM_D_INNER:SSM_D_INNER + SSM_GN].reshape(bsz, length, SSM_GROUPS, SSM_STATE)
        cm = xbc[..., SSM_D_INNER + SSM_GN:].reshape(bsz, length, SSM_GROUPS, SSM_STATE)
        return z, xs, bm, cm, dt

    z, xs, bm, cm, dt = project(u)
    zc, xsc, bmc, cmc, dtc = project(uc)
    a = -jnp.exp(a_log.astype(jnp.float32))
    init = jnp.zeros((bsz, SSM_HEADS, SSM_HEAD_DIM, SSM_STATE), jnp.float32)
    y_parts, yc_parts = [], []
    for direction in range(2):
        if direction == 0:
            flip = lambda t: t
        else:
            flip = lambda t: jnp.flip(t, axis=1)
        delta = jax.nn.softplus(dt[..., direction, :].astype(jnp.float32) + dt_bias[direction])
        delta_c = jax.nn.softplus(dtc[..., direction, :].astype(jnp.float32) + dt_bias[direction])
        y_c, s_c = ssd_chunked(flip(xsc), flip(delta_c), a[direction], flip(bmc), flip(cmc), init)
        y_l, _ = ssd_chunked(flip(xs), flip(delta), a[direction], flip(bm), flip(cm), s_c)
        y_parts.append(flip(y_l) + d_skip[direction][:, None] * xs)
        yc_parts.append(flip(y_c) + d_skip[direction][:, None] * xsc)
    y = (y_parts[0] + y_parts[1]).astype(u.dtype).reshape(bsz, n_tok, SSM_D_INNER)
    yc = (yc_parts[0] + yc_parts[1]).astype(u.dtype).reshape(bsz, uc.shape[1], SSM_D_INNER)
    y = rmsnorm(y * jax.nn.silu(z), norm_g) @ w_out
    yc = rmsnorm(yc * jax.nn.silu(zc), norm_g) @ w_out
    return y, yc


def gqa_attention(u, uc, w_qkv, q_norm, k_norm, w_o, cos, sin, with_ctx_out):
    bsz, n_tok, _ = u.shape
    n_ctx = uc.shape[1]
    grp = GQA_HEADS // GQA_KV_HEADS
    nq = GQA_HEADS * GQA_HEAD_DIM
    nkv = GQA_KV_HEADS * GQA_HEAD_DIM

    def project(h):
        length = h.shape[1]
        p = h @ w_qkv
        q = rmsnorm(p[..., :nq].reshape(bsz, length, GQA_KV_HEADS, grp, GQA_HEAD_DIM), q_norm)
        k = rmsnorm(p[..., nq:nq + nkv].reshape(bsz, length, GQA_KV_HEADS, GQA_HEAD_DIM), k_norm)
        v = p[..., nq + nkv:].reshape(bsz, length, GQA_KV_HEADS, GQA_HEAD_DIM)
        return q, k, v

    q, k, v = project(u)
    qc, kc, vc = project(uc)
    q = apply_rope(q, cos, sin)
    k = apply_rope(k, cos, sin)
    k_all = jnp.concatenate([k, kc], axis=1)
    v_all = jnp.concatenate([v, vc], axis=1)
    nb = n_tok // Q_BLOCK
    q_blocks = jnp.moveaxis(q.reshape(bsz, nb, Q_BLOCK, GQA_KV_HEADS, grp, GQA_HEAD_DIM), 1, 0)
    o = lax.map(lambda qb: attend(qb, k_all, v_all), q_blocks)
    y = jnp.moveaxis(o, 0, 1).reshape(bsz, n_tok, nq) @ w_o
    yc = None
    if with_ctx_out:
        yc = attend(qc, kc, vc).reshape(bsz, n_ctx, nq) @ w_o
    return y, yc


def setup_inputs(seed: int = 0) -> dict:
    key = jax.random.key(seed)
    ks = jax.random.split(key, 25)
    f32 = jnp.float32
    D = D_MODEL

    def nrm(k, shape, scale):
        return jax.random.normal(k, shape, f32) * scale

    x = nrm(ks[0], (BATCH, SEQ, D), 1.0)
    c = nrm(ks[1], (BATCH, D), 1.0)
    ctx = nrm(ks[2], (BATCH, CTX_LEN, D), 1.0)
    c_ctx = nrm(ks[3], (D,), 1.0)
    ada_w = nrm(ks[4], (DEPTH, D, N_MOD * D), 0.5 * D ** -0.5)
    ada_b = nrm(ks[5], (DEPTH, N_MOD * D), 0.02)
    norm_g = 1.0 + nrm(ks[6], (DEPTH, 3, D), 0.02)
    ffn_w_in = nrm(ks[7], (DEPTH, 2, D, 2 * D_FF), D ** -0.5)
    ffn_w_out = nrm(ks[8], (DEPTH, 2, D_FF, D), D_FF ** -0.5)
    na_w_qkv = nrm(ks[9], (N_A, D, 3 * NA_HEADS * NA_HEAD_DIM), D ** -0.5)
    na_rpb = nrm(ks[10], (N_A, NA_HEADS, 2 * NA_WIN_ROWS - 1, 2 * NA_WIN_COLS - 1), 0.1)
    na_w_o = nrm(ks[11], (N_A, NA_HEADS * NA_HEAD_DIM, D), (NA_HEADS * NA_HEAD_DIM) ** -0.5)
    ssm_w_in = nrm(ks[12], (N_B, D, SSM_IN_DIM), D ** -0.5)
    ssm_conv_w = nrm(ks[13], (N_B, SSM_CONV_W, SSM_CONV_DIM), SSM_CONV_W ** -0.5)
    ssm_conv_b = nrm(ks[14], (N_B, SSM_CONV_DIM), 0.02)
    ssm_a_log = jnp.log(jax.random.uniform(ks[15], (N_B, 2, SSM_HEADS), f32, 1.0, 16.0))
    dt0 = jnp.exp(jax.random.uniform(ks[16], (N_B, 2, SSM_HEADS), f32, math.log(1e-3), math.log(1e-1)))
    ssm_dt_bias = dt0 + jnp.log(-jnp.expm1(-dt0))
    ssm_d = 1.0 + nrm(ks[17], (N_B, 2, SSM_HEADS), 0.1)
    ssm_norm_g = 1.0 + nrm(ks[18], (N_B, SSM_D_INNER), 0.02)
    ssm_w_out = nrm(ks[19], (N_B, SSM_D_INNER, D), SSM_D_INNER ** -0.5)
    gqa_w_qkv = nrm(ks[20], (N_C, D, (GQA_HEADS + 2 * GQA_KV_HEADS) * GQA_HEAD_DIM), D ** -0.5)
    gqa_q_norm = 1.0 + nrm(ks[21], (N_C, GQA_HEAD_DIM), 0.02)
    gqa_k_norm = 1.0 + nrm(ks[22], (N_C, GQA_HEAD_DIM), 0.02)
    gqa_w_o = nrm(ks[23], (N_C, GQA_HEADS * GQA_HEAD_DIM, D), (GQA_HEADS * GQA_HEAD_DIM) ** -0.5)
    final_norm_g = 1.0 + nrm(ks[24], (D,), 0.02)
    return {"x": x, "c": c, "ctx": ctx, "c_ctx": c_ctx, "ada_w": ada_w, "ada_b": ada_b,
            "norm_g": norm_g, "ffn_w_in": ffn_w_in, "ffn_w_out": ffn_w_out,
            "na_w_qkv": na_w_qkv, "na_rpb": na_rpb, "na_w_o": na_w_o,
            "ssm_w_in": ssm_w_in, "ssm_conv_w": ssm_conv_w, "ssm_conv_b": ssm_conv_b,
            "ssm_a_log": ssm_a_log, "ssm_dt_bias": ssm_dt_bias, "ssm_d": ssm_d,
            "ssm_norm_g": ssm_norm_g, "ssm_w_out": ssm_w_out,
            "gqa_w_qkv": gqa_w_qkv, "gqa_q_norm": gqa_q_norm, "gqa_k_norm": gqa_k_norm, "gqa_w_o": gqa_w_o,
            "final_norm_g": final_norm_g}


def reference(x, c, ctx, c_ctx, ada_w, ada_b, norm_g, ffn_w_in, ffn_w_out, na_w_qkv, na_rpb, na_w_o,
              ssm_w_in, ssm_conv_w, ssm_conv_b, ssm_a_log, ssm_dt_bias, ssm_d, ssm_norm_g, ssm_w_out,
              gqa_w_qkv, gqa_q_norm, gqa_k_norm, gqa_w_o, final_norm_g):
    bsz, n_tok, d = x.shape
    cos, sin = axial_rope(n_tok, GQA_HEAD_DIM)
    s_lat = jax.nn.silu(c)
    s_ctx = jax.nn.silu(c_ctx)[None]
    h, hc = x, ctx
    for i in range(DEPTH):
        last = i == DEPTH - 1
        mod = (s_lat @ ada_w[i] + ada_b[i]).reshape(bsz, N_MOD, 1, d)
        mod_c = (s_ctx @ ada_w[i] + ada_b[i]).reshape(1, N_MOD, 1, d)
        h = h + HALF_STEP * mod[:, 2] * swiglu(modulate(rmsnorm(h, norm_g[i, 0]), mod[:, 0], mod[:, 1]),
                                               ffn_w_in[i, 0], ffn_w_out[i, 0])
        hc = hc + HALF_STEP * mod_c[:, 2] * swiglu(modulate(rmsnorm(hc, norm_g[i, 0]), mod_c[:, 0], mod_c[:, 1]),
                                                   ffn_w_in[i, 0], ffn_w_out[i, 0])
        u = modulate(rmsnorm(h, norm_g[i, 1]), mod[:, 3], mod[:, 4])
        uc = modulate(rmsnorm(hc, norm_g[i, 1]), mod_c[:, 3], mod_c[:, 4])
        kind, j = i % N_MIXERS, i // N_MIXERS
        if kind == 0:
            y, yc = neighbourhood_attention(u, uc, na_w_qkv[j], na_rpb[j], na_w_o[j], not last)
        elif kind == 1:
            y, yc = mamba2_bidirectional(u, uc, ssm_w_in[j], ssm_conv_w[j], ssm_conv_b[j], ssm_a_log[j],
                                         ssm_dt_bias[j], ssm_d[j], ssm_norm_g[j], ssm_w_out[j])
        else:
            y, yc = gqa_attention(u, uc, gqa_w_qkv[j], gqa_q_norm[j], gqa_k_norm[j], gqa_w_o[j], cos, sin, not last)
        h = h + mod[:, 5] * y
        h = h + HALF_STEP * mod[:, 8] * swiglu(modulate(rmsnorm(h, norm_g[i, 2]), mod[:, 6], mod[:, 7]),
                                               ffn_w_in[i, 1], ffn_w_out[i, 1])
        if not last:
            hc = hc + mod_c[:, 5] * yc
            hc = hc + HALF_STEP * mod_c[:, 8] * swiglu(modulate(rmsnorm(hc, norm_g[i, 2]), mod_c[:, 6], mod_c[:, 7]),
                                                       ffn_w_in[i, 1], ffn_w_out[i, 1])
    return rmsnorm(h, final_norm_g)
```

```python
from contextlib import ExitStack
import numpy as np
import concourse.bass as bass
import concourse.mybir as mybir
from concourse.alu_op_type import AluOpType as ALU
from concourse.bass_utils import run_bass_kernel_spmd

F32 = mybir.dt.float32
BF16 = mybir.dt.bfloat16
AF = mybir.ActivationFunctionType

NCORES = 8
D = 2048
DFF = 5632
SEQ = 8192
CTX = 256
TL = SEQ // NCORES
TC = CTX // NCORES
T = TL + TC
EPS = 1e-6
TILES = [(0, 512), (512, 512), (1024, TC)]
NEG = -30000.0


class Dep:
    __slots__ = ("w", "r")

    def __init__(self):
        self.w = None
        self.r = {}


class Sched:
    def __init__(self, nc, dma_ring=6):
        self.nc = nc
        self.engs = {"pe": nc.tensor, "act": nc.scalar, "dve": nc.vector,
                     "pool": nc.gpsimd, "sp": nc.sync}
        self.sems = []
        self.eng_sem = {}
        self.eng_cnt = {}
        self.seen = {e: {} for e in self.engs}
        self.ring = {}
        self.ring_pos = {}
        self.sem_val = {}
        for e in self.engs:
            self.eng_sem[e] = self._new_sem("c_" + e)
            self.eng_cnt[e] = 0
        for e in ("sp", "act", "pool"):
            self.ring[e] = [self._new_sem(f"d_{e}{i}") for i in range(dma_ring)]
            self.ring_pos[e] = 0
        self.n_wait = 0
        self.n_inst = 0
        self.stack = ExitStack()
        self._tn = 0

    def sb(self, shape, dt, name=None):
        self._tn += 1
        return self.stack.enter_context(self.nc.sbuf_tensor(name or f"sb{self._tn}", list(shape), dt))

    def ps(self, shape, dt=F32, name=None):
        self._tn += 1
        return self.stack.enter_context(self.nc.psum_tensor(name or f"ps{self._tn}", list(shape), dt))

    def _new_sem(self, name):
        h = self.nc.alloc_semaphore(name=name)
        self.sems.append(h)
        self.sem_val[len(self.sems) - 1] = 0
        return len(self.sems) - 1

    def _need(self, reads, writes):
        need = {}
        for d in reads:
            if d.w is not None:
                s, v = d.w
                if need.get(s, 0) < v:
                    need[s] = v
        for d in writes:
            if d.w is not None:
                s, v = d.w
                if need.get(s, 0) < v:
                    need[s] = v
            for s, v in d.r.items():
                if need.get(s, 0) < v:
                    need[s] = v
        return need

    def _emit_waits(self, eng, need):
        E = self.engs[eng]
        seen = self.seen[eng]
        own = self.eng_sem[eng]
        for s, v in need.items():
            if eng == "pe" and s == own:
                continue
            if seen.get(s, 0) < v:
                E.wait_ge(self.sems[s], v)
                seen[s] = v
                self.n_wait += 1

    def _record(self, ev, reads, writes):
        s, v = ev
        for d in reads:
            d.r[s] = v
        for d in writes:
            d.w = ev
            d.r = {}

    def op(self, eng, fn, reads=(), writes=()):
        need = self._need(reads, writes)
        self._emit_waits(eng, need)
        inst = fn(self.engs[eng])
        s = self.eng_sem[eng]
        self.eng_cnt[eng] += 1
        v = self.eng_cnt[eng]
        inst.then_inc(self.sems[s], 1)
        self._record((s, v), reads, writes)
        self.n_inst += 1
        return (s, v)

    def dma(self, eng, out, in_, reads=(), writes=(), **kw):
        need = self._need(reads, writes)
        ring = self.ring[eng]
        s = ring[self.ring_pos[eng] % len(ring)]
        self.ring_pos[eng] += 1
        prev = self.sem_val[s]
        if prev > 0:
            need[s] = max(need.get(s, 0), prev)
        self._emit_waits(eng, need)
        inst = self.engs[eng].dma_start(out=out, in_=in_, **kw)
        inst.then_inc(self.sems[s], 16)
        self.sem_val[s] = prev + 16
        ev = (s, prev + 16)
        self._record(ev, reads, writes)
        self.n_inst += 1
        return ev

    def finish(self, out_deps):
        need = self._need(out_deps, out_deps)
        self._emit_waits("sp", need)
        self.stack.close()


def new_nc():
    return bass.Bass("TRN2", target_bir_lowering=False)


class RR:
    def __init__(self, items):
        self.items = items
        self.i = 0

    def next(self):
        it = self.items[self.i % len(self.items)]
        self.i += 1
        return it


def emit_norm_mod(S, h, dh, xn, dxn, gm_l, sh_l, gm_c, sh_c, dmv, ones, done, ps_stat, dps_stat,
                  nch=16, tiles=TILES, eng_sq="pool"):
    sq = [(S.sb([128, 512], F32), Dep()) for _ in range(2)]
    sqr = RR(sq)
    rt = S.sb([128, 512], F32)
    drt = Dep()
    tmp = RR([(S.sb([128, 512], F32), Dep()) for _ in range(2)])
    for ti, (t0, n) in enumerate(tiles):
        gm, sh = (gm_l, sh_l) if ti < len(tiles) - 1 else (gm_c, sh_c)
        for c in range(nch):
            sqb, dsq = sqr.next()
            S.op(eng_sq, lambda e: e.tensor_tensor(out=sqb[:, :n], in0=h[:, c, t0:t0 + n], in1=h[:, c, t0:t0 + n], op=ALU.mult),
                 reads=[dh[ti]], writes=[dsq])
            S.op("pe", lambda e: e.matmul(ps_stat[:, :n], lhsT=ones[:, :], rhs=sqb[:, :n], start=(c == 0), stop=(c == nch - 1)),
                 reads=[dsq, done], writes=[dps_stat])
        S.op("act", lambda e: e.activation(out=rt[:, :n], in_=ps_stat[:, :n], func=AF.Sqrt, bias=epsb[0][:, 0:1], scale=1.0 / (nch * 128)),
             reads=[dps_stat, epsb[1]], writes=[drt])
        S.op("dve", lambda e: e.reciprocal(out=rt[:, :n], in_=rt[:, :n]), reads=[drt], writes=[drt])
        for c in range(nch):
            tb, dt_ = tmp.next()
            S.op("dve", lambda e: e.tensor_tensor(out=tb[:, :n], in0=h[:, c, t0:t0 + n], in1=rt[:, :n], op=ALU.mult),
                 reads=[dh[ti], drt], writes=[dt_])
            S.op("act", lambda e: e.activation(out=xn[:, c, t0:t0 + n], in_=tb[:, :n], func=AF.Identity,
                                               bias=sh[:, c:c + 1], scale=gm[:, c:c + 1]),
                 reads=[dt_, dmv], writes=[dxn[ti]])


epsb = [None, None]


def make_consts(S):
    ones = S.sb([128, 128], F32)
    done = Dep()
    S.op("pool", lambda e: e.memset(ones[:], 1.0), writes=[done])
    eb = S.sb([128, 1], F32)
    deb = Dep()
    S.op("pool", lambda e: e.memset(eb[:], EPS), writes=[deb])
    epsb[0], epsb[1] = eb, deb
    return ones, done


def build_ffn(final_norm=False):
    nc = new_nc()
    hT = nc.dram_tensor("hT", [D, T], F32, kind="ExternalInput").ap()
    w_in = nc.dram_tensor("w_in", [D, 2 * DFF], F32, kind="ExternalInput").ap()
    w_out = nc.dram_tensor("w_out", [DFF, D], F32, kind="ExternalInput").ap()
    mv = nc.dram_tensor("mv", [128, 8, 16], F32, kind="ExternalInput").ap()
    hT_out = nc.dram_tensor("hT_out", [D, T], F32, kind="ExternalOutput").ap()
    S = Sched(nc)
    ones, done = make_consts(S)
    NCH = D // 128
    h = S.sb([128, NCH, T], F32, "h")
    dh = [Dep() for _ in TILES]
    hv = hT.rearrange("(c p) t -> p c t", p=128)
    for ti, (t0, n) in enumerate(TILES):
        S.dma("sp", h[:, :, t0:t0 + n], hv[:, :, t0:t0 + n], writes=[dh[ti]])
    mvs = S.sb([128, 8, 16], F32, "mvs")
    dmv = Dep()
    S.dma("sp", mvs[:], mv[:, :, :], writes=[dmv])
    gm_l = S.sb([128, 16], F32); gm_c = S.sb([128, 16], F32)
    hg_l = S.sb([128, 16], F32); hg_c = S.sb([128, 16], F32)
    S.op("dve", lambda e: e.scalar_tensor_tensor(out=gm_l[:], in0=mvs[:, 2, :], scalar=1.0, in1=mvs[:, 0, :], op0=ALU.add, op1=ALU.mult), reads=[dmv], writes=[dmv])
    S.op("dve", lambda e: e.scalar_tensor_tensor(out=gm_c[:], in0=mvs[:, 5, :], scalar=1.0, in1=mvs[:, 0, :], op0=ALU.add, op1=ALU.mult), reads=[dmv], writes=[dmv])
    S.op("dve", lambda e: e.tensor_scalar(out=hg_l[:], in0=mvs[:, 3, :], scalar1=0.5, scalar2=None, op0=ALU.mult), reads=[dmv], writes=[dmv])
    S.op("dve", lambda e: e.tensor_scalar(out=hg_c[:], in0=mvs[:, 6, :], scalar1=0.5, scalar2=None, op0=ALU.mult), reads=[dmv], writes=[dmv])

    xn = S.sb([128, NCH, T], BF16, "xn")
    dxn = [Dep() for _ in TILES]
    ps_stat = S.ps([128, 512]); dps_stat = Dep()
    emit_norm_mod(S, h, dh, xn, dxn, gm_l, mvs[:, 1, :], gm_c, mvs[:, 4, :], dmv, ones, done, ps_stat, dps_stat)

    w_in_v = w_in.rearrange("(c p) n -> p c n", p=128)
    w_out_v = w_out.rearrange("(c p) n -> p c n", p=128)
    GC = 4
    NG = DFF // 128 // GC
    wab = RR([(S.sb([128, 2, NCH, 256], BF16), Dep()) for _ in range(2)])
    wo = RR([(S.sb([128, GC, 512], BF16), Dep()) for _ in range(2)])
    gbuf = RR([(S.sb([128, GC, T], BF16), [Dep() for _ in TILES]) for _ in range(2)])
    psA = RR([(S.ps([128, 512]), Dep()) for _ in range(2)])
    psB = RR([(S.ps([128, 512]), Dep()) for _ in range(2)])
    psO = RR([(S.ps([128, 512]), Dep()) for _ in range(2)])
    sa = RR([(S.sb([128, 512], F32), Dep()) for _ in range(2)])
    for gi in range(NG):
        g, dg = gbuf.next()
        for b2 in range(GC // 2):
            f0 = (gi * GC + b2 * 2) * 128
            w, dw = wab.next()
            S.dma("pool", w[:, 0], w_in_v[:, :, f0:f0 + 256], writes=[dw])
            S.dma("pool", w[:, 1], w_in_v[:, :, DFF + f0:DFF + f0 + 256], writes=[dw])
            for j in range(2):
                fl = b2 * 2 + j
                for ti, (t0, n) in enumerate(TILES):
                    pa, dpa = psA.next()
                    pb, dpb = psB.next()
                    for c in range(NCH):
                        S.op("pe", lambda e: e.matmul(pa[:, :n], lhsT=w[:, 0, c, j * 128:(j + 1) * 128], rhs=xn[:, c, t0:t0 + n],
                                                      start=(c == 0), stop=(c == NCH - 1)), reads=[dw, dxn[ti]], writes=[dpa])
                    for c in range(NCH):
                        S.op("pe", lambda e: e.matmul(pb[:, :n], lhsT=w[:, 1, c, j * 128:(j + 1) * 128], rhs=xn[:, c, t0:t0 + n],
                                                      start=(c == 0), stop=(c == NCH - 1)), reads=[dw, dxn[ti]], writes=[dpb])
                    sab, dsa = sa.next()
                    S.op("act", lambda e: e.activation(out=sab[:, :n], in_=pa[:, :n], func=AF.Silu), reads=[dpa], writes=[dsa])
                    S.op("dve", lambda e: e.tensor_tensor(out=g[:, fl, t0:t0 + n], in0=sab[:, :n], in1=pb[:, :n], op=ALU.mult),
                         reads=[dsa, dpb], writes=[dg[ti]])
        for q in range(4):
            wob, dwo = wo.next()
            S.dma("pool", wob[:], w_out_v[:, gi * GC:(gi + 1) * GC, q * 512:(q + 1) * 512], writes=[dwo])
            for dcl in range(4):
                dc = q * 4 + dcl
                for ti, (t0, n) in enumerate(TILES):
                    po, dpo = psO.next()
                    for j in range(GC):
                        S.op("pe", lambda e: e.matmul(po[:, :n], lhsT=wob[:, j, dcl * 128:(dcl + 1) * 128], rhs=g[:, j, t0:t0 + n],
                                                      start=(j == 0), stop=(j == GC - 1)), reads=[dwo, dg[ti]], writes=[dpo])
                    hg = hg_l if ti < len(TILES) - 1 else hg_c
                    S.op("dve", lambda e: e.scalar_tensor_tensor(out=h[:, dc, t0:t0 + n], in0=po[:, :n], scalar=hg[:, dc:dc + 1],
                                                                 in1=h[:, dc, t0:t0 + n], op0=ALU.mult, op1=ALU.add),
                         reads=[dpo, dmv, dh[ti]], writes=[dh[ti]])
    dout = Dep()
    ov = hT_out.rearrange("(c p) t -> p c t", p=128)
    if final_norm:
        zero = S.sb([128, 16], F32)
        S.op("pool", lambda e: e.memset(zero[:], 0.0), writes=[dmv])
        emit_norm_mod(S, h, dh, h, dh, mvs[:, 7, :], zero, mvs[:, 7, :], zero, dmv, ones, done, ps_stat, dps_stat)
    for ti, (t0, n) in enumerate(TILES):
        S.dma("sp", ov[:, :, t0:t0 + n], h[:, :, t0:t0 + n], reads=[dh[ti]], writes=[dout])
    S.finish([dout])
    return nc


MODC = 9 * D // NCORES


def build_mod():
    nc = new_nc()
    cc = nc.dram_tensor("cc", [128, 16, 2], F32, kind="ExternalInput").ap()
    aw = nc.dram_tensor("aw", [4 * D, MODC], F32, kind="ExternalInput").ap()
    ab = nc.dram_tensor("ab", [2, 4 * MODC], F32, kind="ExternalInput").ap()
    mod = nc.dram_tensor("mod", [2, 4 * MODC], F32, kind="ExternalOutput").ap()
    S = Sched(nc)
    sc = S.sb([128, 16, 2], F32); dsc = Dep()
    S.dma("sp", sc[:], cc[:, :, :], writes=[dsc])
    S.op("act", lambda e: e.activation(out=sc[:], in_=sc[:], func=AF.Silu), reads=[dsc], writes=[dsc])
    abs_ = S.sb([2, 4 * MODC], F32); dab = Dep()
    S.dma("sp", abs_[:], ab[:, :], writes=[dab])
    res = S.sb([2, 4 * MODC], F32); dres = Dep()
    wb = RR([(S.sb([128, 16, 512], F32), Dep()) for _ in range(2)])
    pp = RR([(S.ps([2, 512]), Dep()) for _ in range(2)])
    k = 0
    for l in range(4):
        awv = aw[l * D:(l + 1) * D, :].rearrange("(c p) n -> p c n", p=128)
        for c0 in range(0, MODC, 512):
            n = min(512, MODC - c0)
            w, dw = wb.next()
            S.dma("sp" if k % 2 == 0 else "act", w[:, :, :n], awv[:, :, c0:c0 + n], writes=[dw])
            k += 1
            p, dp = pp.next()
            for c in range(16):
                S.op("pe", lambda e: e.matmul(p[:, :n], lhsT=sc[:, c, :], rhs=w[:, c, :n], start=(c == 0), stop=(c == 15)),
                     reads=[dw, dsc], writes=[dp])
            o0 = l * MODC + c0
            S.op("dve", lambda e: e.tensor_tensor(out=res[:, o0:o0 + n], in0=p[:, :n], in1=abs_[:, o0:o0 + n], op=ALU.add),
                 reads=[dp, dab], writes=[dres])
    dout = Dep()
    S.dma("sp", mod[:, :], res[:], reads=[dres], writes=[dout])
    S.finish([dout])
    return nc


def build_proj(nout, kind):
    nc = new_nc()
    hT = nc.dram_tensor("hT", [D, T], F32, kind="ExternalInput").ap()
    w = nc.dram_tensor("w", [D, nout], F32, kind="ExternalInput").ap()
    mv = nc.dram_tensor("mv", [128, 8, 16], F32, kind="ExternalInput").ap()
    oT = nc.dram_tensor("oT", [nout, T], F32, kind="ExternalOutput").ap()
    if kind == "gqa":
        qkn = nc.dram_tensor("qkn", [128, 2], F32, kind="ExternalInput").ap()
        cosF = nc.dram_tensor("cosF", [128, TL], F32, kind="ExternalInput").ap()
        sinF = nc.dram_tensor("sinF", [128, TL], F32, kind="ExternalInput").ap()
        rotm = nc.dram_tensor("rotm", [128, 128], F32, kind="ExternalInput").ap()
    S = Sched(nc)
    ones, done = make_consts(S)
    NCH = D // 128
    h = S.sb([128, NCH, T], F32, "h")
    dh = [Dep() for _ in TILES]
    hv = hT.rearrange("(c p) t -> p c t", p=128)
    for ti, (t0, n) in enumerate(TILES):
        S.dma("sp", h[:, :, t0:t0 + n], hv[:, :, t0:t0 + n], writes=[dh[ti]])
    mvs = S.sb([128, 8, 16], F32, "mvs"); dmv = Dep()
    S.dma("sp", mvs[:], mv[:, :, :], writes=[dmv])
    gm_l = S.sb([128, 16], F32); gm_c = S.sb([128, 16], F32)
    S.op("dve", lambda e: e.scalar_tensor_tensor(out=gm_l[:], in0=mvs[:, 2, :], scalar=1.0, in1=mvs[:, 0, :], op0=ALU.add, op1=ALU.mult), reads=[dmv], writes=[dmv])
    S.op("dve", lambda e: e.scalar_tensor_tensor(out=gm_c[:], in0=mvs[:, 5, :], scalar=1.0, in1=mvs[:, 0, :], op0=ALU.add, op1=ALU.mult), reads=[dmv], writes=[dmv])
    xn = S.sb([128, NCH, T], BF16, "xn")
    dxn = [Dep() for _ in TILES]
    ps_stat = S.ps([128, 512]); dps_stat = Dep()
    emit_norm_mod(S, h, dh, xn, dxn, gm_l, mvs[:, 1, :], gm_c, mvs[:, 4, :], dmv, ones, done, ps_stat, dps_stat)
    if kind == "gqa":
        qk = S.sb([128, 2], F32); cs_ = S.sb([128, TL], F32); sn_ = S.sb([128, TL], F32); rm = S.sb([128, 128], F32)
        dq = Dep()
        S.dma("sp", qk[:], qkn[:, :], writes=[dq])
        S.dma("sp", cs_[:], cosF[:, :], writes=[dq])
        S.dma("sp", sn_[:], sinF[:, :], writes=[dq])
        S.dma("sp", rm[:], rotm[:, :], writes=[dq])
        xs = RR([(S.sb([128, 512], F32), Dep()) for _ in range(2)])
        sqs = RR([(S.sb([128, 512], F32), Dep()) for _ in range(2)])
        rts = RR([(S.sb([128, 512], F32), Dep()) for _ in range(2)])
        t1s = RR([(S.sb([128, 512], F32), Dep()) for _ in range(2)])
        t2s = RR([(S.sb([128, 512], F32), Dep()) for _ in range(2)])
        ps_rot = S.ps([128, 512]); dps_rot = Dep()
    BW = 384
    wv = w.rearrange("(c p) n -> p c n", p=128)
    wblk = RR([(S.sb([128, NCH, BW], BF16), Dep()) for _ in range(2)])
    pso = RR([(S.ps([128, 512]), Dep()) for _ in range(3)])
    stg = RR([(S.sb([128, T], F32), Dep()) for _ in range(3)])
    dout = Dep()
    for b in range(nout // BW):
        wb, dw = wblk.next()
        S.dma("pool", wb[:], wv[:, :, b * BW:(b + 1) * BW], writes=[dw])
        for j in range(BW // 128):
            oc = b * (BW // 128) + j
            st, dst = stg.next()
            for ti, (t0, n) in enumerate(TILES):
                p, dp = pso.next()
                for c in range(NCH):
                    S.op("pe", lambda e: e.matmul(p[:, :n], lhsT=wb[:, c, j * 128:(j + 1) * 128], rhs=xn[:, c, t0:t0 + n],
                                                  start=(c == 0), stop=(c == NCH - 1)), reads=[dw, dxn[ti]], writes=[dp])
                if kind == "gqa" and oc < 20:
                    col = 0 if oc < 16 else 1
                    x, dx = xs.next(); sq, dsq = sqs.next(); rt, drt = rts.next()
                    S.op("act", lambda e: e.copy(out=x[:, :n], in_=p[:, :n]), reads=[dp], writes=[dx])
                    S.op("pool", lambda e: e.tensor_tensor(out=sq[:, :n], in0=x[:, :n], in1=x[:, :n], op=ALU.mult), reads=[dx], writes=[dsq])
                    S.op("pe", lambda e: e.matmul(ps_stat[:, :n], lhsT=ones[:, :], rhs=sq[:, :n], start=True, stop=True),
                         reads=[dsq, done], writes=[dps_stat])
                    S.op("act", lambda e: e.activation(out=rt[:, :n], in_=ps_stat[:, :n], func=AF.Sqrt, bias=epsb[0][:, 0:1], scale=1.0 / 128),
                         reads=[dps_stat, epsb[1]], writes=[drt])
                    S.op("dve", lambda e: e.reciprocal(out=rt[:, :n], in_=rt[:, :n]), reads=[drt], writes=[drt])
                    if ti < len(TILES) - 1:
                        S.op("dve", lambda e: e.scalar_tensor_tensor(out=x[:, :n], in0=x[:, :n], scalar=qk[:, col:col + 1], in1=rt[:, :n], op0=ALU.mult, op1=ALU.mult),
                             reads=[dx, drt, dq], writes=[dx])
                        S.op("pe", lambda e: e.matmul(ps_rot[:, :n], lhsT=rm[:, :], rhs=x[:, :n], start=True, stop=True),
                             reads=[dx, dq], writes=[dps_rot])
                        t1, dt1 = t1s.next(); t2, dt2 = t2s.next()
                        S.op("pool", lambda e: e.tensor_tensor(out=t1[:, :n], in0=x[:, :n], in1=cs_[:, t0:t0 + n], op=ALU.mult), reads=[dx, dq], writes=[dt1])
                        S.op("dve", lambda e: e.tensor_tensor(out=t2[:, :n], in0=ps_rot[:, :n], in1=sn_[:, t0:t0 + n], op=ALU.mult), reads=[dps_rot, dq], writes=[dt2])
                        S.op("dve", lambda e: e.tensor_tensor(out=st[:, t0:t0 + n], in0=t1[:, :n], in1=t2[:, :n], op=ALU.add), reads=[dt1, dt2], writes=[dst])
                    else:
                        S.op("dve", lambda e: e.scalar_tensor_tensor(out=st[:, t0:t0 + n], in0=x[:, :n], scalar=qk[:, col:col + 1], in1=rt[:, :n], op0=ALU.mult, op1=ALU.mult),
                             reads=[dx, drt, dq], writes=[dst])
                else:
                    if (oc + ti) % 2 == 0:
                        S.op("act", lambda e: e.copy(out=st[:, t0:t0 + n], in_=p[:, :n]), reads=[dp], writes=[dst])
                    else:
                        S.op("dve", lambda e: e.tensor_copy(out=st[:, t0:t0 + n], in_=p[:, :n]), reads=[dp], writes=[dst])
            S.dma("sp", oT[oc * 128:(oc + 1) * 128, :], st[:], reads=[dst], writes=[dout])
    S.finish([dout])
    return nc


def build_oproj(kc, ssm):
    nc = new_nc()
    aT = nc.dram_tensor("aT", [kc * 128, T], F32, kind="ExternalInput").ap()
    w = nc.dram_tensor("w", [kc * 128, D], F32, kind="ExternalInput").ap()
    hT = nc.dram_tensor("hT", [D, T], F32, kind="ExternalInput").ap()
    mv = nc.dram_tensor("mv", [128, 8, 16], F32, kind="ExternalInput").ap()
    if ssm:
        ng = nc.dram_tensor("ng", [128, kc], F32, kind="ExternalInput").ap()
    hT_out = nc.dram_tensor("hT_out", [D, T], F32, kind="ExternalOutput").ap()
    S = Sched(nc)
    ones, done = make_consts(S)
    mvs = S.sb([128, 8, 16], F32, "mvs"); dmv = Dep()
    S.dma("sp", mvs[:], mv[:, :, :], writes=[dmv])
    a = S.sb([128, kc, T], BF16, "a")
    da = [Dep() for _ in TILES]
    av = aT.rearrange("(c p) t -> p c t", p=128)
    if not ssm:
        for ti, (t0, n) in enumerate(TILES):
            S.dma("pool", a[:, :, t0:t0 + n], av[:, :, t0:t0 + n], writes=[da[ti]])
    else:
        ngs = S.sb([128, kc], F32); zero = S.sb([128, kc], F32)
        S.dma("sp", ngs[:], ng[:, :], writes=[dmv])
        S.op("pool", lambda e: e.memset(zero[:], 0.0), writes=[dmv])
        ps_stat = S.ps([128, 512]); dps_stat = Dep()
        ytile = S.sb([128, kc, 512], F32, "ytile"); dyt = Dep()
        for ti, (t0, n) in enumerate(TILES):
            S.dma("sp", ytile[:, :, :n], av[:, :, t0:t0 + n], writes=[dyt])

            class _V:
                def __getitem__(self, idx):
                    return ytile[:, idx[1], 0:n]
            emit_norm_mod(S, _V(), [dyt], a[:, :, t0:t0 + n], [da[ti]], ngs, zero, ngs, zero, dmv, ones, done, ps_stat, dps_stat,
                          nch=kc, tiles=[(0, n)])
    wv = w.rearrange("(c p) n -> p c n", p=128)
    hv = hT.rearrange("(c p) t -> p c t", p=128)
    ov = hT_out.rearrange("(c p) t -> p c t", p=128)
    BW2 = 128 if ssm else 256
    wblk = RR([(S.sb([128, kc, BW2], BF16), Dep()) for _ in range(2)])
    hs = RR([(S.sb([128, T], F32), Dep()) for _ in range(3)])
    pso = RR([(S.ps([128, 512]), Dep()) for _ in range(3)])
    dout = Dep()
    for b in range(D // BW2):
        wb, dw = wblk.next()
        S.dma("pool", wb[:], wv[:, :, b * BW2:(b + 1) * BW2], writes=[dw])
        for j in range(BW2 // 128):
            dc = b * (BW2 // 128) + j
            hb, dhb = hs.next()
            S.dma("act", hb[:], hv[:, dc, :], writes=[dhb])
            for ti, (t0, n) in enumerate(TILES):
                p, dp = pso.next()
                for c in range(kc):
                    S.op("pe", lambda e: e.matmul(p[:, :n], lhsT=wb[:, c, j * 128:(j + 1) * 128], rhs=a[:, c, t0:t0 + n],
                                                  start=(c == 0), stop=(c == kc - 1)), reads=[dw, da[ti]], writes=[dp])
                gcol = 3 if ti < len(TILES) - 1 else 6
                S.op("dve", lambda e: e.scalar_tensor_tensor(out=hb[:, t0:t0 + n], in0=p[:, :n], scalar=mvs[:, gcol, dc:dc + 1],
                                                             in1=hb[:, t0:t0 + n], op0=ALU.mult, op1=ALU.add),
                     reads=[dp, dmv, dhb], writes=[dhb])
            S.dma("sp", ov[:, dc, :], hb[:], reads=[dhb], writes=[dout])
    S.finish([dout])
    return nc


def rope_tables():
    t = np.arange(SEQ)
    row = (t // 64).astype(np.float32)
    col = (t % 64).astype(np.float32)
    inv = (np.float32(10000.0) ** (-np.arange(32, dtype=np.float32) / np.float32(32))).astype(np.float32)
    ang = np.concatenate([row[:, None] * inv, col[:, None] * inv], axis=-1).astype(np.float32)
    cos = np.cos(ang).astype(np.float32); sin = np.sin(ang).astype(np.float32)
    cosF = np.ascontiguousarray(np.concatenate([cos, cos], axis=1).T)
    sinF = np.ascontiguousarray(np.concatenate([sin, sin], axis=1).T)
    return cosF, sinF


def rot_matrix():
    m = np.zeros((128, 128), np.float32)
    for i in range(64):
        m[i + 64, i] = -1.0
        m[i, i + 64] = 1.0
    return m


def fm(v):
    v = np.asarray(v, np.float32).reshape(-1, 128)
    return np.ascontiguousarray(v.T)


NTOK = SEQ + CTX
NKC = NTOK // 128
ROWS = SEQ // 64


def na_row_info(r):
    rs = min(max(r - 4, 0), ROWS - 8)
    cb = min(rs // 2, 59)
    return rs, cb, (rs - 2 * cb, r - 2 * cb)


def na_patterns():
    pats = []
    for r in range(ROWS):
        k = na_row_info(r)[2]
        if k not in pats:
            pats.append(k)
    return pats


def na_bias_tables(rpb):
    pats = na_patterns()
    out = np.zeros((rpb.shape[0], len(pats), 128, 7 * 64), np.float32)
    c = np.arange(64)
    cst = np.clip(c - 8, 0, 48)
    for pi, (a, b) in enumerate(pats):
        for i in range(5):
            for half in range(2):
                lr = 2 * i + half
                valid_row = (a <= lr < a + 8)
                kcol = np.arange(64)
                blk = np.full((rpb.shape[0], 64, 64), NEG, np.float32)
                if valid_row:
                    dr = lr - b + 7
                    dcol = kcol[:, None] - c[None, :] + 15
                    ok = (kcol[:, None] >= cst[None, :]) & (kcol[:, None] < cst[None, :] + 16)
                    g = rpb[:, dr][:, np.clip(dcol, 0, 30)]
                    blk = np.where(ok[None], g, np.float32(NEG)).astype(np.float32)
                out[:, pi, half * 64:(half + 1) * 64, i * 64:(i + 1) * 64] = blk
    return out


def build_na():
    nc = new_nc()
    npat = len(na_patterns())
    pats = na_patterns()
    qT = nc.dram_tensor("qT", [2 * 128, NTOK], F32, kind="ExternalInput").ap()
    kT = nc.dram_tensor("kT", [2 * 128, NTOK], F32, kind="ExternalInput").ap()
    v = nc.dram_tensor("v", [2 * NTOK, 128], F32, kind="ExternalInput").ap()
    bm = nc.dram_tensor("bm", [2 * npat * 128, 448], F32, kind="ExternalInput").ap()
    o = nc.dram_tensor("o", [2 * NTOK, 128], F32, kind="ExternalOutput").ap()
    S = Sched(nc)
    scale = 128.0 ** -0.5
    qs, ks, vs, bms, dld = [], [], [], [], []
    for hh in range(2):
        d = Dep()
        q_ = S.sb([128, NTOK], BF16); k_ = S.sb([128, NTOK], BF16); v_ = S.sb([128, NKC, 129], BF16)
        b_ = S.sb([128, npat, 448], F32)
        S.op("pool", lambda e: e.memset(v_[:, :, 128:129], 1.0), writes=[d])
        for c0 in range(0, NTOK, 2112):
            S.dma("pool", q_[:, c0:c0 + 2112], qT[hh * 128:(hh + 1) * 128, c0:c0 + 2112], writes=[d])
            S.dma("pool", k_[:, c0:c0 + 2112], kT[hh * 128:(hh + 1) * 128, c0:c0 + 2112], writes=[d])
        vv = v[hh * NTOK:(hh + 1) * NTOK, :].rearrange("(c p) d -> p c d", p=128)
        for c0 in range(0, NKC, 22):
            S.dma("pool", v_[:, c0:c0 + 22, 0:128], vv[:, c0:c0 + 22, :], writes=[d])
        S.dma("sp", b_[:], bm[hh * npat * 128:(hh + 1) * npat * 128, :].rearrange("(n p) f -> p n f", p=128), writes=[d])
        qs.append(q_); ks.append(k_); vs.append(v_); bms.append(b_); dld.append(d)
    psS = RR([(S.ps([128, 512]), Dep()) for _ in range(2)])
    psO = RR([(S.ps([128, 512]), Dep()) for _ in range(2)])
    tt = RR([(S.sb([128, 448], F32), Dep()) for _ in range(2)])
    pp = RR([(S.sb([128, 448], BF16), Dep()) for _ in range(2)])
    rc = RR([(S.sb([64, 1], F32), Dep()) for _ in range(2)])
    og = RR([(S.sb([64, 8, 128], F32), Dep()) for _ in range(2)])
    dout = Dep()

    def block(hh, q0, chunks, bias_ap, ostg, dostg, slot):
        nch = len(chunks)
        p, dp = psS.next()
        for i, kc in enumerate(chunks):
            S.op("pe", lambda e: e.matmul(p[:, i * 64:(i + 1) * 64], lhsT=ks[hh][:, kc * 128:(kc + 1) * 128], rhs=qs[hh][:, q0:q0 + 64],
                                          start=True, stop=True), reads=[dld[hh]], writes=[dp])
        pb, dpb = pp.next()
        if bias_ap is not None:
            t, dt_ = tt.next()
            S.op("dve", lambda e: e.scalar_tensor_tensor(out=t[:, :nch * 64], in0=p[:, :nch * 64], scalar=scale, in1=bias_ap,
                                                         op0=ALU.mult, op1=ALU.add), reads=[dp, dld[hh]], writes=[dt_])
            S.op("act", lambda e: e.activation(out=pb[:, :nch * 64], in_=t[:, :nch * 64], func=AF.Exp), reads=[dt_], writes=[dpb])
        else:
            S.op("act", lambda e: e.activation(out=pb[:, :nch * 64], in_=p[:, :nch * 64], func=AF.Exp, scale=scale), reads=[dp], writes=[dpb])
        po, dpo = psO.next()
        for i, kc in enumerate(chunks):
            S.op("pe", lambda e: e.matmul(po[0:64, 0:129], lhsT=pb[:, i * 64:(i + 1) * 64], rhs=vs[hh][:, kc, :],
                                          start=(i == 0), stop=(i == nch - 1)), reads=[dpb, dld[hh]], writes=[dpo])
        r_, dr_ = rc.next()
        S.op("dve", lambda e: e.reciprocal(out=r_[:, :], in_=po[0:64, 128:129]), reads=[dpo], writes=[dr_])
        S.op("act", lambda e: e.activation(out=ostg[:, slot, :], in_=po[0:64, 0:128], func=AF.Identity, scale=r_[:, 0:1]),
             reads=[dpo, dr_], writes=[dostg])

    for hh in range(2):
        ov = o[hh * NTOK:(hh + 1) * NTOK, :]
        for r in range(ROWS):
            rs, cb, key = na_row_info(r)
            pi = pats.index(key)
            if r % 8 == 0:
                ostg, dostg = og.next()
            block(hh, r * 64, list(range(cb, cb + 5)) + [64, 65], bms[hh][:, pi, :], ostg, dostg, r % 8)
            if r % 8 == 7:
                r0 = r - 7
                S.dma("sp", ov[r0 * 64:(r0 + 8) * 64, :].rearrange("(r q) d -> q r d", q=64), ostg[:], reads=[dostg], writes=[dout])
        ostg, dostg = og.next()
        for qb in range(4):
            block(hh, SEQ + qb * 64, [64, 65], None, ostg, dostg, qb)
        S.dma("sp", ov[SEQ:SEQ + 256, :].rearrange("(r q) d -> q r d", q=64), ostg[:, 0:4, :], reads=[dostg], writes=[dout])
    S.finish([dout])
    return nc


def build_gqa():
    nc = new_nc()
    qT = nc.dram_tensor("qT", [2 * 128, NTOK], F32, kind="ExternalInput").ap()
    kT = nc.dram_tensor("kT", [128, NTOK], F32, kind="ExternalInput").ap()
    v = nc.dram_tensor("v", [NTOK, 128], F32, kind="ExternalInput").ap()
    o = nc.dram_tensor("o", [2 * NTOK, 128], F32, kind="ExternalOutput").ap()
    S = Sched(nc)
    scale = 128.0 ** -0.5
    dld = Dep()
    qs = [S.sb([128, NTOK], BF16) for _ in range(2)]
    k_ = S.sb([128, NTOK], BF16); v_ = S.sb([128, NKC, 129], BF16)
    S.op("pool", lambda e: e.memset(v_[:, :, 128:129], 1.0), writes=[dld])
    for c0 in range(0, NTOK, 2112):
        for hh in range(2):
            S.dma("pool", qs[hh][:, c0:c0 + 2112], qT[hh * 128:(hh + 1) * 128, c0:c0 + 2112], writes=[dld])
        S.dma("pool", k_[:, c0:c0 + 2112], kT[:, c0:c0 + 2112], writes=[dld])
    vv = v.rearrange("(c p) d -> p c d", p=128)
    for c0 in range(0, NKC, 22):
        S.dma("pool", v_[:, c0:c0 + 22, 0:128], vv[:, c0:c0 + 22, :], writes=[dld])
    psS = RR([(S.ps([128, 512]), Dep()) for _ in range(3)])
    psO = [(S.ps([128, 512]), Dep()) for _ in range(4)]
    pp = RR([(S.sb([128, 512], BF16), Dep()) for _ in range(3)])
    rc = RR([(S.sb([128, 1], F32), Dep()) for _ in range(2)])
    og = RR([(S.sb([128, 4, 128], F32), Dep()) for _ in range(2)])
    dout = Dep()
    for hh in range(2):
        ov = o[hh * NTOK:(hh + 1) * NTOK, :]
        blocks = [(qb * 512, 512, list(range(NKC))) for qb in range(SEQ // 512)] + [(SEQ, 256, [64, 65])]
        for q0, nq, chunks in blocks:
            nsub = nq // 128
            for i, kc in enumerate(chunks):
                p, dp = psS.next()
                S.op("pe", lambda e: e.matmul(p[:, :nq], lhsT=k_[:, kc * 128:(kc + 1) * 128], rhs=qs[hh][:, q0:q0 + nq], start=True, stop=True),
                     reads=[dld], writes=[dp])
                pb, dpb = pp.next()
                S.op("act", lambda e: e.activation(out=pb[:, :nq], in_=p[:, :nq], func=AF.Exp, scale=scale), reads=[dp], writes=[dpb])
                for sub in range(nsub):
                    po, dpo = psO[sub]
                    S.op("pe", lambda e: e.matmul(po[:, 0:129], lhsT=pb[:, sub * 128:(sub + 1) * 128], rhs=v_[:, kc, :],
                                                  start=(i == 0), stop=(i == len(chunks) - 1)), reads=[dpb, dld], writes=[dpo])
            ostg, dostg = og.next()
            for sub in range(nsub):
                po, dpo = psO[sub]
                r_, dr_ = rc.next()
                S.op("dve", lambda e: e.reciprocal(out=r_[:, :], in_=po[:, 128:129]), reads=[dpo], writes=[dr_])
                S.op("dve", lambda e: e.tensor_scalar(out=ostg[:, sub, :], in0=po[:, 0:128], scalar1=r_[:, 0:1], scalar2=None, op0=ALU.mult),
                     reads=[dpo, dr_], writes=[dostg])
            S.dma("sp", ov[q0:q0 + nq, :].rearrange("(s p) d -> p s d", p=128), ostg[:, 0:nsub, :], reads=[dostg], writes=[dout])
    S.finish([dout])
    return nc


SSD_ORDER = [[64, 65] + list(range(64)), [65, 64] + list(range(63, -1, -1))]


def ssd_consts():
    k = np.arange(128)
    Uf = (k[:, None] <= k[None, :]).astype(np.float32)
    Ub = (k[:, None] >= k[None, :]).astype(np.float32)
    ident = np.eye(128, dtype=np.float32)
    NEGf = np.where(k[None, :] < k[:, None], np.float32(NEG), np.float32(0)).astype(np.float32)
    NEGb = np.where(k[None, :] > k[:, None], np.float32(NEG), np.float32(0)).astype(np.float32)
    return np.ascontiguousarray(np.stack([Uf, Ub, -Uf, -Ub, ident, NEGf, NEGb], axis=1))


def build_ssd():
    nc = new_nc()
    xbc = nc.dram_tensor("xbc", [768, NTOK], F32, kind="ExternalInput").ap()
    cw = nc.dram_tensor("cw", [128, 6, 5], F32, kind="ExternalInput").ap()
    cb = nc.dram_tensor("cb", [128, 6], F32, kind="ExternalInput").ap()
    dtr = nc.dram_tensor("dtr", [NTOK, 16], F32, kind="ExternalInput").ap()
    vecs = nc.dram_tensor("vecs", [128, 3, 16], F32, kind="ExternalInput").ap()
    z = nc.dram_tensor("z", [NTOK, 512], F32, kind="ExternalInput").ap()
    consts = nc.dram_tensor("consts", [128, 7, 128], F32, kind="ExternalInput").ap()
    yg = nc.dram_tensor("yg", [NTOK, 512], F32, kind="ExternalOutput").ap()
    S = Sched(nc)
    ones, done = make_consts(S)
    cst = S.sb([128, 7, 128], F32); dcst = Dep()
    S.dma("sp", cst[:], consts[:, :, :], writes=[dcst])
    identb = S.sb([128, 128], BF16)
    S.op("dve", lambda e: e.tensor_copy(out=identb[:], in_=cst[:, 4, :]), reads=[dcst], writes=[dcst])
    neg4 = S.sb([128, 2, 4, 128], F32)
    for d in range(2):
        S.op("dve", lambda e: e.tensor_copy(out=neg4[:, d], in_=cst[:, 5 + d, :].unsqueeze(1).to_broadcast([128, 4, 128])), reads=[dcst], writes=[dcst])
    cws = S.sb([128, 6, 5], F32); cbs = S.sb([128, 6], F32); dcw = Dep()
    S.dma("sp", cws[:], cw[:, :, :], writes=[dcw])
    S.dma("sp", cbs[:], cb[:, :], writes=[dcw])
    dts = S.sb([128, NKC, 16], F32); das = S.sb([128, NKC, 16], F32); vs_ = S.sb([128, 3, 16], F32)
    ddt = Dep(); dvs = Dep()
    S.dma("sp", dts[:], dtr.rearrange("(c p) j -> p c j", p=128), writes=[ddt])
    S.dma("sp", vs_[:], vecs[:, :, :], writes=[dvs])
    S.op("dve", lambda e: e.tensor_tensor(out=dts[:], in0=dts[:], in1=vs_[:, 0, :].unsqueeze(1).to_broadcast([128, NKC, 16]), op=ALU.add),
         reads=[dvs], writes=[ddt])
    S.op("act", lambda e: e.activation(out=dts[:], in_=dts[:], func=AF.Exp), writes=[ddt])
    S.op("act", lambda e: e.activation(out=dts[:], in_=dts[:], func=AF.Ln, bias=ones[:, 0:1]), reads=[done], writes=[ddt])
    ea = S.sb([128, 16], F32); dsum = S.sb([128, 8], F32)
    S.op("act", lambda e: e.activation(out=ea[:], in_=vs_[:, 1, :], func=AF.Exp), reads=[dvs], writes=[dvs])
    S.op("dve", lambda e: e.scalar_tensor_tensor(out=das[:], in0=dts[:], scalar=-1.0, in1=ea[:, :].unsqueeze(1).to_broadcast([128, NKC, 16]),
                                                 op0=ALU.mult, op1=ALU.mult), reads=[ddt, dvs], writes=[ddt])
    S.op("dve", lambda e: e.tensor_tensor(out=dsum[:], in0=vs_[:, 2, 0:8], in1=vs_[:, 2, 8:16], op=ALU.add), reads=[dvs], writes=[dvs])
    xtok = S.sb([128, NKC, 512], BF16, "xtok"); btok = S.sb([128, NKC, 128], BF16, "btok")
    BT = S.sb([128, NTOK], BF16, "BT"); CT = S.sb([128, NTOK], BF16, "CT")
    dxt = [Dep() for _ in range(NKC)]
    PL = 512
    pieces = [(a, PL, 0, SEQ) for a in range(0, SEQ, PL)] + [(SEQ, CTX, SEQ, NTOK)]
    xin = RR([(S.sb([128, PL + 4], F32), Dep()) for _ in range(2)])
    acc = RR([(S.sb([128, PL], F32), Dep()) for _ in range(2)])
    cob = RR([(S.sb([128, PL], BF16), Dep()) for _ in range(2)])
    pst = S.ps([128, 4, 128], BF16); dpst = Dep()
    for cch in range(6):
        for (a, L, s0, s1) in pieces:
            xi, dxi = xin.next()
            lo, hi = max(a - 2, s0), min(a + L + 2, s1)
            if lo > a - 2:
                S.op("pool", lambda e: e.memset(xi[:, 0:2], 0.0), writes=[dxi])
            if hi < a + L + 2:
                S.op("pool", lambda e: e.memset(xi[:, L + 2:L + 4], 0.0), writes=[dxi])
            S.dma("sp" if cch % 2 == 0 else "act", xi[:, lo - (a - 2):hi - (a - 2)], xbc[cch * 128:(cch + 1) * 128, lo:hi], writes=[dxi])
            ac, dac_ = acc.next()
            S.op("dve", lambda e: e.tensor_scalar(out=ac[:, :L], in0=xi[:, 2:2 + L], scalar1=cws[:, cch, 2:3], scalar2=None, op0=ALU.mult),
                 reads=[dxi, dcw], writes=[dac_])
            for k in (0, 1, 3, 4):
                S.op("dve", lambda e: e.scalar_tensor_tensor(out=ac[:, :L], in0=xi[:, k:k + L], scalar=cws[:, cch, k:k + 1], in1=ac[:, :L],
                                                             op0=ALU.mult, op1=ALU.add), reads=[dxi, dcw], writes=[dac_])
            tcs = list(range(a // 128, (a + L) // 128))
            if cch < 5:
                co, dco = cob.next()
                S.op("act", lambda e: e.activation(out=co[:, :L], in_=ac[:, :L], func=AF.Silu, bias=cbs[:, cch:cch + 1]), reads=[dac_, dcw], writes=[dco])
                if cch == 4:
                    S.op("pool", lambda e: e.tensor_copy(out=BT[:, a:a + L], in_=co[:, :L]), reads=[dco], writes=[dxt[t] for t in tcs])
                for g0 in range(0, len(tcs), 4):
                    grp = tcs[g0:g0 + 4]
                    for i, tc_ in enumerate(grp):
                        S.op("pe", lambda e: e.transpose(out=pst[:, i, :], in_=co[:, (g0 + i) * 128:(g0 + i + 1) * 128], identity=identb[:, :]),
                             reads=[dco, dcst], writes=[dpst])
                    dst = xtok[:, grp[0]:grp[0] + len(grp), cch * 128:(cch + 1) * 128] if cch < 4 else btok[:, grp[0]:grp[0] + len(grp), :]
                    S.op("act" if (g0 // 4) % 2 == 0 else "dve",
                         (lambda e: e.copy(out=dst, in_=pst[:, 0:len(grp), :])) if (g0 // 4) % 2 == 0 else (lambda e: e.tensor_copy(out=dst, in_=pst[:, 0:len(grp), :])),
                         reads=[dpst], writes=[dxt[t] for t in grp])
            else:
                S.op("act", lambda e: e.activation(out=CT[:, a:a + L], in_=ac[:, :L], func=AF.Silu, bias=cbs[:, cch:cch + 1]),
                     reads=[dac_, dcw], writes=[dxt[t] for t in tcs])
    Sst = [S.sb([128, 8, 64], F32) for _ in range(2)]
    Sbf = [S.sb([128, 512], BF16) for _ in range(2)]
    dS = [Dep(), Dep()]; dSbf = [Dep(), Dep()]
    for d in range(2):
        S.op("pool", lambda e: e.memset(Sst[d][:], 0.0), writes=[dS[d]])
        S.op("pool", lambda e: e.memset(Sbf[d][:], 0.0), writes=[dSbf[d]])
    rhsA = RR([(S.sb([128, 8, 128], F32), Dep()) for _ in range(2)])
    rhsB = RR([(S.sb([128, 8, 128], F32), Dep()) for _ in range(2)])
    expD = RR([(S.sb([128, 4, 128], F32), Dep()) for _ in range(4)])
    ecs_ = RR([(S.sb([128, 16], F32), Dep()) for _ in range(2)])
    MT = RR([(S.sb([128, 8, 128], BF16), Dep()) for _ in range(2)])
    xdt = RR([(S.sb([128, 8, 64], BF16), Dep()) for _ in range(2)])
    xw = RR([(S.sb([128, 8, 64], BF16), Dep()) for _ in range(2)])
    ysb = RR([(S.sb([128, 8, 64], F32), Dep()) for _ in range(2)])
    tb_ = RR([(S.sb([128, 8, 64], F32), Dep()) for _ in range(2)])
    ub_ = RR([(S.sb([128, 8, 64], F32), Dep()) for _ in range(2)])
    zb = RR([(S.sb([128, 512], F32), Dep()) for _ in range(2)])
    pv = RR([(S.sb([128, 512], F32), Dep()) for _ in range(2)])
    psD = RR([(S.ps([128, 4, 128]), Dep()) for _ in range(2)])
    psCB = S.ps([128, 512]); dCB = Dep()
    psSm = S.ps([128, 512]); dSm = Dep()
    psY = S.ps([128, 8, 64]); dY = Dep()
    psYo = S.ps([128, 8, 64]); dYo = Dep()
    psSp = S.ps([128, 8, 64]); dSp = Dep()
    dyg = [Dep() for _ in range(NKC)]
    visited = set()
    for step in range(NKC):
        for d in range(2):
            c = SSD_ORDER[d][step]
            j0 = d * 8
            U = cst[:, d, :]; negU = cst[:, 2 + d, :]; idn = cst[:, 4, :]
            col = 127 if d == 0 else 0
            dac = das[:, c, j0:j0 + 8]
            ra, dra = rhsA.next(); rb, drb = rhsB.next()
            S.op("dve", lambda e: e.tensor_tensor(out=ra[:], in0=U.unsqueeze(1).to_broadcast([128, 8, 128]),
                                                  in1=dac.unsqueeze(2).to_broadcast([128, 8, 128]), op=ALU.mult), reads=[dcst, ddt], writes=[dra])
            S.op("pool", lambda e: e.tensor_copy(out=rb[:], in_=dac.unsqueeze(2).to_broadcast([128, 8, 128])), reads=[ddt], writes=[drb])
            S.op("pe", lambda e: e.matmul(psSm[:, 0:8], lhsT=U, rhs=dac, start=True, stop=True), reads=[dcst, ddt], writes=[dSm])
            S.op("pe", lambda e: e.matmul(psSm[:, 8:16], lhsT=ones[:, :], rhs=dac, start=True, stop=True), reads=[done, ddt], writes=[dSm])
            ec, dec = ecs_.next()
            S.op("act", lambda e: e.activation(out=ec[:], in_=psSm[:, 0:16], func=AF.Exp), reads=[dSm], writes=[dec])
            eds = []
            for half in range(2):
                h4 = half * 4
                pD, dpD = psD.next()
                S.op("pe", lambda e: e.matmul(pD[:], lhsT=ones[:, :], rhs=ra[:, h4:h4 + 4, :], start=True, stop=False), reads=[dra, done], writes=[dpD])
                S.op("pe", lambda e: e.matmul(pD[:], lhsT=negU, rhs=rb[:, h4:h4 + 4, :], start=False, stop=False), reads=[drb, dcst], writes=[dpD])
                S.op("pe", lambda e: e.matmul(pD[:], lhsT=idn, rhs=neg4[:, d], start=False, stop=True), reads=[dcst], writes=[dpD])
                ed, ded = expD.next()
                S.op("act", lambda e: e.activation(out=ed[:], in_=pD[:], func=AF.Exp), reads=[dpD], writes=[ded])
                eds.append((ed, ded))
            S.op("pe", lambda e: e.matmul(psCB[:, 0:128], lhsT=BT[:, c * 128:(c + 1) * 128], rhs=CT[:, c * 128:(c + 1) * 128], start=True, stop=True),
                 reads=[dxt[c]], writes=[dCB])
            mt, dmt = MT.next()
            for half in range(2):
                h4 = half * 4
                ed, ded = eds[half]
                S.op("dve", lambda e: e.tensor_tensor(out=mt[:, h4:h4 + 4, :], in0=ed[:], in1=psCB[:, 0:128].unsqueeze(1).to_broadcast([128, 4, 128]),
                                                      op=ALU.mult), reads=[ded, dCB], writes=[dmt])
            xd, dxd = xdt.next(); xw_, dxw = xw.next()
            xt3 = xtok[:, c, :].rearrange("p (h q) -> p h q", h=8)
            S.op("pool", lambda e: e.tensor_tensor(out=xd[:], in0=xt3, in1=dts[:, c, j0:j0 + 8].unsqueeze(2).to_broadcast([128, 8, 64]), op=ALU.mult),
                 reads=[dxt[c], ddt], writes=[dxd])
            for half in range(2):
                h4 = half * 4
                ed, ded = eds[half]
                S.op("pool", lambda e: e.tensor_tensor(out=xw_[:, h4:h4 + 4, :], in0=xd[:, h4:h4 + 4, :],
                                                       in1=ed[:, :, col:col + 1].to_broadcast([128, 4, 64]), op=ALU.mult), reads=[dxd, ded], writes=[dxw])
            for hh in range(8):
                S.op("pe", lambda e: e.matmul(psY[:, hh, :], lhsT=mt[:, hh, :], rhs=xd[:, hh, :], start=True, stop=True), reads=[dmt, dxd], writes=[dY])
            S.op("pe", lambda e: e.matmul(psYo[:], lhsT=CT[:, c * 128:(c + 1) * 128], rhs=Sbf[d][:], start=True, stop=True),
                 reads=[dxt[c], dSbf[d]], writes=[dYo])
            S.op("pe", lambda e: e.matmul(psSp[:], lhsT=btok[:, c, :], rhs=xw_[:], start=True, stop=True), reads=[dxt[c], dxw], writes=[dSp])
            ys, dys = ysb.next(); t, dt_ = tb_.next()
            S.op("act", lambda e: e.copy(out=ys[:], in_=psY[:]), reads=[dY], writes=[dys])
            S.op("dve", lambda e: e.tensor_tensor(out=t[:], in0=psYo[:], in1=ec[:, 0:8].unsqueeze(2).to_broadcast([128, 8, 64]), op=ALU.mult),
                 reads=[dYo, dec], writes=[dt_])
            S.op("pool", lambda e: e.tensor_tensor(out=t[:], in0=t[:], in1=ys[:], op=ALU.add), reads=[dys], writes=[dt_])
            if d == 0:
                u, du = ub_.next()
                S.op("pool", lambda e: e.tensor_tensor(out=u[:], in0=xt3, in1=dsum[:, :].unsqueeze(2).to_broadcast([128, 8, 64]), op=ALU.mult),
                     reads=[dxt[c], dvs], writes=[du])
                S.op("pool", lambda e: e.tensor_tensor(out=t[:], in0=t[:], in1=u[:], op=ALU.add), reads=[du], writes=[dt_])
            zt, dz = zb.next()
            S.dma("act", zt[:], z[c * 128:(c + 1) * 128, :], writes=[dz])
            S.op("act", lambda e: e.activation(out=zt[:], in_=zt[:], func=AF.Silu), writes=[dz])
            t2 = t[:].rearrange("p h q -> p (h q)")
            S.op("dve", lambda e: e.tensor_tensor(out=t2, in0=t2, in1=zt[:], op=ALU.mult), reads=[dz], writes=[dt_])
            if c in visited:
                pvb, dpv = pv.next()
                S.dma("sp", pvb[:], yg[c * 128:(c + 1) * 128, :], reads=[dyg[c]], writes=[dpv])
                S.op("pool", lambda e: e.tensor_tensor(out=t2, in0=t2, in1=pvb[:], op=ALU.add), reads=[dpv], writes=[dt_])
            visited.add(c)
            S.dma("sp", yg[c * 128:(c + 1) * 128, :], t2, reads=[dt_], writes=[dyg[c]])
            S.op("dve", lambda e: e.tensor_tensor(out=Sst[d][:], in0=Sst[d][:], in1=ec[:, 8:16].unsqueeze(2).to_broadcast([128, 8, 64]), op=ALU.mult),
                 reads=[dec], writes=[dS[d]])
            S.op("dve", lambda e: e.tensor_tensor(out=Sst[d][:], in0=Sst[d][:], in1=psSp[:], op=ALU.add), reads=[dSp], writes=[dS[d]])
            S.op("act", lambda e: e.copy(out=Sbf[d][:], in_=Sst[d][:].rearrange("p h q -> p (h q)")), reads=[dS[d]], writes=[dSbf[d]])
    S.finish(dyg)
    return nc


_PROGS = {}


def _prog(key, fn):
    if key not in _PROGS:
        _PROGS[key] = fn()
    return _PROGS[key]


def _run(nc, in_maps):
    res = run_bass_kernel_spmd(nc, in_maps, core_ids=list(range(NCORES)))
    return res.results


def to_global(per_core):
    lat = np.concatenate([a[:, :TL] for a in per_core], axis=1)
    ctx = np.concatenate([a[:, TL:] for a in per_core], axis=1)
    return np.concatenate([lat, ctx], axis=1)


def to_cores(g):
    return [np.ascontiguousarray(np.concatenate([g[:, j * TL:(j + 1) * TL], g[:, SEQ + j * TC:SEQ + (j + 1) * TC]], axis=1))
            for j in range(NCORES)]


def make_mv(g, mod, base, final_g=None):
    rows = [g, mod[0, base], mod[0, base + 1], mod[0, base + 2], mod[1, base], mod[1, base + 1], mod[1, base + 2],
            final_g if final_g is not None else g]
    return np.ascontiguousarray(np.stack([fm(r) for r in rows], axis=1))


def heads_to_T(o_cores):
    heads = []
    for oc in o_cores:
        heads.append(oc[:NTOK]); heads.append(oc[NTOK:])
    og = np.concatenate(heads, axis=1)
    return to_cores(np.ascontiguousarray(og.T))


def kernel(x, c, ctx, c_ctx, ada_w, ada_b, norm_g, ffn_w_in, ffn_w_out, na_w_qkv, na_rpb, na_w_o,
           ssm_w_in, ssm_conv_w, ssm_conv_b, ssm_a_log, ssm_dt_bias, ssm_d, ssm_norm_g, ssm_w_out,
           gqa_w_qkv, gqa_q_norm, gqa_k_norm, gqa_w_o, final_norm_g):
    f32 = np.float32
    x = np.asarray(x, f32); ctx = np.asarray(ctx, f32)
    cc = np.ascontiguousarray(np.stack([np.asarray(c, f32).reshape(D), np.asarray(c_ctx, f32).reshape(D)]).reshape(2, 16, 128).transpose(2, 1, 0))
    ada_w = np.asarray(ada_w, f32); ada_b = np.asarray(ada_b, f32)
    ins = []
    for j in range(NCORES):
        sl = slice(j * MODC, (j + 1) * MODC)
        ab = np.ascontiguousarray(ada_b[:, sl].reshape(4 * MODC))
        ins.append({"cc": cc, "aw": np.ascontiguousarray(ada_w[:, :, sl]).reshape(4 * D, MODC), "ab": np.stack([ab, ab])})
    outs = _run(_prog("mod", build_mod), ins)
    mods = []
    for l in range(4):
        m = np.concatenate([outs[j]["mod"][:, l * MODC:(l + 1) * MODC] for j in range(NCORES)], axis=1)
        mods.append(m.reshape(2, 9, D))
    xg = np.concatenate([x[0].T, ctx[0].T], axis=1)
    hT = to_cores(xg)
    cosF, sinF = rope_tables()
    rotm = rot_matrix()
    norm_g = np.asarray(norm_g, f32)

    def ffn(i, k, final):
        nonlocal hT
        mv = make_mv(norm_g[i, 2 * k], mods[i], 6 * k, np.asarray(final_norm_g, f32) if final else None)
        w_in = np.asarray(ffn_w_in[i, k], f32); w_out = np.asarray(ffn_w_out[i, k], f32)
        nc = _prog("ffn_final" if final else "ffn", lambda: build_ffn(final_norm=final))
        outs = _run(nc, [{"hT": hT[j], "w_in": w_in, "w_out": w_out, "mv": mv} for j in range(NCORES)])
        hT = [o["hT_out"] for o in outs]

    def oproj(aT, w, mv, kc, ssm, ng=None):
        nonlocal hT
        nc = _prog(("oproj", kc, ssm), lambda: build_oproj(kc, ssm))
        ins = []
        for j in range(NCORES):
            d = {"aT": aT[j], "w": w, "hT": hT[j], "mv": mv}
            if ssm:
                d["ng"] = ng
            ins.append(d)
        outs = _run(nc, ins)
        hT = [o["hT_out"] for o in outs]

    for i in range(4):
        ffn(i, 0, False)
        kind, jj = i % 3, i // 3
        mv = make_mv(norm_g[i, 1], mods[i], 3)
        if kind == 0:
            w = np.asarray(na_w_qkv[jj], f32)
            outs = _run(_prog("proj_na", lambda: build_proj(6144, "na")), [{"hT": hT[j], "w": w, "mv": mv} for j in range(NCORES)])
            qkv = to_global([o["oT"] for o in outs])
            bmt = na_bias_tables(np.asarray(na_rpb[jj], f32))
            npat = bmt.shape[1]
            ins = []
            for j in range(NCORES):
                r0 = 2 * j * 128
                vv = qkv[4096 + r0:4096 + r0 + 256].reshape(2, 128, NTOK).transpose(0, 2, 1)
                ins.append({"qT": np.ascontiguousarray(qkv[r0:r0 + 256]), "kT": np.ascontiguousarray(qkv[2048 + r0:2048 + r0 + 256]),
                            "v": np.ascontiguousarray(vv).reshape(2 * NTOK, 128),
                            "bm": np.ascontiguousarray(bmt[2 * j:2 * j + 2]).reshape(2 * npat * 128, 448)})
            outs = _run(_prog("na", build_na), ins)
            aT = heads_to_T([o["o"] for o in outs])
            oproj(aT, np.asarray(na_w_o[jj], f32), mv, 16, False)
        elif kind == 1:
            w = np.asarray(ssm_w_in[jj], f32)
            outs = _run(_prog("proj_ssm", lambda: build_proj(10368, "ssm")), [{"hT": hT[j], "w": w, "mv": mv} for j in range(NCORES)])
            zx = to_global([o["oT"] for o in outs])
            cwf = np.asarray(ssm_conv_w[jj], f32); cbf = np.asarray(ssm_conv_b[jj], f32)
            consts = ssd_consts()
            ins = []
            for j in range(NCORES):
                chans = np.concatenate([np.arange(512 * j, 512 * j + 512), 4096 + np.arange(128 * j, 128 * j + 128),
                                        5120 + np.arange(128 * j, 128 * j + 128)])
                hd = np.concatenate([np.arange(8 * j, 8 * j + 8), 64 + np.arange(8 * j, 8 * j + 8)])
                vecs = np.stack([np.asarray(ssm_dt_bias[jj], f32).reshape(128)[hd], np.asarray(ssm_a_log[jj], f32).reshape(128)[hd],
                                 np.asarray(ssm_d[jj], f32).reshape(128)[hd]])
                ins.append({"xbc": np.ascontiguousarray(zx[4096 + chans]),
                            "cw": np.ascontiguousarray(cwf[:, chans].T.reshape(6, 128, 5).transpose(1, 0, 2)),
                            "cb": np.ascontiguousarray(cbf[chans].reshape(6, 128).T),
                            "dtr": np.ascontiguousarray(zx[10240 + hd].T),
                            "vecs": np.ascontiguousarray(np.broadcast_to(vecs[None], (128, 3, 16))),
                            "z": np.ascontiguousarray(zx[512 * j:512 * j + 512].T),
                            "consts": consts})
            outs = _run(_prog("ssd", build_ssd), ins)
            ygT = np.concatenate([o["yg"].T for o in outs], axis=0)
            oproj(to_cores(ygT), np.asarray(ssm_w_out[jj], f32), mv, 32, True, fm(np.asarray(ssm_norm_g[jj], f32)))
        else:
            w = np.asarray(gqa_w_qkv[jj], f32)
            qkn = np.ascontiguousarray(np.stack([np.asarray(gqa_q_norm[jj], f32), np.asarray(gqa_k_norm[jj], f32)], axis=1))
            ins = [{"hT": hT[j], "w": w, "mv": mv, "qkn": qkn, "cosF": np.ascontiguousarray(cosF[:, j * TL:(j + 1) * TL]),
                    "sinF": np.ascontiguousarray(sinF[:, j * TL:(j + 1) * TL]), "rotm": rotm} for j in range(NCORES)]
            outs = _run(_prog("proj_gqa", lambda: build_proj(3072, "gqa")), ins)
            qkv = to_global([o["oT"] for o in outs])
            ins = []
            for j in range(NCORES):
                kv = j // 2
                ins.append({"qT": np.ascontiguousarray(qkv[2 * j * 128:2 * j * 128 + 256]),
                            "kT": np.ascontiguousarray(qkv[2048 + kv * 128:2048 + (kv + 1) * 128]),
                            "v": np.ascontiguousarray(qkv[2560 + kv * 128:2560 + (kv + 1) * 128].T)})
            outs = _run(_prog("gqa", build_gqa), ins)
            aT = heads_to_T([o["o"] for o in outs])
            oproj(aT, np.asarray(gqa_w_o[jj], f32), mv, 16, False)
        ffn(i, 1, i == 3)
    out = np.concatenate([hT[j][:, :TL].T for j in range(NCORES)], axis=0)
    return np.ascontiguousarray(out[None]).astype(np.float32)
```

```python
from contextlib import ExitStack
import numpy as np
import concourse.bass as bass
import concourse.mybir as mybir
from concourse.alu_op_type import AluOpType as ALU
from concourse.bass_utils import run_bass_kernel_spmd

F32 = mybir.dt.float32
BF16 = mybir.dt.bfloat16
AF = mybir.ActivationFunctionType

NCORES = 8
D = 2048
DFF = 5632
SEQ = 8192
CTX = 256
TL = SEQ // NCORES
TC = CTX // NCORES
T = TL + TC
EPS = 1e-6
TILES = [(0, 512), (512, 512), (1024, TC)]
NEG = -30000.0


class Dep:
    __slots__ = ("w", "r")

    def __init__(self):
        self.w = None
        self.r = {}


class Sched:
    def __init__(self, nc, dma_ring=6):
        self.nc = nc
        self.engs = {"pe": nc.tensor, "act": nc.scalar, "dve": nc.vector,
                     "pool": nc.gpsimd, "sp": nc.sync}
        self.sems = []
        self.eng_sem = {}
        self.eng_cnt = {}
        self.seen = {e: {} for e in self.engs}
        self.ring = {}
        self.ring_pos = {}
        self.sem_val = {}
        for e in self.engs:
            self.eng_sem[e] = self._new_sem("c_" + e)
            self.eng_cnt[e] = 0
        for e in ("sp", "act", "pool"):
            self.ring[e] = [self._new_sem(f"d_{e}{i}") for i in range(dma_ring)]
            self.ring_pos[e] = 0
        self.n_wait = 0
        self.n_inst = 0
        self.stack = ExitStack()
        self._tn = 0

    def sb(self, shape, dt, name=None):
        self._tn += 1
        return self.stack.enter_context(self.nc.sbuf_tensor(f"{name or 'sb'}_{self._tn}", list(shape), dt))

    def ps(self, shape, dt=F32, name=None):
        self._tn += 1
        return self.stack.enter_context(self.nc.psum_tensor(f"{name or 'ps'}_{self._tn}", list(shape), dt))

    def _new_sem(self, name):
        h = self.nc.alloc_semaphore(name=name)
        self.sems.append(h)
        self.sem_val[len(self.sems) - 1] = 0
        return len(self.sems) - 1

    def _need(self, reads, writes):
        need = {}
        for d in reads:
            if d.w is not None:
                s, v = d.w
                if need.get(s, 0) < v:
                    need[s] = v
        for d in writes:
            if d.w is not None:
                s, v = d.w
                if need.get(s, 0) < v:
                    need[s] = v
            for s, v in d.r.items():
                if need.get(s, 0) < v:
                    need[s] = v
        return need

    def _emit_waits(self, eng, need):
        E = self.engs[eng]
        seen = self.seen[eng]
        own = self.eng_sem[eng]
        for s, v in need.items():
            if eng == "pe" and s == own:
                continue
            if seen.get(s, 0) < v:
                E.wait_ge(self.sems[s], v)
                seen[s] = v
                self.n_wait += 1

    def _record(self, ev, reads, writes):
        s, v = ev
        for d in reads:
            d.r[s] = v
        for d in writes:
            d.w = ev
            d.r = {}

    def op(self, eng, fn, reads=(), writes=()):
        need = self._need(reads, writes)
        self._emit_waits(eng, need)
        inst = fn(self.engs[eng])
        s = self.eng_sem[eng]
        self.eng_cnt[eng] += 1
        v = self.eng_cnt[eng]
        inst.then_inc(self.sems[s], 1)
        self._record((s, v), reads, writes)
        self.n_inst += 1
        return (s, v)

    def dma(self, eng, out, in_, reads=(), writes=(), **kw):
        need = self._need(reads, writes)
        ring = self.ring[eng]
        s = ring[self.ring_pos[eng] % len(ring)]
        self.ring_pos[eng] += 1
        prev = self.sem_val[s]
        if prev > 0:
            need[s] = max(need.get(s, 0), prev)
        self._emit_waits(eng, need)
        inst = self.engs[eng].dma_start(out=out, in_=in_, **kw)
        inst.then_inc(self.sems[s], 16)
        self.sem_val[s] = prev + 16
        ev = (s, prev + 16)
        self._record(ev, reads, writes)
        self.n_inst += 1
        return ev

    def cc(self, kind, ins, outs, reads=(), writes=()):
        need = self._need(reads, writes)
        if not hasattr(self, "cc_sem"):
            self.cc_sem = self._new_sem("cc")
        s = self.cc_sem
        prev = self.sem_val[s]
        if prev > 0:
            need[s] = max(need.get(s, 0), prev)
        self._emit_waits("pool", need)
        inst = self.engs["pool"].collective_compute(kind, ALU.bypass, replica_groups=[list(range(NCORES))], ins=ins, outs=outs)
        inst.then_inc(self.sems[s])
        self.sem_val[s] = prev + 1
        ev = (s, prev + 1)
        self._record(ev, reads, writes)
        self.n_inst += 1
        return ev

    def barrier(self):
        need = {s: v for s, v in self.sem_val.items() if v > 0}
        for e, s in self.eng_sem.items():
            if self.eng_cnt[e] > 0:
                need[s] = self.eng_cnt[e]
        for e in self.engs:
            self._emit_waits(e, dict(need))

    def phase_begin(self):
        self.gstack = getattr(self, "gstack", None) or self.stack
        self.stack = ExitStack()

    def phase_end(self):
        self.barrier()
        self.stack.close()
        self.stack = self.gstack

    def finish(self, out_deps=()):
        self.barrier()
        self.stack.close()


def new_nc():
    return bass.Bass("TRN2", target_bir_lowering=False)


class RR:
    def __init__(self, items):
        self.items = items
        self.i = 0

    def next(self):
        it = self.items[self.i % len(self.items)]
        self.i += 1
        return it


def emit_norm_mod(S, h, dh, xn, dxn, gm_l, sh_l, gm_c, sh_c, dmv, ones, done, ps_stat, dps_stat,
                  nch=16, tiles=TILES, eng_sq="pool"):
    sq = [(S.sb([128, 512], F32), Dep()) for _ in range(2)]
    sqr = RR(sq)
    rt = S.sb([128, 512], F32)
    drt = Dep()
    tmp = RR([(S.sb([128, 512], F32), Dep()) for _ in range(2)])
    for ti, (t0, n) in enumerate(tiles):
        gm, sh = (gm_l, sh_l) if ti < len(tiles) - 1 else (gm_c, sh_c)
        for c in range(nch):
            sqb, dsq = sqr.next()
            S.op(eng_sq, lambda e: e.tensor_tensor(out=sqb[:, :n], in0=h[:, c, t0:t0 + n], in1=h[:, c, t0:t0 + n], op=ALU.mult),
                 reads=[dh[ti]], writes=[dsq])
            S.op("pe", lambda e: e.matmul(ps_stat[:, :n], lhsT=ones[:, :], rhs=sqb[:, :n], start=(c == 0), stop=(c == nch - 1)),
                 reads=[dsq, done], writes=[dps_stat])
        S.op("act", lambda e: e.activation(out=rt[:, :n], in_=ps_stat[:, :n], func=AF.Sqrt, bias=epsb[0][:, 0:1], scale=1.0 / (nch * 128)),
             reads=[dps_stat, epsb[1]], writes=[drt])
        S.op("dve", lambda e: e.reciprocal(out=rt[:, :n], in_=rt[:, :n]), reads=[drt], writes=[drt])
        for c in range(nch):
            tb, dt_ = tmp.next()
            S.op("dve", lambda e: e.tensor_tensor(out=tb[:, :n], in0=h[:, c, t0:t0 + n], in1=rt[:, :n], op=ALU.mult),
                 reads=[dh[ti], drt], writes=[dt_])
            S.op("act", lambda e: e.activation(out=xn[:, c, t0:t0 + n], in_=tb[:, :n], func=AF.Identity,
                                               bias=sh[:, c:c + 1], scale=gm[:, c:c + 1]),
                 reads=[dt_, dmv], writes=[dxn[ti]])


epsb = [None, None]


def make_consts(S):
    ones = S.sb([128, 128], F32)
    done = Dep()
    S.op("pool", lambda e: e.memset(ones[:], 1.0), writes=[done])
    eb = S.sb([128, 1], F32)
    deb = Dep()
    S.op("pool", lambda e: e.memset(eb[:], EPS), writes=[deb])
    epsb[0], epsb[1] = eb, deb
    return ones, done


I32 = mybir.dt.int32
MODC = 9 * D // NCORES
NTOK = SEQ + CTX
NKC = NTOK // 128
ROWS = SEQ // 64
OPAD = 384


class G_:
    pass


def lat_ctx_views(ap2d, jreg):
    lat = ap2d[:, 0:SEQ].rearrange("x (j t) -> j x t", t=TL)[jreg]
    ctx = ap2d[:, SEQ:NTOK].rearrange("x (j t) -> j x t", t=TC)[jreg]
    return lat, ctx


def emit_mod(S, G):
    S.phase_begin()
    sc = S.sb([128, 16, 2], F32); dsc = Dep()
    S.dma("sp", sc[:], G.cc[:, :, :], writes=[dsc])
    S.op("act", lambda e: e.activation(out=sc[:], in_=sc[:], func=AF.Silu), reads=[dsc], writes=[dsc])
    abs_ = S.sb([128, 72], F32); dab = Dep()
    S.dma("sp", abs_[:], G.abT[:, :], writes=[dab])
    res = S.sb([128, 2, 72], F32); dres = Dep()
    wb = RR([(S.sb([128, 16, 512], F32), Dep()) for _ in range(2)])
    ps = S.ps([128, 72, 2]); dps = Dep()
    k = 0
    for l in range(4):
        awv = G.aw[l * D:(l + 1) * D, :].rearrange("(c p) n -> p c n", p=128)
        for c0 in range(0, MODC, 512):
            n = min(512, MODC - c0)
            w, dw = wb.next()
            S.dma("sp" if k % 2 == 0 else "act", w[:, :, :n], awv[:, :, c0:c0 + n], writes=[dw])
            k += 1
            for q in range(n // 128):
                col = l * 18 + c0 // 128 + q
                for c in range(16):
                    S.op("pe", lambda e: e.matmul(ps[:, col, :], lhsT=w[:, c, q * 128:(q + 1) * 128], rhs=sc[:, c, :], start=(c == 0), stop=(c == 15)),
                         reads=[dw, dsc], writes=[dps])
    for r in range(2):
        S.op("dve", lambda e: e.tensor_tensor(out=res[:, r, :], in0=ps[:, :, r], in1=abs_[:], op=ALU.add), reads=[dps, dab], writes=[dres])
    dmod = Dep(); dall = Dep()
    S.dma("sp", G.modT_local[:, :], res[:].rearrange("p r q -> p (r q)"), reads=[dres], writes=[dmod])
    S.cc("AllGather", [G.modT_local], [G.modT_all], reads=[dmod], writes=[dall])
    S.phase_end()


def load_mv(S, G, mvs, dmv, l, base, gidx, final=False):
    S.dma("sp", mvs[:, 0, :], G.ng_all[:, gidx, :], writes=[dmv])
    S.dma("sp", mvs[:, 7, :], G.ng_all[:, 12 if final else gidx, :], writes=[dmv])
    for r in range(2):
        for mi in range(3):
            m = base + mi
            row = 1 + r * 3 + mi
            c = 0
            while c < 16:
                q = 16 * m + c
                jq, ql = q // 18, q % 18
                run = min(16 - c, 18 - ql)
                col = r * 72 + l * 18 + ql
                S.dma("sp", mvs[:, row, c:c + run], G.modT_all[jq * 128:(jq + 1) * 128, col:col + run], writes=[dmv])
                c += run


def mv_prep(S, mvs, dmv, need_gate_half):
    gm_l = S.sb([128, 16], F32); gm_c = S.sb([128, 16], F32)
    S.op("dve", lambda e: e.scalar_tensor_tensor(out=gm_l[:], in0=mvs[:, 2, :], scalar=1.0, in1=mvs[:, 0, :], op0=ALU.add, op1=ALU.mult), reads=[dmv], writes=[dmv])
    S.op("dve", lambda e: e.scalar_tensor_tensor(out=gm_c[:], in0=mvs[:, 5, :], scalar=1.0, in1=mvs[:, 0, :], op0=ALU.add, op1=ALU.mult), reads=[dmv], writes=[dmv])
    hg_l = hg_c = None
    if need_gate_half:
        hg_l = S.sb([128, 16], F32); hg_c = S.sb([128, 16], F32)
        S.op("dve", lambda e: e.tensor_scalar(out=hg_l[:], in0=mvs[:, 3, :], scalar1=0.5, scalar2=None, op0=ALU.mult), reads=[dmv], writes=[dmv])
        S.op("dve", lambda e: e.tensor_scalar(out=hg_c[:], in0=mvs[:, 6, :], scalar1=0.5, scalar2=None, op0=ALU.mult), reads=[dmv], writes=[dmv])
    return gm_l, gm_c, hg_l, hg_c


def emit_ffn(S, G, hin, hout, w_in, w_out, l, k, final_norm=False):
    S.phase_begin()
    ones, done = G.ones, G.done
    NCH = D // 128
    h = S.sb([128, NCH, T], F32, "h")
    dh = [Dep() for _ in TILES]
    hv = hin.rearrange("(c p) t -> p c t", p=128)
    for ti, (t0, n) in enumerate(TILES):
        S.dma("sp", h[:, :, t0:t0 + n], hv[:, :, t0:t0 + n], writes=[dh[ti]])
    mvs = S.sb([128, 8, 16], F32, "mvs"); dmv = Dep()
    load_mv(S, G, mvs, dmv, l, 6 * k, 3 * l + 2 * k, final_norm)
    gm_l, gm_c, hg_l, hg_c = mv_prep(S, mvs, dmv, True)
    xn = S.sb([128, NCH, T], BF16, "xn")
    dxn = [Dep() for _ in TILES]
    ps_stat = S.ps([128, 512]); dps_stat = Dep()
    emit_norm_mod(S, h, dh, xn, dxn, gm_l, mvs[:, 1, :], gm_c, mvs[:, 4, :], dmv, ones, done, ps_stat, dps_stat)
    w_in_v = w_in.rearrange("(c p) n -> p c n", p=128)
    w_out_v = w_out.rearrange("(c p) n -> p c n", p=128)
    GC = 4
    NG = DFF // 128 // GC
    wab = RR([(S.sb([128, 2, NCH, 256], BF16), Dep()) for _ in range(2)])
    wo = RR([(S.sb([128, GC, 512], BF16), Dep()) for _ in range(2)])
    gbuf = RR([(S.sb([128, GC, T], BF16), [Dep() for _ in TILES]) for _ in range(2)])
    psA = RR([(S.ps([128, 512]), Dep()) for _ in range(2)])
    psB = RR([(S.ps([128, 512]), Dep()) for _ in range(2)])
    psO = RR([(S.ps([128, 512]), Dep()) for _ in range(2)])
    sa = RR([(S.sb([128, 512], F32), Dep()) for _ in range(2)])
    for gi in range(NG):
        g, dg = gbuf.next()
        for b2 in range(GC // 2):
            f0 = (gi * GC + b2 * 2) * 128
            w, dw = wab.next()
            S.dma("pool", w[:, 0], w_in_v[:, :, f0:f0 + 256], writes=[dw])
            S.dma("pool", w[:, 1], w_in_v[:, :, DFF + f0:DFF + f0 + 256], writes=[dw])
            for j in range(2):
                fl = b2 * 2 + j
                for ti, (t0, n) in enumerate(TILES):
                    pa, dpa = psA.next()
                    pb, dpb = psB.next()
                    for c in range(NCH):
                        S.op("pe", lambda e: e.matmul(pa[:, :n], lhsT=w[:, 0, c, j * 128:(j + 1) * 128], rhs=xn[:, c, t0:t0 + n],
                                                      start=(c == 0), stop=(c == NCH - 1)), reads=[dw, dxn[ti]], writes=[dpa])
                    for c in range(NCH):
                        S.op("pe", lambda e: e.matmul(pb[:, :n], lhsT=w[:, 1, c, j * 128:(j + 1) * 128], rhs=xn[:, c, t0:t0 + n],
                                                      start=(c == 0), stop=(c == NCH - 1)), reads=[dw, dxn[ti]], writes=[dpb])
                    sab, dsa = sa.next()
                    S.op("act", lambda e: e.activation(out=sab[:, :n], in_=pa[:, :n], func=AF.Silu), reads=[dpa], writes=[dsa])
                    S.op("dve", lambda e: e.tensor_tensor(out=g[:, fl, t0:t0 + n], in0=sab[:, :n], in1=pb[:, :n], op=ALU.mult),
                         reads=[dsa, dpb], writes=[dg[ti]])
        for q in range(4):
            wob, dwo = wo.next()
            S.dma("pool", wob[:], w_out_v[:, gi * GC:(gi + 1) * GC, q * 512:(q + 1) * 512], writes=[dwo])
            for dcl in range(4):
                dc = q * 4 + dcl
                for ti, (t0, n) in enumerate(TILES):
                    po, dpo = psO.next()
                    for j in range(GC):
                        S.op("pe", lambda e: e.matmul(po[:, :n], lhsT=wob[:, j, dcl * 128:(dcl + 1) * 128], rhs=g[:, j, t0:t0 + n],
                                                      start=(j == 0), stop=(j == GC - 1)), reads=[dwo, dg[ti]], writes=[dpo])
                    hg = hg_l if ti < len(TILES) - 1 else hg_c
                    S.op("dve", lambda e: e.scalar_tensor_tensor(out=h[:, dc, t0:t0 + n], in0=po[:, :n], scalar=hg[:, dc:dc + 1],
                                                                 in1=h[:, dc, t0:t0 + n], op0=ALU.mult, op1=ALU.add),
                         reads=[dpo, dmv, dh[ti]], writes=[dh[ti]])
    dout = Dep()
    ov = hout.rearrange("(c p) t -> p c t", p=128)
    if final_norm:
        zero = S.sb([128, 16], F32)
        S.op("pool", lambda e: e.memset(zero[:], 0.0), writes=[dmv])
        emit_norm_mod(S, h, dh, h, dh, mvs[:, 7, :], zero, mvs[:, 7, :], zero, dmv, ones, done, ps_stat, dps_stat)
    for ti, (t0, n) in enumerate(TILES):
        S.dma("sp", ov[:, :, t0:t0 + n], h[:, :, t0:t0 + n], reads=[dh[ti]], writes=[dout])
    S.phase_end()


def emit_proj(S, G, hin, w, nout, kind, l, out_chunk, st_dt=F32):
    S.phase_begin()
    ones, done = G.ones, G.done
    NCH = D // 128
    h = S.sb([128, NCH, T], F32, "h")
    dh = [Dep() for _ in TILES]
    hv = hin.rearrange("(c p) t -> p c t", p=128)
    for ti, (t0, n) in enumerate(TILES):
        S.dma("sp", h[:, :, t0:t0 + n], hv[:, :, t0:t0 + n], writes=[dh[ti]])
    mvs = S.sb([128, 8, 16], F32, "mvs"); dmv = Dep()
    load_mv(S, G, mvs, dmv, l, 3, 3 * l + 1)
    gm_l, gm_c, _, _ = mv_prep(S, mvs, dmv, False)
    xn = S.sb([128, NCH, T], BF16, "xn")
    dxn = [Dep() for _ in TILES]
    ps_stat = S.ps([128, 512]); dps_stat = Dep()
    emit_norm_mod(S, h, dh, xn, dxn, gm_l, mvs[:, 1, :], gm_c, mvs[:, 4, :], dmv, ones, done, ps_stat, dps_stat)
    if kind == "gqa":
        qk = S.sb([128, 2], F32); cs_ = S.sb([128, TL], F32); sn_ = S.sb([128, TL], F32); rm = S.sb([128, 128], F32)
        dq = Dep()
        S.dma("sp", qk[:], G.qkn[:, :], writes=[dq])
        S.dma("sp", cs_[:], G.cosF[:, :], writes=[dq])
        S.dma("sp", sn_[:], G.sinF[:, :], writes=[dq])
        S.dma("sp", rm[:], G.rotm[:, :], writes=[dq])
        xs = RR([(S.sb([128, 512], F32), Dep()) for _ in range(2)])
        sqs = RR([(S.sb([128, 512], F32), Dep()) for _ in range(2)])
        rts = RR([(S.sb([128, 512], F32), Dep()) for _ in range(2)])
        t1s = RR([(S.sb([128, 512], F32), Dep()) for _ in range(2)])
        t2s = RR([(S.sb([128, 512], F32), Dep()) for _ in range(2)])
        ps_rot = S.ps([128, 512]); dps_rot = Dep()
    BW = 384
    wv = w.rearrange("(c p) n -> p c n", p=128)
    wblk = RR([(S.sb([128, NCH, BW], BF16), Dep()) for _ in range(2)])
    pso = RR([(S.ps([128, 512]), Dep()) for _ in range(3)])
    stg = RR([(S.sb([128, T], st_dt), Dep()) for _ in range(3)])
    for b in range(nout // BW):
        wb, dw = wblk.next()
        S.dma("pool", wb[:], wv[:, :, b * BW:(b + 1) * BW], writes=[dw])
        for j in range(BW // 128):
            oc = b * (BW // 128) + j
            st, dst = stg.next()
            for ti, (t0, n) in enumerate(TILES):
                p, dp = pso.next()
                for c in range(NCH):
                    S.op("pe", lambda e: e.matmul(p[:, :n], lhsT=wb[:, c, j * 128:(j + 1) * 128], rhs=xn[:, c, t0:t0 + n],
                                                  start=(c == 0), stop=(c == NCH - 1)), reads=[dw, dxn[ti]], writes=[dp])
                if kind == "gqa" and oc < 20:
                    col = 0 if oc < 16 else 1
                    x, dx = xs.next(); sq, dsq = sqs.next(); rt, drt = rts.next()
                    S.op("act", lambda e: e.copy(out=x[:, :n], in_=p[:, :n]), reads=[dp], writes=[dx])
                    S.op("pool", lambda e: e.tensor_tensor(out=sq[:, :n], in0=x[:, :n], in1=x[:, :n], op=ALU.mult), reads=[dx], writes=[dsq])
                    S.op("pe", lambda e: e.matmul(ps_stat[:, :n], lhsT=ones[:, :], rhs=sq[:, :n], start=True, stop=True),
                         reads=[dsq, done], writes=[dps_stat])
                    S.op("act", lambda e: e.activation(out=rt[:, :n], in_=ps_stat[:, :n], func=AF.Sqrt, bias=epsb[0][:, 0:1], scale=1.0 / 128),
                         reads=[dps_stat, epsb[1]], writes=[drt])
                    S.op("dve", lambda e: e.reciprocal(out=rt[:, :n], in_=rt[:, :n]), reads=[drt], writes=[drt])
                    if ti < len(TILES) - 1:
                        S.op("dve", lambda e: e.scalar_tensor_tensor(out=x[:, :n], in0=x[:, :n], scalar=qk[:, col:col + 1], in1=rt[:, :n], op0=ALU.mult, op1=ALU.mult),
                             reads=[dx, drt, dq], writes=[dx])
                        S.op("pe", lambda e: e.matmul(ps_rot[:, :n], lhsT=rm[:, :], rhs=x[:, :n], start=True, stop=True),
                             reads=[dx, dq], writes=[dps_rot])
                        t1, dt1 = t1s.next(); t2, dt2 = t2s.next()
                        S.op("pool", lambda e: e.tensor_tensor(out=t1[:, :n], in0=x[:, :n], in1=cs_[:, t0:t0 + n], op=ALU.mult), reads=[dx, dq], writes=[dt1])
                        S.op("dve", lambda e: e.tensor_tensor(out=t2[:, :n], in0=ps_rot[:, :n], in1=sn_[:, t0:t0 + n], op=ALU.mult), reads=[dps_rot, dq], writes=[dt2])
                        S.op("dve", lambda e: e.tensor_tensor(out=st[:, t0:t0 + n], in0=t1[:, :n], in1=t2[:, :n], op=ALU.add), reads=[dt1, dt2], writes=[dst])
                    else:
                        S.op("dve", lambda e: e.scalar_tensor_tensor(out=st[:, t0:t0 + n], in0=x[:, :n], scalar=qk[:, col:col + 1], in1=rt[:, :n], op0=ALU.mult, op1=ALU.mult),
                             reads=[dx, drt, dq], writes=[dst])
                else:
                    if (oc + ti) % 2 == 0:
                        S.op("act", lambda e: e.copy(out=st[:, t0:t0 + n], in_=p[:, :n]), reads=[dp], writes=[dst])
                    else:
                        S.op("dve", lambda e: e.tensor_copy(out=st[:, t0:t0 + n], in_=p[:, :n]), reads=[dp], writes=[dst])
            out_chunk(oc, st, dst)


def emit_oproj(S, G, a_src, w, hin, hout, l, kc, ssm, zT=None):
    S.phase_begin()
    ones, done = G.ones, G.done
    mvs = S.sb([128, 8, 16], F32, "mvs"); dmv = Dep()
    load_mv(S, G, mvs, dmv, l, 3, 3 * l + 1)
    a = S.sb([128, kc, T], BF16, "a")
    da = [Dep() for _ in TILES]
    groups = a_src("sp" if ssm else "pool")

    def asrc(g, t0, n):
        return g[2][:, :, t0:t0 + n] if t0 < TL else g[3][:, :, t0 - TL:t0 - TL + n]
    if not ssm:
        for ti, (t0, n) in enumerate(TILES):
            for g in groups:
                S.dma("pool", a[:, g[0]:g[0] + g[1], t0:t0 + n], asrc(g, t0, n), writes=[da[ti]])
    else:
        ngs = S.sb([128, kc], F32); zero = S.sb([128, kc], F32)
        S.dma("sp", ngs[:], G.ssm_ng[:, :], writes=[dmv])
        S.op("pool", lambda e: e.memset(zero[:], 0.0), writes=[dmv])
        ps_stat = S.ps([128, 512]); dps_stat = Dep()
        ytile = S.sb([128, kc, 512], F32, "ytile"); dyt = Dep()
        zv = zT.rearrange("(c p) t -> p c t", p=128)
        yb = RR([(S.sb([128, 512], F32), Dep()) for _ in range(2)])
        zb = RR([(S.sb([128, 512], F32), Dep()) for _ in range(2)])
        for ti, (t0, n) in enumerate(TILES):
            for c in range(kc):
                y_, dy_ = yb.next(); z_, dz_ = zb.next()
                S.dma("sp", y_[:, :n], asrc(groups[0], t0, n)[:, c, :], writes=[dy_])
                S.dma("sp", z_[:, :n], zv[:, c, t0:t0 + n], writes=[dz_])
                S.op("act", lambda e: e.activation(out=z_[:, :n], in_=z_[:, :n], func=AF.Silu), writes=[dz_])
                S.op("dve", lambda e: e.tensor_tensor(out=ytile[:, c, :n], in0=y_[:, :n], in1=z_[:, :n], op=ALU.mult), reads=[dy_, dz_], writes=[dyt])

            class _V:
                def __getitem__(self, idx):
                    return ytile[:, idx[1], 0:n]
            emit_norm_mod(S, _V(), [dyt], a[:, :, t0:t0 + n], [da[ti]], ngs, zero, ngs, zero, dmv, ones, done, ps_stat, dps_stat,
                          nch=kc, tiles=[(0, n)])
    wv = w.rearrange("(c p) n -> p c n", p=128)
    hv = hin.rearrange("(c p) t -> p c t", p=128)
    ov = hout.rearrange("(c p) t -> p c t", p=128)
    BW2 = 128 if ssm else 256
    wblk = RR([(S.sb([128, kc, BW2], BF16), Dep()) for _ in range(2)])
    hs = RR([(S.sb([128, T], F32), Dep()) for _ in range(3)])
    pso = RR([(S.ps([128, 512]), Dep()) for _ in range(3)])
    dout = Dep()
    for b in range(D // BW2):
        wb, dw = wblk.next()
        S.dma("pool", wb[:], wv[:, :, b * BW2:(b + 1) * BW2], writes=[dw])
        for j in range(BW2 // 128):
            dc = b * (BW2 // 128) + j
            hb, dhb = hs.next()
            S.dma("sp", hb[:], hv[:, dc, :], writes=[dhb])
            for ti, (t0, n) in enumerate(TILES):
                p, dp = pso.next()
                for c in range(kc):
                    S.op("pe", lambda e: e.matmul(p[:, :n], lhsT=wb[:, c, j * 128:(j + 1) * 128], rhs=a[:, c, t0:t0 + n],
                                                  start=(c == 0), stop=(c == kc - 1)), reads=[dw, da[ti]], writes=[dp])
                gcol = 3 if ti < len(TILES) - 1 else 6
                S.op("dve", lambda e: e.scalar_tensor_tensor(out=hb[:, t0:t0 + n], in0=p[:, :n], scalar=mvs[:, gcol, dc:dc + 1],
                                                             in1=hb[:, t0:t0 + n], op0=ALU.mult, op1=ALU.add),
                     reads=[dp, dmv, dhb], writes=[dhb])
            S.dma("sp", ov[:, dc, :], hb[:], reads=[dhb], writes=[dout])
    S.phase_end()


def na_row_info(r):
    rs = min(max(r - 4, 0), ROWS - 8)
    cb = min(rs // 2, 59)
    return rs, cb, (rs - 2 * cb, r - 2 * cb)


def na_patterns():
    pats = []
    for r in range(ROWS):
        k = na_row_info(r)[2]
        if k not in pats:
            pats.append(k)
    return pats


def na_bias_tables(rpb):
    pats = na_patterns()
    out = np.zeros((rpb.shape[0], len(pats), 128, 7 * 64), np.float32)
    c = np.arange(64)
    cst = np.clip(c - 8, 0, 48)
    kcol = np.arange(64)
    for pi, (a, b) in enumerate(pats):
        for i in range(5):
            for half in range(2):
                lr = 2 * i + half
                blk = np.full((rpb.shape[0], 64, 64), NEG, np.float32)
                if a <= lr < a + 8:
                    dr = lr - b + 7
                    dcol = kcol[:, None] - c[None, :] + 15
                    ok = (kcol[:, None] >= cst[None, :]) & (kcol[:, None] < cst[None, :] + 16)
                    g = rpb[:, dr][:, np.clip(dcol, 0, 30)]
                    blk = np.where(ok[None], g, np.float32(NEG)).astype(np.float32)
                out[:, pi, half * 64:(half + 1) * 64, i * 64:(i + 1) * 64] = blk
    return out


def load_feat(S, eng, dst, src_lat, src_ctx, dep):
    for r0 in range(0, 8, 2):
        S.dma(eng, dst[:, r0 * TL:(r0 + 2) * TL].rearrange("d (r t) -> d r t", t=TL), src_lat[r0:r0 + 2].rearrange("r d t -> d r t"), writes=[dep])
    S.dma(eng, dst[:, SEQ:NTOK].rearrange("d (r t) -> d r t", t=TC), src_ctx.rearrange("r d t -> d r t"), writes=[dep])


def transpose_v(S, G, vT, V, dsrc, ddst, pst, dpst):
    for g0 in range(0, NKC, 4):
        grp = list(range(g0, min(g0 + 4, NKC)))
        for i, kc in enumerate(grp):
            S.op("pe", lambda e: e.transpose(out=pst[:, i, :], in_=vT[:, kc * 128:(kc + 1) * 128], identity=G.identb[:, :]),
                 reads=[dsrc, G.dident], writes=[dpst])
        if (g0 // 4) % 2 == 0:
            S.op("act", lambda e: e.copy(out=V[:, g0:g0 + len(grp), :], in_=pst[:, 0:len(grp), :]), reads=[dpst], writes=[ddst])
        else:
            S.op("dve", lambda e: e.tensor_copy(out=V[:, g0:g0 + len(grp), :], in_=pst[:, 0:len(grp), :]), reads=[dpst], writes=[ddst])


def emit_na_core(S, G, src, bm_ap, oT_out, dma_eng="sp"):
    pats = na_patterns()
    npat = len(pats)
    scale = 128.0 ** -0.5
    pst = S.ps([128, 4, 128], BF16); dpst = Dep()
    qs, ks, vs, bms, dld = [], [], [], [], []
    for hh in range(2):
        d = Dep(); dvt = Dep()
        q_ = S.sb([128, NTOK], BF16); k_ = S.sb([128, NTOK], BF16); vT = S.sb([128, NTOK], BF16); v_ = S.sb([128, NKC, 128], BF16)
        b_ = S.sb([128, npat, 448], F32)
        for w_, dst_, dd in ((0, q_, d), (1, k_, d), (2, vT, dvt)):
            sl, sc_ = src(dma_eng, hh, w_)
            load_feat(S, dma_eng, dst_, sl, sc_, dd)
        S.dma("sp", b_[:], bm_ap(hh).rearrange("(n p) f -> p n f", p=128), writes=[d])
        transpose_v(S, G, vT, v_, dvt, d, pst, dpst)
        qs.append(q_); ks.append(k_); vs.append(v_); bms.append(b_); dld.append(d)
    psS = RR([(S.ps([128, 512]), Dep()) for _ in range(2)])
    psO = RR([(S.ps([128, 512]), Dep()) for _ in range(2)])
    psM = RR([(S.ps([128, 512]), Dep()) for _ in range(2)])
    tt = RR([(S.sb([128, 448], F32), Dep()) for _ in range(2)])
    pp = RR([(S.sb([128, 448], BF16), Dep()) for _ in range(2)])
    rc = RR([(S.sb([128, 64], F32), Dep()) for _ in range(2)])
    og = RR([(S.sb([128, 512], F32), Dep()) for _ in range(2)])
    dout = Dep()

    def block(hh, q0, chunks, bias_ap, ostg, dostg, slot):
        nch = len(chunks)
        p, dp = psS.next()
        for i, kc in enumerate(chunks):
            S.op("pe", lambda e: e.matmul(p[:, i * 64:(i + 1) * 64], lhsT=ks[hh][:, kc * 128:(kc + 1) * 128], rhs=qs[hh][:, q0:q0 + 64],
                                          start=True, stop=True), reads=[dld[hh]], writes=[dp])
        pb, dpb = pp.next()
        if bias_ap is not None:
            t, dt_ = tt.next()
            S.op("dve", lambda e: e.scalar_tensor_tensor(out=t[:, :nch * 64], in0=p[:, :nch * 64], scalar=scale, in1=bias_ap,
                                                         op0=ALU.mult, op1=ALU.add), reads=[dp, dld[hh]], writes=[dt_])
            S.op("act", lambda e: e.activation(out=pb[:, :nch * 64], in_=t[:, :nch * 64], func=AF.Exp), reads=[dt_], writes=[dpb])
        else:
            S.op("act", lambda e: e.activation(out=pb[:, :nch * 64], in_=p[:, :nch * 64], func=AF.Exp, scale=scale), reads=[dp], writes=[dpb])
        po, dpo = psO.next(); pm, dpm = psM.next()
        for i, kc in enumerate(chunks):
            S.op("pe", lambda e: e.matmul(po[:, 0:64], lhsT=vs[hh][:, kc, :], rhs=pb[:, i * 64:(i + 1) * 64],
                                          start=(i == 0), stop=(i == nch - 1)), reads=[dpb, dld[hh]], writes=[dpo])
        for i, kc in enumerate(chunks):
            S.op("pe", lambda e: e.matmul(pm[:, 0:64], lhsT=G.onesb[:, :], rhs=pb[:, i * 64:(i + 1) * 64],
                                          start=(i == 0), stop=(i == nch - 1)), reads=[dpb, G.dident], writes=[dpm])
        r_, dr_ = rc.next()
        S.op("dve", lambda e: e.reciprocal(out=r_[:, :], in_=pm[:, 0:64]), reads=[dpm], writes=[dr_])
        S.op("dve", lambda e: e.tensor_tensor(out=ostg[:, slot * 64:(slot + 1) * 64], in0=po[:, 0:64], in1=r_[:, :], op=ALU.mult),
             reads=[dpo, dr_], writes=[dostg])

    for hh in range(2):
        ov = oT_out(hh)
        for r in range(ROWS):
            rs, cb, key = na_row_info(r)
            pi = pats.index(key)
            if r % 8 == 0:
                ostg, dostg = og.next()
            block(hh, r * 64, list(range(cb, cb + 5)) + [64, 65], bms[hh][:, pi, :], ostg, dostg, r % 8)
            if r % 8 == 7:
                r0 = r - 7
                S.dma("sp", ov[:, r0 * 64:(r0 + 8) * 64], ostg[:], reads=[dostg], writes=[dout])
        ostg, dostg = og.next()
        for qb in range(4):
            block(hh, SEQ + qb * 64, [64, 65], None, ostg, dostg, qb)
        S.dma("sp", ov[:, SEQ:SEQ + 256], ostg[:, 0:256], reads=[dostg], writes=[dout])
    return dout


def emit_gqa_core(S, G, src_q, src_kv, oT_out):
    scale = 128.0 ** -0.5
    dld = Dep(); dvt = Dep()
    qs = [S.sb([128, NTOK], BF16) for _ in range(2)]
    k_ = S.sb([128, NTOK], BF16); vT = S.sb([128, NTOK], BF16); v_ = S.sb([128, NKC, 128], BF16)
    pst = S.ps([128, 4, 128], BF16); dpst = Dep()
    for hh in range(2):
        sl, sc_ = src_q("pool", hh)
        load_feat(S, "pool", qs[hh], sl, sc_, dld)
    sl, sc_ = src_kv("pool", 0)
    load_feat(S, "pool", k_, sl, sc_, dld)
    sl, sc_ = src_kv("pool", 1)
    load_feat(S, "pool", vT, sl, sc_, dvt)
    transpose_v(S, G, vT, v_, dvt, dld, pst, dpst)
    psS = RR([(S.ps([128, 512]), Dep()) for _ in range(3)])
    psO = RR([(S.ps([128, 512]), Dep()) for _ in range(2)])
    psM = RR([(S.ps([128, 512]), Dep()) for _ in range(2)])
    pp = RR([(S.sb([128, 512], BF16), Dep()) for _ in range(3)])
    rc = RR([(S.sb([128, 512], F32), Dep()) for _ in range(2)])
    og = RR([(S.sb([128, 512], F32), Dep()) for _ in range(2)])
    dout = Dep()
    for hh in range(2):
        ov = oT_out(hh)
        blocks = [(qb * 512, 512, list(range(NKC))) for qb in range(SEQ // 512)] + [(SEQ, 256, [64, 65])]
        for q0, nq, chunks in blocks:
            po, dpo = psO.next(); pm, dpm = psM.next()
            for i, kc in enumerate(chunks):
                p, dp = psS.next()
                S.op("pe", lambda e: e.matmul(p[:, :nq], lhsT=k_[:, kc * 128:(kc + 1) * 128], rhs=qs[hh][:, q0:q0 + nq], start=True, stop=True),
                     reads=[dld], writes=[dp])
                pb, dpb = pp.next()
                S.op("act", lambda e: e.activation(out=pb[:, :nq], in_=p[:, :nq], func=AF.Exp, scale=scale), reads=[dp], writes=[dpb])
                S.op("pe", lambda e: e.matmul(po[:, :nq], lhsT=v_[:, kc, :], rhs=pb[:, :nq], start=(i == 0), stop=(i == len(chunks) - 1)),
                     reads=[dpb, dld], writes=[dpo])
                S.op("pe", lambda e: e.matmul(pm[:, :nq], lhsT=G.onesb[:, :], rhs=pb[:, :nq], start=(i == 0), stop=(i == len(chunks) - 1)),
                     reads=[dpb, G.dident], writes=[dpm])
            r_, dr_ = rc.next(); ostg, dostg = og.next()
            S.op("dve", lambda e: e.reciprocal(out=r_[:, :nq], in_=pm[:, :nq]), reads=[dpm], writes=[dr_])
            S.op("dve", lambda e: e.tensor_tensor(out=ostg[:, :nq], in0=po[:, :nq], in1=r_[:, :nq], op=ALU.mult), reads=[dpo, dr_], writes=[dostg])
            S.dma("sp", ov[:, q0:q0 + nq], ostg[:, :nq], reads=[dostg], writes=[dout])
    return dout


SSD_ORDER = [[64, 65] + list(range(64)), [65, 64] + list(range(63, -1, -1))]


def ssd_consts():
    k = np.arange(128)
    Uf = (k[:, None] <= k[None, :]).astype(np.float32)
    Ub = (k[:, None] >= k[None, :]).astype(np.float32)
    ident = np.eye(128, dtype=np.float32)
    NEGf = np.where(k[None, :] < k[:, None], np.float32(NEG), np.float32(0)).astype(np.float32)
    NEGb = np.where(k[None, :] > k[:, None], np.float32(NEG), np.float32(0)).astype(np.float32)
    return np.ascontiguousarray(np.stack([Uf, Ub, -Uf, -Ub, ident, NEGf, NEGb], axis=1))


def emit_ssd_core(S, G, src, yT_out):
    ones, done = G.ones, G.done
    cst, dcst = G.cst, G.dident
    identb = G.identb
    neg4 = S.sb([128, 2, 4, 128], F32); dn4 = Dep()
    for d in range(2):
        S.op("dve", lambda e: e.tensor_copy(out=neg4[:, d], in_=cst[:, 5 + d, :].unsqueeze(1).to_broadcast([128, 4, 128])), reads=[dcst], writes=[dn4])
    cws = S.sb([128, 6, 5], F32); cbs = S.sb([128, 6], F32); dcw = Dep()
    S.dma("sp", cws[:], G.ssm_cw[:, :, :], writes=[dcw])
    S.dma("sp", cbs[:], G.ssm_cb[:, :], writes=[dcw])
    psSm = S.ps([128, 512]); dSm = Dep(); dCB = Dep()
    sp_src = src("sp")
    dts = S.sb([128, NKC, 16], F32); das = S.sb([128, NKC, 16], F32); vs_ = S.sb([128, 3, 16], F32)
    ddt = Dep(); dvs = Dep()
    S.dma("sp", vs_[:], G.ssm_vecs[:, :, :], writes=[dvs])
    dtTs = RR([(S.sb([16, TL], F32), Dep()) for _ in range(2)])
    for r in range(9):
        dtT, ddtT = dtTs.next()
        if r < 8:
            S.dma("sp", dtT[:, :], sp_src[r, 768:784, 0:TL], writes=[ddtT])
            c0, ncl = r * 8, 8
        else:
            S.dma("sp", dtT[:, 0:CTX].rearrange("d (r t) -> d r t", t=TC), sp_src[:, 768:784, TL:T].rearrange("r d t -> d r t"), writes=[ddtT])
            c0, ncl = 64, 2
        for i in range(ncl):
            S.op("pe", lambda e: e.transpose(out=psSm[:, i * 16:(i + 1) * 16], in_=dtT[:, i * 128:(i + 1) * 128], identity=cst[0:16, 4, 0:16]),
                 reads=[ddtT, dcst], writes=[dSm])
        S.op("dve", lambda e: e.tensor_copy(out=dts[:, c0:c0 + ncl, :], in_=psSm[:, 0:ncl * 16].rearrange("p (c j) -> p c j", j=16)),
             reads=[dSm], writes=[ddt])
    S.op("dve", lambda e: e.tensor_tensor(out=dts[:], in0=dts[:], in1=vs_[:, 0, :].unsqueeze(1).to_broadcast([128, NKC, 16]), op=ALU.add),
         reads=[dvs], writes=[ddt])
    S.op("act", lambda e: e.activation(out=dts[:], in_=dts[:], func=AF.Exp), writes=[ddt])
    S.op("act", lambda e: e.activation(out=dts[:], in_=dts[:], func=AF.Ln, bias=ones[:, 0:1]), reads=[done], writes=[ddt])
    ea = S.sb([128, 16], F32); dsum = S.sb([128, 8], F32)
    S.op("act", lambda e: e.activation(out=ea[:], in_=vs_[:, 1, :], func=AF.Exp), reads=[dvs], writes=[dvs])
    S.op("dve", lambda e: e.scalar_tensor_tensor(out=das[:], in0=dts[:], scalar=-1.0, in1=ea[:, :].unsqueeze(1).to_broadcast([128, NKC, 16]),
                                                 op0=ALU.mult, op1=ALU.mult), reads=[ddt, dvs], writes=[ddt])
    S.op("dve", lambda e: e.tensor_tensor(out=dsum[:], in0=vs_[:, 2, 0:8], in1=vs_[:, 2, 8:16], op=ALU.add), reads=[dvs], writes=[dvs])
    xtok = S.sb([128, NKC, 512], BF16, "xtok"); btok = S.sb([128, NKC, 128], BF16, "btok")
    BT = S.sb([128, NTOK], BF16, "BT"); CT = S.sb([128, NTOK], BF16, "CT")
    dxt = [Dep() for _ in range(NKC)]
    PL = 512
    pieces = [(a, PL) for a in range(0, SEQ, PL)] + [(SEQ, CTX)]
    xin = RR([(S.sb([128, PL + 4], F32), Dep()) for _ in range(2)])
    acc = RR([(S.sb([128, PL], F32), Dep()) for _ in range(2)])
    cob = RR([(S.sb([128, PL], BF16), Dep()) for _ in range(2)])
    pst = S.ps([128, 4, 128], BF16); dpst = Dep()
    qi = 0
    for cch in range(6):
        for (a, L) in pieces:
            xi, dxi = xin.next()
            eng = "sp" if qi % 2 == 0 else "act"
            qi += 1
            sv = src(eng)[:, cch * 128:(cch + 1) * 128, :]
            if a >= SEQ:
                S.op("pool", lambda e: e.memset(xi[:, 0:2], 0.0), writes=[dxi])
                S.op("pool", lambda e: e.memset(xi[:, L + 2:L + 4], 0.0), writes=[dxi])
                S.dma(eng, xi[:, 2:2 + L].rearrange("p (r t) -> p r t", t=TC), sv[:, :, TL:T].rearrange("r p t -> p r t"), writes=[dxi])
            else:
                r, o = a // TL, a % TL
                lo = o - 2 if o >= 2 else o
                hi = o + L + 2 if o + L + 2 <= TL else o + L
                S.dma(eng, xi[:, 2 + (lo - o):2 + (hi - o)], sv[r, :, lo:hi], writes=[dxi])
                if lo == o:
                    if a == 0:
                        S.op("pool", lambda e: e.memset(xi[:, 0:2], 0.0), writes=[dxi])
                    else:
                        S.dma(eng, xi[:, 0:2], sv[r - 1, :, TL - 2:TL], writes=[dxi])
                if hi == o + L:
                    if a + L == SEQ:
                        S.op("pool", lambda e: e.memset(xi[:, L + 2:L + 4], 0.0), writes=[dxi])
                    else:
                        S.dma(eng, xi[:, L + 2:L + 4], sv[r + 1, :, 0:2], writes=[dxi])
            ac, dac_ = acc.next()
            S.op("dve", lambda e: e.tensor_scalar(out=ac[:, :L], in0=xi[:, 2:2 + L], scalar1=cws[:, cch, 2:3], scalar2=None, op0=ALU.mult),
                 reads=[dxi, dcw], writes=[dac_])
            for k in (0, 1, 3, 4):
                S.op("dve", lambda e: e.scalar_tensor_tensor(out=ac[:, :L], in0=xi[:, k:k + L], scalar=cws[:, cch, k:k + 1], in1=ac[:, :L],
                                                             op0=ALU.mult, op1=ALU.add), reads=[dxi, dcw], writes=[dac_])
            tcs = list(range(a // 128, (a + L) // 128))
            if cch < 5:
                co, dco = cob.next()
                S.op("act", lambda e: e.activation(out=co[:, :L], in_=ac[:, :L], func=AF.Silu, bias=cbs[:, cch:cch + 1]), reads=[dac_, dcw], writes=[dco])
                if cch == 4:
                    S.op("pool", lambda e: e.tensor_copy(out=BT[:, a:a + L], in_=co[:, :L]), reads=[dco], writes=[dxt[t] for t in tcs])
                for g0 in range(0, len(tcs), 4):
                    grp = tcs[g0:g0 + 4]
                    for i, tc_ in enumerate(grp):
                        S.op("pe", lambda e: e.transpose(out=pst[:, i, :], in_=co[:, (g0 + i) * 128:(g0 + i + 1) * 128], identity=identb[:, :]),
                             reads=[dco, dcst], writes=[dpst])
                    dst = xtok[:, grp[0]:grp[0] + len(grp), cch * 128:(cch + 1) * 128] if cch < 4 else btok[:, grp[0]:grp[0] + len(grp), :]
                    if (g0 // 4) % 2 == 0:
                        S.op("act", lambda e: e.copy(out=dst, in_=pst[:, 0:len(grp), :]), reads=[dpst], writes=[dxt[t] for t in grp])
                    else:
                        S.op("dve", lambda e: e.tensor_copy(out=dst, in_=pst[:, 0:len(grp), :]), reads=[dpst], writes=[dxt[t] for t in grp])
            else:
                S.op("act", lambda e: e.activation(out=CT[:, a:a + L], in_=ac[:, :L], func=AF.Silu, bias=cbs[:, cch:cch + 1]),
                     reads=[dac_, dcw], writes=[dxt[t] for t in tcs])
    Sst = [S.sb([128, 8, 64], F32) for _ in range(2)]
    Sbf = [S.sb([128, 512], BF16) for _ in range(2)]
    dS = [Dep(), Dep()]; dSbf = [Dep(), Dep()]
    for d in range(2):
        S.op("pool", lambda e: e.memset(Sst[d][:], 0.0), writes=[dS[d]])
        S.op("pool", lambda e: e.memset(Sbf[d][:], 0.0), writes=[dSbf[d]])
    rhsA = RR([(S.sb([128, 8, 128], F32), Dep()) for _ in range(2)])
    rhsB = RR([(S.sb([128, 8, 128], F32), Dep()) for _ in range(1)])
    expD = RR([(S.sb([128, 4, 128], F32), Dep()) for _ in range(4)])
    ecs_ = RR([(S.sb([128, 16], F32), Dep()) for _ in range(2)])
    MT = RR([(S.sb([128, 8, 128], BF16), Dep()) for _ in range(2)])
    xdt = RR([(S.sb([128, 8, 64], BF16), Dep()) for _ in range(2)])
    xw = RR([(S.sb([128, 8, 64], BF16), Dep()) for _ in range(2)])
    ysb = RR([(S.sb([128, 8, 64], F32), Dep()) for _ in range(2)])
    tb_ = RR([(S.sb([128, 8, 64], F32), Dep()) for _ in range(2)])
    ub_ = RR([(S.sb([128, 8, 64], F32), Dep()) for _ in range(1)])
    yts = RR([(S.sb([128, 4, 128], F32), Dep()) for _ in range(2)])
    pv = RR([(S.sb([128, 4, 128], F32), Dep()) for _ in range(2)])
    psD = RR([(S.ps([128, 4, 128]), Dep()) for _ in range(2)])
    psY = S.ps([128, 8, 64]); dY = Dep()
    psYo = S.ps([128, 8, 64]); dYo = Dep()
    psSp = S.ps([128, 8, 64]); dSp = Dep()
    psT = S.ps([128, 4, 128]); dT = Dep()
    dyg = [Dep() for _ in range(NKC)]
    yov = yT_out.rearrange("(cc p) t -> p cc t", p=128)
    visited = set()
    for step in range(NKC):
        for d in range(2):
            c = SSD_ORDER[d][step]
            j0 = d * 8
            U = cst[:, d, :]; negU = cst[:, 2 + d, :]; idn = cst[:, 4, :]
            col = 127 if d == 0 else 0
            dac = das[:, c, j0:j0 + 8]
            ra, dra = rhsA.next(); rb, drb = rhsB.next()
            S.op("dve", lambda e: e.tensor_tensor(out=ra[:], in0=U.unsqueeze(1).to_broadcast([128, 8, 128]),
                                                  in1=dac.unsqueeze(2).to_broadcast([128, 8, 128]), op=ALU.mult), reads=[dcst, ddt], writes=[dra])
            S.op("pool", lambda e: e.tensor_copy(out=rb[:], in_=dac.unsqueeze(2).to_broadcast([128, 8, 128])), reads=[ddt], writes=[drb])
            S.op("pe", lambda e: e.matmul(psSm[:, 0:8], lhsT=U, rhs=dac, start=True, stop=True), reads=[dcst, ddt], writes=[dSm])
            S.op("pe", lambda e: e.matmul(psSm[:, 8:16], lhsT=ones[:, :], rhs=dac, start=True, stop=True), reads=[done, ddt], writes=[dSm])
            ec, dec = ecs_.next()
            S.op("act", lambda e: e.activation(out=ec[:], in_=psSm[:, 0:16], func=AF.Exp), reads=[dSm], writes=[dec])
            eds = []
            for half in range(2):
                h4 = half * 4
                pD, dpD = psD.next()
                S.op("pe", lambda e: e.matmul(pD[:], lhsT=ones[:, :], rhs=ra[:, h4:h4 + 4, :], start=True, stop=False), reads=[dra, done], writes=[dpD])
                S.op("pe", lambda e: e.matmul(pD[:], lhsT=negU, rhs=rb[:, h4:h4 + 4, :], start=False, stop=False), reads=[drb, dcst], writes=[dpD])
                S.op("pe", lambda e: e.matmul(pD[:], lhsT=idn, rhs=neg4[:, d], start=False, stop=True), reads=[dcst, dn4], writes=[dpD])
                ed, ded = expD.next()
                S.op("act", lambda e: e.activation(out=ed[:], in_=pD[:], func=AF.Exp), reads=[dpD], writes=[ded])
                eds.append((ed, ded))
            S.op("pe", lambda e: e.matmul(psSm[:, 128:256], lhsT=BT[:, c * 128:(c + 1) * 128], rhs=CT[:, c * 128:(c + 1) * 128], start=True, stop=True),
                 reads=[dxt[c]], writes=[dCB])
            mt, dmt = MT.next()
            for half in range(2):
                h4 = half * 4
                ed, ded = eds[half]
                S.op("dve", lambda e: e.tensor_tensor(out=mt[:, h4:h4 + 4, :], in0=ed[:], in1=psSm[:, 128:256].unsqueeze(1).to_broadcast([128, 4, 128]),
                                                      op=ALU.mult), reads=[ded, dCB], writes=[dmt])
            xd, dxd = xdt.next(); xw_, dxw = xw.next()
            xt3 = xtok[:, c, :].rearrange("p (h q) -> p h q", h=8)
            S.op("pool", lambda e: e.tensor_tensor(out=xd[:], in0=xt3, in1=dts[:, c, j0:j0 + 8].unsqueeze(2).to_broadcast([128, 8, 64]), op=ALU.mult),
                 reads=[dxt[c], ddt], writes=[dxd])
            for half in range(2):
                h4 = half * 4
                ed, ded = eds[half]
                S.op("pool", lambda e: e.tensor_tensor(out=xw_[:, h4:h4 + 4, :], in0=xd[:, h4:h4 + 4, :],
                                                       in1=ed[:, :, col:col + 1].to_broadcast([128, 4, 64]), op=ALU.mult), reads=[dxd, ded], writes=[dxw])
            for hh in range(8):
                S.op("pe", lambda e: e.matmul(psY[:, hh, :], lhsT=mt[:, hh, :], rhs=xd[:, hh, :], start=True, stop=True), reads=[dmt, dxd], writes=[dY])
            S.op("pe", lambda e: e.matmul(psYo[:], lhsT=CT[:, c * 128:(c + 1) * 128], rhs=Sbf[d][:], start=True, stop=True),
                 reads=[dxt[c], dSbf[d]], writes=[dYo])
            S.op("pe", lambda e: e.matmul(psSp[:], lhsT=btok[:, c, :], rhs=xw_[:], start=True, stop=True), reads=[dxt[c], dxw], writes=[dSp])
            ys, dys = ysb.next(); t, dt_ = tb_.next()
            S.op("act", lambda e: e.copy(out=ys[:], in_=psY[:]), reads=[dY], writes=[dys])
            S.op("dve", lambda e: e.tensor_tensor(out=t[:], in0=psYo[:], in1=ec[:, 0:8].unsqueeze(2).to_broadcast([128, 8, 64]), op=ALU.mult),
                 reads=[dYo, dec], writes=[dt_])
            S.op("pool", lambda e: e.tensor_tensor(out=t[:], in0=t[:], in1=ys[:], op=ALU.add), reads=[dys], writes=[dt_])
            if d == 0:
                u, du = ub_.next()
                S.op("pool", lambda e: e.tensor_tensor(out=u[:], in0=xt3, in1=dsum[:, :].unsqueeze(2).to_broadcast([128, 8, 64]), op=ALU.mult),
                     reads=[dxt[c], dvs], writes=[du])
                S.op("pool", lambda e: e.tensor_tensor(out=t[:], in0=t[:], in1=u[:], op=ALU.add), reads=[du], writes=[dt_])
            t2 = t[:].rearrange("p h q -> p (h q)")
            for cc in range(4):
                S.op("pe", lambda e: e.transpose(out=psT[:, cc, :], in_=t2[:, cc * 128:(cc + 1) * 128], identity=idn), reads=[dt_, dcst], writes=[dT])
            yt, dyt = yts.next()
            S.op("act", lambda e: e.copy(out=yt[:], in_=psT[:]), reads=[dT], writes=[dyt])
            if c in visited:
                pvb, dpv = pv.next()
                S.dma("sp", pvb[:], yov[:, :, c * 128:(c + 1) * 128], reads=[dyg[c]], writes=[dpv])
                S.op("pool", lambda e: e.tensor_tensor(out=yt[:], in0=yt[:], in1=pvb[:], op=ALU.add), reads=[dpv], writes=[dyt])
            visited.add(c)
            S.dma("sp", yov[:, :, c * 128:(c + 1) * 128], yt[:], reads=[dyt], writes=[dyg[c]])
            S.op("dve", lambda e: e.tensor_tensor(out=Sst[d][:], in0=Sst[d][:], in1=ec[:, 8:16].unsqueeze(2).to_broadcast([128, 8, 64]), op=ALU.mult),
                 reads=[dec], writes=[dS[d]])
            S.op("dve", lambda e: e.tensor_tensor(out=Sst[d][:], in0=Sst[d][:], in1=psSp[:], op=ALU.add), reads=[dSp], writes=[dS[d]])
            S.op("act", lambda e: e.copy(out=Sbf[d][:], in_=Sst[d][:].rearrange("p h q -> p (h q)")), reads=[dS[d]], writes=[dSbf[d]])
    return dyg


def build_fused(n_layers=4):
    nc = new_nc()
    S = Sched(nc)
    G = G_()
    npat = len(na_patterns())

    def ext(name, shape, dt=F32):
        return nc.dram_tensor(name, list(shape), dt, kind="ExternalInput").ap()

    def internal(name, shape, dt=F32, shared=False):
        if shared:
            return nc.dram_tensor(name, list(shape), dt, addr_space="Shared").ap()
        return nc.dram_tensor(name, list(shape), dt).ap()

    xT = ext("xT", [D, T]); cid = ext("cid", [1, 2], I32)
    G.cc = ext("cc", [128, 16, 2]); G.aw = ext("aw", [4 * D, MODC]); G.abT = ext("abT", [128, 72])
    G.ng_all = ext("ng_all", [128, 13, 16])
    n_na = (n_layers + 2) // 3
    ffn_w_in = ext("ffn_w_in", [2 * n_layers * D, 2 * DFF]); ffn_w_out = ext("ffn_w_out", [2 * n_layers * DFF, D])
    na_w_qkv = ext("na_w_qkv", [n_na * D, 6144]); na_w_o = ext("na_w_o", [n_na * D, D])
    ssm_w_in = ext("ssm_w_in", [D, 10368] if n_layers > 1 else [128, 128]); ssm_w_out = ext("ssm_w_out", [4096, D] if n_layers > 1 else [128, 128])
    gqa_w_qkv = ext("gqa_w_qkv", [D, 3072] if n_layers > 2 else [128, 128]); gqa_w_o = ext("gqa_w_o", [D, D] if n_layers > 2 else [128, 128])
    na_bm = ext("na_bm", [2 * 2 * npat * 128, 448])
    G.ssm_cw = ext("ssm_cw", [128, 6, 5]); G.ssm_cb = ext("ssm_cb", [128, 6]); G.ssm_vecs = ext("ssm_vecs", [128, 3, 16])
    G.ssm_ng = ext("ssm_ng", [128, 32])
    consts = ext("consts", [128, 7, 128])
    G.qkn = ext("qkn", [128, 2]); G.cosF = ext("cosF", [128, TL]); G.sinF = ext("sinF", [128, TL]); G.rotm = ext("rotm", [128, 128])
    outT = nc.dram_tensor("outT", [D, T], F32, kind="ExternalOutput").ap()

    hbuf = [internal("hA", [D, T]), internal("hB", [D, T])]
    G.modT_local = internal("modT_local", [128, 144]); G.modT_all = internal("modT_all", [NCORES * 128, 144], shared=True)
    qkv_send = internal("qkv_send", [6144, T], BF16); qkv_all = internal("qkv_all", [NCORES * 6144, T], BF16, shared=True)
    o_send = internal("o_send", [OPAD, NTOK]); o_all = internal("o_all", [NCORES * OPAD, NTOK], shared=True)
    ssm_send = internal("ssm_send", [6272, T]); ssm_all = internal("ssm_all", [NCORES * 6272, T], shared=True)
    zT = internal("zT", [4096, T])
    y_send = internal("y_send", [512, NTOK]); y_all = internal("y_all", [NCORES * 512, NTOK], shared=True)
    gqa_send = internal("gqa_send", [3072, T]); gqa_all = internal("gqa_all", [NCORES * 3072, T], shared=True)

    G.ones, G.done = make_consts(S)
    G.cst = S.sb([128, 7, 128], F32); G.dident = Dep()
    S.dma("sp", G.cst[:], consts[:, :, :], writes=[G.dident])
    G.identb = S.sb([128, 128], BF16); G.onesb = S.sb([128, 128], BF16)
    S.op("dve", lambda e: e.tensor_copy(out=G.identb[:], in_=G.cst[:, 4, :]), reads=[G.dident], writes=[G.dident])
    S.op("dve", lambda e: e.tensor_copy(out=G.onesb[:], in_=G.ones[:, :]), reads=[G.done], writes=[G.dident])
    jreg, kvreg = {}, {}
    for e in ("sp", "act", "pool"):
        E = S.engs[e]
        r1 = E.alloc_register("cid_" + e); r2 = E.alloc_register("kv_" + e)
        E.reg_load(r1, cid[0:1, 0:1]); E.reg_load(r2, cid[0:1, 1:2])
        jreg[e] = E.snap(r1, min_val=0, max_val=NCORES - 1)
        kvreg[e] = E.snap(r2, min_val=0, max_val=3)

    qkv_mine = internal("qkv_mine", [NCORES * 768, T], BF16)
    o_mine = internal("o_mine", [D, T])
    ssm_mine = internal("ssm_mine", [NCORES * 784, T])
    y_mine = internal("y_mine", [4096, T])
    q_mine = internal("q_mine", [NCORES * 256, T])
    kv_mine = internal("kv_mine", [NCORES * 256, T])
    J = jreg["sp"]

    def pull_tokens(dst, src_all, nrow_groups, h_all, h_use, reads, dep):
        sv = src_all.rearrange("(r h p) t -> r h p t", r=nrow_groups, h=h_all, p=128)[:, 0:h_use]
        dv = dst.rearrange("(r h p) t -> r h p t", r=nrow_groups, h=h_use, p=128)
        S.dma("sp", dv[:, :, :, 0:TL], sv[:, :, :, 0:SEQ].rearrange("r h p (j t) -> j r h p t", t=TL)[J], reads=reads, writes=[dep])
        S.dma("sp", dv[:, :, :, TL:T], sv[:, :, :, SEQ:NTOK].rearrange("r h p (j t) -> j r h p t", t=TC)[J], reads=reads, writes=[dep])

    def o_src(eng):
        v = o_mine.rearrange("(c p) t -> p c t", p=128)
        return [(0, 16, v[:, :, 0:TL], v[:, :, TL:T])]

    def y_src(eng):
        v = y_mine.rearrange("(c p) t -> p c t", p=128)
        return [(0, 32, v[:, :, 0:TL], v[:, :, TL:T])]

    emit_mod(S, G)
    cur = xT
    hi = 0

    def nxt_buf():
        nonlocal hi
        b = hbuf[hi]
        hi ^= 1
        return b

    for i in range(n_layers):
        last = (i == n_layers - 1)
        nb = nxt_buf()
        emit_ffn(S, G, cur, nb, ffn_w_in[(2 * i) * D:(2 * i + 1) * D, :], ffn_w_out[(2 * i) * DFF:(2 * i + 1) * DFF, :], i, 0)
        cur = nb
        kind, jj = i % 3, i // 3
        dsend = Dep(); dall = Dep()
        if kind == 0:
            def oc_na(oc, st, dst):
                which, head = oc // 16, oc % 16
                row = (head // 2) * 768 + (which * 2 + head % 2) * 128
                S.dma("sp", qkv_send[row:row + 128, :], st[:], reads=[dst], writes=[dsend])
            emit_proj(S, G, cur, na_w_qkv[jj * D:(jj + 1) * D, :], 6144, "na", i, oc_na, st_dt=BF16)
            S.cc("AllGather", [qkv_send], [qkv_all], reads=[dsend], writes=[dall])
            S.dma("sp", qkv_mine.rearrange("(r x) t -> r x t", x=768),
                  qkv_all.rearrange("(r j x) t -> j r x t", r=NCORES, j=NCORES, x=768)[J], reads=[dall], writes=[Dep()])
            S.phase_end()
            S.phase_begin()

            def na_src(eng, hh, w):
                vv = qkv_mine.rearrange("(r w d) t -> r w d t", r=NCORES, w=6, d=128)[:, w * 2 + hh]
                return vv[:, :, 0:TL], vv[:, :, TL:T]
            bm0 = (jj * 2) * npat * 128
            dout = emit_na_core(S, G, na_src, lambda hh: na_bm[bm0 + hh * npat * 128: bm0 + (hh + 1) * npat * 128, :],
                                lambda hh: o_send[hh * 128:(hh + 1) * 128, :])
            S.cc("AllGather", [o_send], [o_all], reads=[dout], writes=[dall])
            pull_tokens(o_mine, o_all, NCORES, 3, 2, [dall], Dep())
            S.phase_end()
            nb = nxt_buf()
            emit_oproj(S, G, o_src, na_w_o[jj * D:(jj + 1) * D, :], cur, nb, i, 16, False)
            cur = nb
        elif kind == 1:
            def oc_ssm(oc, st, dst):
                if oc < 32:
                    S.dma("sp", zT[oc * 128:(oc + 1) * 128, :], st[:], reads=[dst], writes=[dsend])
                elif oc < 64:
                    xch = oc - 32
                    row = (xch // 4) * 784 + (xch % 4) * 128
                    S.dma("sp", ssm_send[row:row + 128, :], st[:], reads=[dst], writes=[dsend])
                elif oc < 80:
                    j = (oc - 64) % 8
                    row = j * 784 + (512 if oc < 72 else 640)
                    S.dma("sp", ssm_send[row:row + 128, :], st[:], reads=[dst], writes=[dsend])
                else:
                    for j in range(NCORES):
                        for d in range(2):
                            row = j * 784 + 768 + d * 8
                            S.dma("sp", ssm_send[row:row + 8, :], st[d * 64 + 8 * j:d * 64 + 8 * j + 8, :], reads=[dst], writes=[dsend])
            emit_proj(S, G, cur, ssm_w_in, 10368, "ssm", i, oc_ssm)
            S.cc("AllGather", [ssm_send], [ssm_all], reads=[dsend], writes=[dall])
            S.dma("sp", ssm_mine.rearrange("(r x) t -> r x t", x=784),
                  ssm_all.rearrange("(r j x) t -> j r x t", r=NCORES, j=NCORES, x=784)[J], reads=[dall], writes=[Dep()])
            S.phase_end()
            S.phase_begin()
            dyg = emit_ssd_core(S, G, lambda eng: ssm_mine.rearrange("(r x) t -> r x t", x=784), y_send)
            S.cc("AllGather", [y_send], [y_all], reads=dyg, writes=[dall])
            pull_tokens(y_mine, y_all, NCORES * 4, 1, 1, [dall], Dep())
            S.phase_end()
            nb = nxt_buf()
            emit_oproj(S, G, y_src, ssm_w_out, cur, nb, i, 32, True, zT=zT)
            cur = nb
        else:
            def oc_gqa(oc, st, dst):
                S.dma("sp", gqa_send[oc * 128:(oc + 1) * 128, :], st[:], reads=[dst], writes=[dsend])
            emit_proj(S, G, cur, gqa_w_qkv, 3072, "gqa", i, oc_gqa)
            S.cc("AllGather", [gqa_send], [gqa_all], reads=[dsend], writes=[dall])
            g3 = gqa_all.rearrange("(r x) t -> r x t", x=3072)
            S.dma("sp", q_mine.rearrange("(r x) t -> r x t", x=256),
                  g3[:, 0:2048, :].rearrange("r (j x) t -> j r x t", j=NCORES)[J], reads=[dall], writes=[Dep()])
            KV = kvreg["sp"]
            for w_ in range(2):
                S.dma("sp", kv_mine.rearrange("(r w d) t -> r w d t", w=2, d=128)[:, w_],
                      g3[:, 2048 + w_ * 512:2048 + (w_ + 1) * 512, :].rearrange("r (k d) t -> k r d t", k=4)[KV], reads=[dall], writes=[Dep()])
            S.phase_end()
            S.phase_begin()

            def src_q(eng, hh):
                v = q_mine.rearrange("(r h d) t -> r h d t", h=2, d=128)[:, hh]
                return v[:, :, 0:TL], v[:, :, TL:T]

            def src_kv(eng, w):
                v = kv_mine.rearrange("(r w d) t -> r w d t", w=2, d=128)[:, w]
                return v[:, :, 0:TL], v[:, :, TL:T]
            dout = emit_gqa_core(S, G, src_q, src_kv, lambda hh: o_send[hh * 128:(hh + 1) * 128, :])
            S.cc("AllGather", [o_send], [o_all], reads=[dout], writes=[dall])
            pull_tokens(o_mine, o_all, NCORES, 3, 2, [dall], Dep())
            S.phase_end()
            nb = nxt_buf()
            emit_oproj(S, G, o_src, gqa_w_o, cur, nb, i, 16, False)
            cur = nb
        nb = outT if last else nxt_buf()
        emit_ffn(S, G, cur, nb, ffn_w_in[(2 * i + 1) * D:(2 * i + 2) * D, :], ffn_w_out[(2 * i + 1) * DFF:(2 * i + 2) * DFF, :], i, 1,
                 final_norm=(i == 3))
        cur = nb
    S.finish()
    print("fused program: inst", S.n_inst, "waits", S.n_wait)
    return nc


def rope_tables():
    t = np.arange(SEQ)
    row = (t // 64).astype(np.float32)
    col = (t % 64).astype(np.float32)
    inv = (np.float32(10000.0) ** (-np.arange(32, dtype=np.float32) / np.float32(32))).astype(np.float32)
    ang = np.concatenate([row[:, None] * inv, col[:, None] * inv], axis=-1).astype(np.float32)
    cos = np.cos(ang).astype(np.float32); sin = np.sin(ang).astype(np.float32)
    cosF = np.ascontiguousarray(np.concatenate([cos, cos], axis=1).T)
    sinF = np.ascontiguousarray(np.concatenate([sin, sin], axis=1).T)
    return cosF, sinF


def rot_matrix():
    m = np.zeros((128, 128), np.float32)
    for i in range(64):
        m[i + 64, i] = -1.0
        m[i, i + 64] = 1.0
    return m


def fm(v):
    v = np.asarray(v, np.float32).reshape(-1, 128)
    return np.ascontiguousarray(v.T)


def to_cores(g):
    return [np.ascontiguousarray(np.concatenate([g[:, j * TL:(j + 1) * TL], g[:, SEQ + j * TC:SEQ + (j + 1) * TC]], axis=1))
            for j in range(NCORES)]


_NC = {}
N_LAYERS = 4


def make_inputs(x, c, ctx, c_ctx, ada_w, ada_b, norm_g, ffn_w_in, ffn_w_out, na_w_qkv, na_rpb, na_w_o,
                ssm_w_in, ssm_conv_w, ssm_conv_b, ssm_a_log, ssm_dt_bias, ssm_d, ssm_norm_g, ssm_w_out,
                gqa_w_qkv, gqa_q_norm, gqa_k_norm, gqa_w_o, final_norm_g):
    f32 = np.float32
    A = lambda v: np.asarray(v, f32)
    x = A(x); ctx = A(ctx); ada_w = A(ada_w); ada_b = A(ada_b); norm_g = A(norm_g)
    xg = np.concatenate([x[0].T, ctx[0].T], axis=1)
    xT = to_cores(xg)
    cc = np.ascontiguousarray(np.stack([A(c).reshape(D), A(c_ctx).reshape(D)]).reshape(2, 16, 128).transpose(2, 1, 0))
    ng_all = np.ascontiguousarray(np.stack([fm(norm_g[i, k]) for i in range(4) for k in range(3)] + [fm(A(final_norm_g))], axis=1))
    shared = {
        "cc": cc, "ng_all": ng_all,
        "ffn_w_in": A(ffn_w_in).reshape(8 * D, 2 * DFF)[:2 * N_LAYERS * D], "ffn_w_out": A(ffn_w_out).reshape(8 * DFF, D)[:2 * N_LAYERS * DFF],
        "na_w_qkv": A(na_w_qkv).reshape(2 * D, 6144)[:((N_LAYERS + 2) // 3) * D], "na_w_o": A(na_w_o).reshape(2 * D, D)[:((N_LAYERS + 2) // 3) * D],
        "ssm_w_in": A(ssm_w_in).reshape(D, 10368) if N_LAYERS > 1 else np.zeros((128, 128), f32),
        "ssm_w_out": A(ssm_w_out).reshape(4096, D) if N_LAYERS > 1 else np.zeros((128, 128), f32),
        "gqa_w_qkv": A(gqa_w_qkv).reshape(D, 3072) if N_LAYERS > 2 else np.zeros((128, 128), f32),
        "gqa_w_o": A(gqa_w_o).reshape(D, D) if N_LAYERS > 2 else np.zeros((128, 128), f32),
        "ssm_ng": fm(A(ssm_norm_g).reshape(4096)), "consts": ssd_consts(),
        "qkn": np.ascontiguousarray(np.stack([A(gqa_q_norm).reshape(128), A(gqa_k_norm).reshape(128)], axis=1)),
        "rotm": rot_matrix(),
    }
    cosF, sinF = rope_tables()
    bmt = [na_bias_tables(A(na_rpb)[jj]) for jj in range(2)]
    npat = bmt[0].shape[1]
    cwf = A(ssm_conv_w)[0]; cbf = A(ssm_conv_b)[0]
    ins = []
    for j in range(NCORES):
        sl = slice(j * MODC, (j + 1) * MODC)
        chans = np.concatenate([np.arange(512 * j, 512 * j + 512), 4096 + np.arange(128 * j, 128 * j + 128),
                                5120 + np.arange(128 * j, 128 * j + 128)])
        hd = np.concatenate([np.arange(8 * j, 8 * j + 8), 64 + np.arange(8 * j, 8 * j + 8)])
        vecs = np.stack([A(ssm_dt_bias)[0].reshape(128)[hd], A(ssm_a_log)[0].reshape(128)[hd], A(ssm_d)[0].reshape(128)[hd]])
        d = dict(shared)
        d.update({
            "xT": xT[j], "cid": np.array([[j, j // 2]], np.int32),
            "aw": np.ascontiguousarray(ada_w[:, :, sl]).reshape(4 * D, MODC),
            "abT": np.ascontiguousarray(ada_b[:, sl].reshape(4, 18, 128).transpose(2, 0, 1)).reshape(128, 72),
            "na_bm": np.ascontiguousarray(np.concatenate([bmt[jj][2 * j:2 * j + 2] for jj in range(2)], axis=0)).reshape(4 * npat * 128, 448),
            "ssm_cw": np.ascontiguousarray(cwf[:, chans].T.reshape(6, 128, 5).transpose(1, 0, 2)),
            "ssm_cb": np.ascontiguousarray(cbf[chans].reshape(6, 128).T),
            "ssm_vecs": np.ascontiguousarray(np.broadcast_to(vecs[None], (128, 3, 16))),
            "cosF": np.ascontiguousarray(cosF[:, j * TL:(j + 1) * TL]), "sinF": np.ascontiguousarray(sinF[:, j * TL:(j + 1) * TL]),
        })
        ins.append(d)
    return ins


def kernel(**inputs):
    ins = make_inputs(**inputs)
    if N_LAYERS not in _NC:
        _NC[N_LAYERS] = build_fused(N_LAYERS)
    res = run_bass_kernel_spmd(_NC[N_LAYERS], ins, core_ids=list(range(NCORES)))
    out = np.concatenate([res.results[j]["outT"][:, :TL].T for j in range(NCORES)], axis=0)
    return np.ascontiguousarray(out[None]).astype(np.float32)
```

```python
from contextlib import ExitStack
import numpy as np
import concourse.bass as bass
import concourse.mybir as mybir
from concourse.alu_op_type import AluOpType as ALU
from concourse.bass_utils import run_bass_kernel_spmd

F32 = mybir.dt.float32
BF16 = mybir.dt.bfloat16
AF = mybir.ActivationFunctionType

NCORES = 8
D = 2048
DFF = 5632
SEQ = 8192
CTX = 256
TL = SEQ // NCORES
TC = CTX // NCORES
T = TL + TC
EPS = 1e-6
TILES = [(0, 512), (512, 512), (1024, TC)]
NEG = -30000.0


class Dep:
    __slots__ = ("w", "r")

    def __init__(self):
        self.w = None
        self.r = {}


class Sched:
    def __init__(self, nc, dma_ring=6):
        self.nc = nc
        self.engs = {"pe": nc.tensor, "act": nc.scalar, "dve": nc.vector,
                     "pool": nc.gpsimd, "sp": nc.sync}
        self.sems = []
        self.eng_sem = {}
        self.eng_cnt = {}
        self.seen = {e: {} for e in self.engs}
        self.ring = {}
        self.ring_pos = {}
        self.sem_val = {}
        for e in self.engs:
            self.eng_sem[e] = self._new_sem("c_" + e)
            self.eng_cnt[e] = 0
        for e in ("sp", "act", "pool"):
            self.ring[e] = [self._new_sem(f"d_{e}{i}") for i in range(dma_ring)]
            self.ring_pos[e] = 0
        self.n_wait = 0
        self.n_inst = 0
        self.stack = ExitStack()
        self._tn = 0

    def sb(self, shape, dt, name=None):
        self._tn += 1
        return self.stack.enter_context(self.nc.sbuf_tensor(f"{name or 'sb'}_{self._tn}", list(shape), dt))

    def ps(self, shape, dt=F32, name=None):
        self._tn += 1
        return self.stack.enter_context(self.nc.psum_tensor(f"{name or 'ps'}_{self._tn}", list(shape), dt))

    def _new_sem(self, name):
        h = self.nc.alloc_semaphore(name=name)
        self.sems.append(h)
        self.sem_val[len(self.sems) - 1] = 0
        return len(self.sems) - 1

    def _need(self, reads, writes):
        need = {}
        for d in reads:
            if d.w is not None:
                s, v = d.w
                if need.get(s, 0) < v:
                    need[s] = v
        for d in writes:
            if d.w is not None:
                s, v = d.w
                if need.get(s, 0) < v:
                    need[s] = v
            for s, v in d.r.items():
                if need.get(s, 0) < v:
                    need[s] = v
        return need

    def _emit_waits(self, eng, need):
        E = self.engs[eng]
        seen = self.seen[eng]
        own = self.eng_sem[eng]
        for s, v in need.items():
            if eng == "pe" and s == own:
                continue
            if seen.get(s, 0) < v:
                E.wait_ge(self.sems[s], v)
                seen[s] = v
                self.n_wait += 1

    def _record(self, ev, reads, writes):
        s, v = ev
        for d in reads:
            d.r[s] = v
        for d in writes:
            d.w = ev
            d.r = {}

    def op(self, eng, fn, reads=(), writes=()):
        need = self._need(reads, writes)
        self._emit_waits(eng, need)
        inst = fn(self.engs[eng])
        s = self.eng_sem[eng]
        self.eng_cnt[eng] += 1
        v = self.eng_cnt[eng]
        inst.then_inc(self.sems[s], 1)
        self._record((s, v), reads, writes)
        self.n_inst += 1
        return (s, v)

    def dma(self, eng, out, in_, reads=(), writes=(), **kw):
        need = self._need(reads, writes)
        ring = self.ring[eng]
        s = ring[self.ring_pos[eng] % len(ring)]
        self.ring_pos[eng] += 1
        prev = self.sem_val[s]
        if prev > 0:
            need[s] = max(need.get(s, 0), prev)
        self._emit_waits(eng, need)
        inst = self.engs[eng].dma_start(out=out, in_=in_, **kw)
        inst.then_inc(self.sems[s], 16)
        self.sem_val[s] = prev + 16
        ev = (s, prev + 16)
        self._record(ev, reads, writes)
        self.n_inst += 1
        return ev

    def cc(self, kind, ins, outs, reads=(), writes=()):
        need = self._need(reads, writes)
        if not hasattr(self, "cc_sem"):
            self.cc_sem = self._new_sem("cc")
        s = self.cc_sem
        prev = self.sem_val[s]
        if prev > 0:
            need[s] = max(need.get(s, 0), prev)
        self._emit_waits("pool", need)
        inst = self.engs["pool"].collective_compute(kind, ALU.bypass, replica_groups=[list(range(NCORES))], ins=ins, outs=outs)
        inst.then_inc(self.sems[s])
        self.sem_val[s] = prev + 1
        ev = (s, prev + 1)
        self._record(ev, reads, writes)
        self.n_inst += 1
        return ev

    def barrier(self):
        need = {s: v for s, v in self.sem_val.items() if v > 0}
        for e, s in self.eng_sem.items():
            if self.eng_cnt[e] > 0:
                need[s] = self.eng_cnt[e]
        for e in self.engs:
            self._emit_waits(e, dict(need))

    def phase_begin(self):
        self.gstack = getattr(self, "gstack", None) or self.stack
        self.stack = ExitStack()

    def phase_end(self):
        self.barrier()
        self.stack.close()
        self.stack = self.gstack

    def finish(self, out_deps=()):
        self.barrier()
        self.stack.close()


def new_nc():
    return bass.Bass("TRN2", target_bir_lowering=False)


class RR:
    def __init__(self, items):
        self.items = items
        self.i = 0

    def next(self):
        it = self.items[self.i % len(self.items)]
        self.i += 1
        return it


def emit_norm_mod(S, h, dh, xn, dxn, gm_l, sh_l, gm_c, sh_c, dmv, ones, done, ps_stat, dps_stat,
                  nch=16, tiles=TILES, eng_sq="pool"):
    sq = [(S.sb([128, 512], F32), Dep()) for _ in range(2)]
    sqr = RR(sq)
    rt = S.sb([128, 512], F32)
    drt = Dep()
    tmp = RR([(S.sb([128, 512], F32), Dep()) for _ in range(2)])
    for ti, (t0, n) in enumerate(tiles):
        gm, sh = (gm_l, sh_l) if ti < len(tiles) - 1 else (gm_c, sh_c)
        for c in range(nch):
            sqb, dsq = sqr.next()
            S.op(eng_sq, lambda e: e.tensor_tensor(out=sqb[:, :n], in0=h[:, c, t0:t0 + n], in1=h[:, c, t0:t0 + n], op=ALU.mult),
                 reads=[dh[ti]], writes=[dsq])
            S.op("pe", lambda e: e.matmul(ps_stat[:, :n], lhsT=ones[:, :], rhs=sqb[:, :n], start=(c == 0), stop=(c == nch - 1)),
                 reads=[dsq, done], writes=[dps_stat])
        S.op("act", lambda e: e.activation(out=rt[:, :n], in_=ps_stat[:, :n], func=AF.Sqrt, bias=epsb[0][:, 0:1], scale=1.0 / (nch * 128)),
             reads=[dps_stat, epsb[1]], writes=[drt])
        S.op("dve", lambda e: e.reciprocal(out=rt[:, :n], in_=rt[:, :n]), reads=[drt], writes=[drt])
        for c in range(nch):
            tb, dt_ = tmp.next()
            S.op("dve", lambda e: e.tensor_tensor(out=tb[:, :n], in0=h[:, c, t0:t0 + n], in1=rt[:, :n], op=ALU.mult),
                 reads=[dh[ti], drt], writes=[dt_])
            S.op("act", lambda e: e.activation(out=xn[:, c, t0:t0 + n], in_=tb[:, :n], func=AF.Identity,
                                               bias=sh[:, c:c + 1], scale=gm[:, c:c + 1]),
                 reads=[dt_, dmv], writes=[dxn[ti]])


epsb = [None, None]


def make_consts(S):
    ones = S.sb([128, 128], F32)
    done = Dep()
    S.op("pool", lambda e: e.memset(ones[:], 1.0), writes=[done])
    eb = S.sb([128, 1], F32)
    deb = Dep()
    S.op("pool", lambda e: e.memset(eb[:], EPS), writes=[deb])
    epsb[0], epsb[1] = eb, deb
    return ones, done


I32 = mybir.dt.int32
MODC = 9 * D // NCORES
NTOK = SEQ + CTX
NKC = NTOK // 128
ROWS = SEQ // 64
OPAD = 384


class G_:
    pass


def lat_ctx_views(ap2d, jreg):
    lat = ap2d[:, 0:SEQ].rearrange("x (j t) -> j x t", t=TL)[jreg]
    ctx = ap2d[:, SEQ:NTOK].rearrange("x (j t) -> j x t", t=TC)[jreg]
    return lat, ctx


def emit_mod(S, G):
    S.phase_begin()
    sc = S.sb([128, 16, 2], F32); dsc = Dep()
    S.dma("sp", sc[:], G.cc[:, :, :], writes=[dsc])
    S.op("act", lambda e: e.activation(out=sc[:], in_=sc[:], func=AF.Silu), reads=[dsc], writes=[dsc])
    abs_ = S.sb([128, 72], F32); dab = Dep()
    S.dma("sp", abs_[:], G.abT[:, :], writes=[dab])
    res = S.sb([128, 2, 72], F32); dres = Dep()
    wb = RR([(S.sb([128, 16, 512], F32), Dep()) for _ in range(2)])
    ps = S.ps([128, 72, 2]); dps = Dep()
    k = 0
    for l in range(4):
        awv = G.aw[l * D:(l + 1) * D, :].rearrange("(c p) n -> p c n", p=128)
        for c0 in range(0, MODC, 512):
            n = min(512, MODC - c0)
            w, dw = wb.next()
            S.dma("sp" if k % 2 == 0 else "act", w[:, :, :n], awv[:, :, c0:c0 + n], writes=[dw])
            k += 1
            for q in range(n // 128):
                col = l * 18 + c0 // 128 + q
                for c in range(16):
                    S.op("pe", lambda e: e.matmul(ps[:, col, :], lhsT=w[:, c, q * 128:(q + 1) * 128], rhs=sc[:, c, :], start=(c == 0), stop=(c == 15)),
                         reads=[dw, dsc], writes=[dps])
    for r in range(2):
        S.op("dve", lambda e: e.tensor_tensor(out=res[:, r, :], in0=ps[:, :, r], in1=abs_[:], op=ALU.add), reads=[dps, dab], writes=[dres])
    dmod = Dep(); dall = Dep()
    S.dma("sp", G.modT_local[:, :], res[:].rearrange("p r q -> p (r q)"), reads=[dres], writes=[dmod])
    S.cc("AllGather", [G.modT_local], [G.modT_all], reads=[dmod], writes=[dall])
    S.phase_end()


def load_mv(S, G, mvs, dmv, l, base, gidx, final=False):
    S.dma("sp", mvs[:, 0, :], G.ng_all[:, gidx, :], writes=[dmv])
    S.dma("sp", mvs[:, 7, :], G.ng_all[:, 12 if final else gidx, :], writes=[dmv])
    for r in range(2):
        for mi in range(3):
            m = base + mi
            row = 1 + r * 3 + mi
            c = 0
            while c < 16:
                q = 16 * m + c
                jq, ql = q // 18, q % 18
                run = min(16 - c, 18 - ql)
                col = r * 72 + l * 18 + ql
                S.dma("sp", mvs[:, row, c:c + run], G.modT_all[jq * 128:(jq + 1) * 128, col:col + run], writes=[dmv])
                c += run


def mv_prep(S, mvs, dmv, need_gate_half):
    gm_l = S.sb([128, 16], F32); gm_c = S.sb([128, 16], F32)
    S.op("dve", lambda e: e.scalar_tensor_tensor(out=gm_l[:], in0=mvs[:, 2, :], scalar=1.0, in1=mvs[:, 0, :], op0=ALU.add, op1=ALU.mult), reads=[dmv], writes=[dmv])
    S.op("dve", lambda e: e.scalar_tensor_tensor(out=gm_c[:], in0=mvs[:, 5, :], scalar=1.0, in1=mvs[:, 0, :], op0=ALU.add, op1=ALU.mult), reads=[dmv], writes=[dmv])
    hg_l = hg_c = None
    if need_gate_half:
        hg_l = S.sb([128, 16], F32); hg_c = S.sb([128, 16], F32)
        S.op("dve", lambda e: e.tensor_scalar(out=hg_l[:], in0=mvs[:, 3, :], scalar1=0.5, scalar2=None, op0=ALU.mult), reads=[dmv], writes=[dmv])
        S.op("dve", lambda e: e.tensor_scalar(out=hg_c[:], in0=mvs[:, 6, :], scalar1=0.5, scalar2=None, op0=ALU.mult), reads=[dmv], writes=[dmv])
    return gm_l, gm_c, hg_l, hg_c


def emit_ffn(S, G, hin, hout, w_in, w_out, l, k, final_norm=False):
    S.phase_begin()
    ones, done = G.ones, G.done
    NCH = D // 128
    h = S.sb([128, NCH, T], F32, "h")
    dh = [Dep() for _ in TILES]
    hv = hin.rearrange("(c p) t -> p c t", p=128)
    for ti, (t0, n) in enumerate(TILES):
        S.dma("sp", h[:, :, t0:t0 + n], hv[:, :, t0:t0 + n], writes=[dh[ti]])
    mvs = S.sb([128, 8, 16], F32, "mvs"); dmv = Dep()
    load_mv(S, G, mvs, dmv, l, 6 * k, 3 * l + 2 * k, final_norm)
    gm_l, gm_c, hg_l, hg_c = mv_prep(S, mvs, dmv, True)
    xn = S.sb([128, NCH, T], BF16, "xn")
    dxn = [Dep() for _ in TILES]
    ps_stat = S.ps([128, 512]); dps_stat = Dep()
    emit_norm_mod(S, h, dh, xn, dxn, gm_l, mvs[:, 1, :], gm_c, mvs[:, 4, :], dmv, ones, done, ps_stat, dps_stat)
    w_in_v = w_in.rearrange("(c p) n -> p c n", p=128)
    w_out_v = w_out.rearrange("(c p) n -> p c n", p=128)
    GC = 4
    NG = DFF // 128 // GC
    wab = RR([(S.sb([128, 2, NCH, 256], BF16), Dep()) for _ in range(2)])
    wo = RR([(S.sb([128, GC, 512], BF16), Dep()) for _ in range(2)])
    gbuf = RR([(S.sb([128, GC, T], BF16), [Dep() for _ in TILES]) for _ in range(2)])
    psA = RR([(S.ps([128, 512]), Dep()) for _ in range(2)])
    psB = RR([(S.ps([128, 512]), Dep()) for _ in range(2)])
    psO = RR([(S.ps([128, 512]), Dep()) for _ in range(2)])
    sa = RR([(S.sb([128, 512], F32), Dep()) for _ in range(2)])
    for gi in range(NG):
        g, dg = gbuf.next()
        for b2 in range(GC // 2):
            f0 = (gi * GC + b2 * 2) * 128
            w, dw = wab.next()
            S.dma("pool", w[:, 0], w_in_v[:, :, f0:f0 + 256], writes=[dw])
            S.dma("pool", w[:, 1], w_in_v[:, :, DFF + f0:DFF + f0 + 256], writes=[dw])
            for j in range(2):
                fl = b2 * 2 + j
                for ti, (t0, n) in enumerate(TILES):
                    pa, dpa = psA.next()
                    pb, dpb = psB.next()
                    for c in range(NCH):
                        S.op("pe", lambda e: e.matmul(pa[:, :n], lhsT=w[:, 0, c, j * 128:(j + 1) * 128], rhs=xn[:, c, t0:t0 + n],
                                                      start=(c == 0), stop=(c == NCH - 1)), reads=[dw, dxn[ti]], writes=[dpa])
                    for c in range(NCH):
                        S.op("pe", lambda e: e.matmul(pb[:, :n], lhsT=w[:, 1, c, j * 128:(j + 1) * 128], rhs=xn[:, c, t0:t0 + n],
                                                      start=(c == 0), stop=(c == NCH - 1)), reads=[dw, dxn[ti]], writes=[dpb])
                    sab, dsa = sa.next()
                    S.op("act", lambda e: e.activation(out=sab[:, :n], in_=pa[:, :n], func=AF.Silu), reads=[dpa], writes=[dsa])
                    S.op("dve", lambda e: e.tensor_tensor(out=g[:, fl, t0:t0 + n], in0=sab[:, :n], in1=pb[:, :n], op=ALU.mult),
                         reads=[dsa, dpb], writes=[dg[ti]])
        for q in range(4):
            wob, dwo = wo.next()
            S.dma("pool", wob[:], w_out_v[:, gi * GC:(gi + 1) * GC, q * 512:(q + 1) * 512], writes=[dwo])
            for dcl in range(4):
                dc = q * 4 + dcl
                for ti, (t0, n) in enumerate(TILES):
                    po, dpo = psO.next()
                    for j in range(GC):
                        S.op("pe", lambda e: e.matmul(po[:, :n], lhsT=wob[:, j, dcl * 128:(dcl + 1) * 128], rhs=g[:, j, t0:t0 + n],
                                                      start=(j == 0), stop=(j == GC - 1)), reads=[dwo, dg[ti]], writes=[dpo])
                    hg = hg_l if ti < len(TILES) - 1 else hg_c
                    S.op("dve", lambda e: e.scalar_tensor_tensor(out=h[:, dc, t0:t0 + n], in0=po[:, :n], scalar=hg[:, dc:dc + 1],
                                                                 in1=h[:, dc, t0:t0 + n], op0=ALU.mult, op1=ALU.add),
                         reads=[dpo, dmv, dh[ti]], writes=[dh[ti]])
    dout = Dep()
    ov = hout.rearrange("(c p) t -> p c t", p=128)
    if final_norm:
        zero = S.sb([128, 16], F32)
        S.op("pool", lambda e: e.memset(zero[:], 0.0), writes=[dmv])
        emit_norm_mod(S, h, dh, h, dh, mvs[:, 7, :], zero, mvs[:, 7, :], zero, dmv, ones, done, ps_stat, dps_stat)
    for ti, (t0, n) in enumerate(TILES):
        S.dma("sp", ov[:, :, t0:t0 + n], h[:, :, t0:t0 + n], reads=[dh[ti]], writes=[dout])
    S.phase_end()


def emit_proj(S, G, hin, w, nout, kind, l, out_chunk, st_dt=F32):
    S.phase_begin()
    ones, done = G.ones, G.done
    NCH = D // 128
    h = S.sb([128, NCH, T], F32, "h")
    dh = [Dep() for _ in TILES]
    hv = hin.rearrange("(c p) t -> p c t", p=128)
    for ti, (t0, n) in enumerate(TILES):
        S.dma("sp", h[:, :, t0:t0 + n], hv[:, :, t0:t0 + n], writes=[dh[ti]])
    mvs = S.sb([128, 8, 16], F32, "mvs"); dmv = Dep()
    load_mv(S, G, mvs, dmv, l, 3, 3 * l + 1)
    gm_l, gm_c, _, _ = mv_prep(S, mvs, dmv, False)
    xn = S.sb([128, NCH, T], BF16, "xn")
    dxn = [Dep() for _ in TILES]
    ps_stat = S.ps([128, 512]); dps_stat = Dep()
    emit_norm_mod(S, h, dh, xn, dxn, gm_l, mvs[:, 1, :], gm_c, mvs[:, 4, :], dmv, ones, done, ps_stat, dps_stat)
    if kind == "gqa":
        qk = S.sb([128, 2], F32); cs_ = S.sb([128, TL], F32); sn_ = S.sb([128, TL], F32); rm = S.sb([128, 128], F32)
        dq = Dep()
        S.dma("sp", qk[:], G.qkn[:, :], writes=[dq])
        S.dma("sp", cs_[:], G.cosF[:, :], writes=[dq])
        S.dma("sp", sn_[:], G.sinF[:, :], writes=[dq])
        S.dma("sp", rm[:], G.rotm[:, :], writes=[dq])
        xs = RR([(S.sb([128, 512], F32), Dep()) for _ in range(2)])
        sqs = RR([(S.sb([128, 512], F32), Dep()) for _ in range(2)])
        rts = RR([(S.sb([128, 512], F32), Dep()) for _ in range(2)])
        t1s = RR([(S.sb([128, 512], F32), Dep()) for _ in range(2)])
        t2s = RR([(S.sb([128, 512], F32), Dep()) for _ in range(2)])
        ps_rot = S.ps([128, 512]); dps_rot = Dep()
    BW = 384
    wv = w.rearrange("(c p) n -> p c n", p=128)
    wblk = RR([(S.sb([128, NCH, BW], BF16), Dep()) for _ in range(2)])
    pso = RR([(S.ps([128, 512]), Dep()) for _ in range(3)])
    stg = RR([(S.sb([128, T], st_dt), Dep()) for _ in range(3)])
    for b in range(nout // BW):
        wb, dw = wblk.next()
        S.dma("pool", wb[:], wv[:, :, b * BW:(b + 1) * BW], writes=[dw])
        for j in range(BW // 128):
            oc = b * (BW // 128) + j
            st, dst = stg.next()
            for ti, (t0, n) in enumerate(TILES):
                p, dp = pso.next()
                for c in range(NCH):
                    S.op("pe", lambda e: e.matmul(p[:, :n], lhsT=wb[:, c, j * 128:(j + 1) * 128], rhs=xn[:, c, t0:t0 + n],
                                                  start=(c == 0), stop=(c == NCH - 1)), reads=[dw, dxn[ti]], writes=[dp])
                if kind == "gqa" and oc < 20:
                    col = 0 if oc < 16 else 1
                    x, dx = xs.next(); sq, dsq = sqs.next(); rt, drt = rts.next()
                    S.op("act", lambda e: e.copy(out=x[:, :n], in_=p[:, :n]), reads=[dp], writes=[dx])
                    S.op("pool", lambda e: e.tensor_tensor(out=sq[:, :n], in0=x[:, :n], in1=x[:, :n], op=ALU.mult), reads=[dx], writes=[dsq])
                    S.op("pe", lambda e: e.matmul(ps_stat[:, :n], lhsT=ones[:, :], rhs=sq[:, :n], start=True, stop=True),
                         reads=[dsq, done], writes=[dps_stat])
                    S.op("act", lambda e: e.activation(out=rt[:, :n], in_=ps_stat[:, :n], func=AF.Sqrt, bias=epsb[0][:, 0:1], scale=1.0 / 128),
                         reads=[dps_stat, epsb[1]], writes=[drt])
                    S.op("dve", lambda e: e.reciprocal(out=rt[:, :n], in_=rt[:, :n]), reads=[drt], writes=[drt])
                    if ti < len(TILES) - 1:
                        S.op("dve", lambda e: e.scalar_tensor_tensor(out=x[:, :n], in0=x[:, :n], scalar=qk[:, col:col + 1], in1=rt[:, :n], op0=ALU.mult, op1=ALU.mult),
                             reads=[dx, drt, dq], writes=[dx])
                        S.op("pe", lambda e: e.matmul(ps_rot[:, :n], lhsT=rm[:, :], rhs=x[:, :n], start=True, stop=True),
                             reads=[dx, dq], writes=[dps_rot])
                        t1, dt1 = t1s.next(); t2, dt2 = t2s.next()
                        S.op("pool", lambda e: e.tensor_tensor(out=t1[:, :n], in0=x[:, :n], in1=cs_[:, t0:t0 + n], op=ALU.mult), reads=[dx, dq], writes=[dt1])
                        S.op("dve", lambda e: e.tensor_tensor(out=t2[:, :n], in0=ps_rot[:, :n], in1=sn_[:, t0:t0 + n], op=ALU.mult), reads=[dps_rot, dq], writes=[dt2])
                        S.op("dve", lambda e: e.tensor_tensor(out=st[:, t0:t0 + n], in0=t1[:, :n], in1=t2[:, :n], op=ALU.add), reads=[dt1, dt2], writes=[dst])
                    else:
                        S.op("dve", lambda e: e.scalar_tensor_tensor(out=st[:, t0:t0 + n], in0=x[:, :n], scalar=qk[:, col:col + 1], in1=rt[:, :n], op0=ALU.mult, op1=ALU.mult),
                             reads=[dx, drt, dq], writes=[dst])
                else:
                    if (oc + ti) % 2 == 0:
                        S.op("act", lambda e: e.copy(out=st[:, t0:t0 + n], in_=p[:, :n]), reads=[dp], writes=[dst])
                    else:
                        S.op("dve", lambda e: e.tensor_copy(out=st[:, t0:t0 + n], in_=p[:, :n]), reads=[dp], writes=[dst])
            out_chunk(oc, st, dst)


def emit_oproj(S, G, a_src, w, hin, hout, l, kc, ssm, zT=None):
    S.phase_begin()
    ones, done = G.ones, G.done
    mvs = S.sb([128, 8, 16], F32, "mvs"); dmv = Dep()
    load_mv(S, G, mvs, dmv, l, 3, 3 * l + 1)
    a = S.sb([128, kc, T], BF16, "a")
    da = [Dep() for _ in TILES]
    groups = a_src("sp" if ssm else "pool")

    def asrc(g, t0, n):
        return g[2][:, :, t0:t0 + n] if t0 < TL else g[3][:, :, t0 - TL:t0 - TL + n]
    if not ssm:
        for ti, (t0, n) in enumerate(TILES):
            for g in groups:
                S.dma("pool", a[:, g[0]:g[0] + g[1], t0:t0 + n], asrc(g, t0, n), writes=[da[ti]])
    else:
        ngs = S.sb([128, kc], F32); zero = S.sb([128, kc], F32)
        S.dma("sp", ngs[:], G.ssm_ng[:, :], writes=[dmv])
        S.op("pool", lambda e: e.memset(zero[:], 0.0), writes=[dmv])
        ps_stat = S.ps([128, 512]); dps_stat = Dep()
        ytile = S.sb([128, kc, 512], F32, "ytile"); dyt = Dep()
        zv = zT.rearrange("(c p) t -> p c t", p=128)
        yb = RR([(S.sb([128, 512], F32), Dep()) for _ in range(2)])
        zb = RR([(S.sb([128, 512], F32), Dep()) for _ in range(2)])
        for ti, (t0, n) in enumerate(TILES):
            for c in range(kc):
                y_, dy_ = yb.next(); z_, dz_ = zb.next()
                S.dma("sp", y_[:, :n], asrc(groups[0], t0, n)[:, c, :], writes=[dy_])
                S.dma("sp", z_[:, :n], zv[:, c, t0:t0 + n], writes=[dz_])
                S.op("act", lambda e: e.activation(out=z_[:, :n], in_=z_[:, :n], func=AF.Silu), writes=[dz_])
                S.op("dve", lambda e: e.tensor_tensor(out=ytile[:, c, :n], in0=y_[:, :n], in1=z_[:, :n], op=ALU.mult), reads=[dy_, dz_], writes=[dyt])

            class _V:
                def __getitem__(self, idx):
                    return ytile[:, idx[1], 0:n]
            emit_norm_mod(S, _V(), [dyt], a[:, :, t0:t0 + n], [da[ti]], ngs, zero, ngs, zero, dmv, ones, done, ps_stat, dps_stat,
                          nch=kc, tiles=[(0, n)])
    wv = w.rearrange("(c p) n -> p c n", p=128)
    hv = hin.rearrange("(c p) t -> p c t", p=128)
    ov = hout.rearrange("(c p) t -> p c t", p=128)
    BW2 = 128 if ssm else 256
    wblk = RR([(S.sb([128, kc, BW2], BF16), Dep()) for _ in range(2)])
    hs = RR([(S.sb([128, T], F32), Dep()) for _ in range(3)])
    pso = RR([(S.ps([128, 512]), Dep()) for _ in range(3)])
    dout = Dep()
    for b in range(D // BW2):
        wb, dw = wblk.next()
        S.dma("pool", wb[:], wv[:, :, b * BW2:(b + 1) * BW2], writes=[dw])
        for j in range(BW2 // 128):
            dc = b * (BW2 // 128) + j
            hb, dhb = hs.next()
            S.dma("sp", hb[:], hv[:, dc, :], writes=[dhb])
            for ti, (t0, n) in enumerate(TILES):
                p, dp = pso.next()
                for c in range(kc):
                    S.op("pe", lambda e: e.matmul(p[:, :n], lhsT=wb[:, c, j * 128:(j + 1) * 128], rhs=a[:, c, t0:t0 + n],
                                                  start=(c == 0), stop=(c == kc - 1)), reads=[dw, da[ti]], writes=[dp])
                gcol = 3 if ti < len(TILES) - 1 else 6
                S.op("dve", lambda e: e.scalar_tensor_tensor(out=hb[:, t0:t0 + n], in0=p[:, :n], scalar=mvs[:, gcol, dc:dc + 1],
                                                             in1=hb[:, t0:t0 + n], op0=ALU.mult, op1=ALU.add),
                     reads=[dp, dmv, dhb], writes=[dhb])
            S.dma("sp", ov[:, dc, :], hb[:], reads=[dhb], writes=[dout])
    S.phase_end()


def na_row_info(r):
    rs = min(max(r - 4, 0), ROWS - 8)
    cb = min(rs // 2, 59)
    return rs, cb, (rs - 2 * cb, r - 2 * cb)


def na_patterns():
    pats = []
    for r in range(ROWS):
        k = na_row_info(r)[2]
        if k not in pats:
            pats.append(k)
    return pats


def na_bias_tables(rpb):
    pats = na_patterns()
    out = np.zeros((rpb.shape[0], len(pats), 128, 7 * 64), np.float32)
    c = np.arange(64)
    cst = np.clip(c - 8, 0, 48)
    kcol = np.arange(64)
    for pi, (a, b) in enumerate(pats):
        for i in range(5):
            for half in range(2):
                lr = 2 * i + half
                blk = np.full((rpb.shape[0], 64, 64), NEG, np.float32)
                if a <= lr < a + 8:
                    dr = lr - b + 7
                    dcol = kcol[:, None] - c[None, :] + 15
                    ok = (kcol[:, None] >= cst[None, :]) & (kcol[:, None] < cst[None, :] + 16)
                    g = rpb[:, dr][:, np.clip(dcol, 0, 30)]
                    blk = np.where(ok[None], g, np.float32(NEG)).astype(np.float32)
                out[:, pi, half * 64:(half + 1) * 64, i * 64:(i + 1) * 64] = blk
    return out


def load_feat(S, eng, dst, src_lat, src_ctx, dep):
    for r0 in range(0, 8, 2):
        S.dma(eng, dst[:, r0 * TL:(r0 + 2) * TL].rearrange("d (r t) -> d r t", t=TL), src_lat[r0:r0 + 2].rearrange("r d t -> d r t"), writes=[dep])
    S.dma(eng, dst[:, SEQ:NTOK].rearrange("d (r t) -> d r t", t=TC), src_ctx.rearrange("r d t -> d r t"), writes=[dep])


def transpose_v(S, G, vT, V, dsrc, ddst, pst, dpst):
    for g0 in range(0, NKC, 4):
        grp = list(range(g0, min(g0 + 4, NKC)))
        for i, kc in enumerate(grp):
            S.op("pe", lambda e: e.transpose(out=pst[:, i, :], in_=vT[:, kc * 128:(kc + 1) * 128], identity=G.identb[:, :]),
                 reads=[dsrc, G.dident], writes=[dpst])
        if (g0 // 4) % 2 == 0:
            S.op("act", lambda e: e.copy(out=V[:, g0:g0 + len(grp), :], in_=pst[:, 0:len(grp), :]), reads=[dpst], writes=[ddst])
        else:
            S.op("dve", lambda e: e.tensor_copy(out=V[:, g0:g0 + len(grp), :], in_=pst[:, 0:len(grp), :]), reads=[dpst], writes=[ddst])


def emit_na_core(S, G, src, bm_ap, oT_out, dma_eng="sp"):
    pats = na_patterns()
    npat = len(pats)
    scale = 128.0 ** -0.5
    pst = S.ps([128, 4, 128], BF16); dpst = Dep()
    qs, ks, vs, bms, dld = [], [], [], [], []
    for hh in range(2):
        d = Dep(); dvt = Dep()
        q_ = S.sb([128, NTOK], BF16); k_ = S.sb([128, NTOK], BF16); vT = S.sb([128, NTOK], BF16); v_ = S.sb([128, NKC, 128], BF16)
        b_ = S.sb([128, npat, 448], F32)
        for w_, dst_, dd in ((0, q_, d), (1, k_, d), (2, vT, dvt)):
            sl, sc_ = src(dma_eng, hh, w_)
            load_feat(S, dma_eng, dst_, sl, sc_, dd)
        S.dma("sp", b_[:], bm_ap(hh).rearrange("(n p) f -> p n f", p=128), writes=[d])
        transpose_v(S, G, vT, v_, dvt, d, pst, dpst)
        qs.append(q_); ks.append(k_); vs.append(v_); bms.append(b_); dld.append(d)
    psS = RR([(S.ps([128, 512]), Dep()) for _ in range(2)])
    psO = RR([(S.ps([128, 512]), Dep()) for _ in range(2)])
    psM = RR([(S.ps([128, 512]), Dep()) for _ in range(2)])
    tt = RR([(S.sb([128, 448], F32), Dep()) for _ in range(2)])
    pp = RR([(S.sb([128, 448], BF16), Dep()) for _ in range(2)])
    rc = RR([(S.sb([128, 64], F32), Dep()) for _ in range(2)])
    og = RR([(S.sb([128, 512], F32), Dep()) for _ in range(2)])
    dout = Dep()

    def stage1(hh, q0, chunks, bias_ap):
        nch = len(chunks)
        p, dp = psS.next()
        for i, kc in enumerate(chunks):
            S.op("pe", lambda e: e.matmul(p[:, i * 64:(i + 1) * 64], lhsT=ks[hh][:, kc * 128:(kc + 1) * 128], rhs=qs[hh][:, q0:q0 + 64],
                                          start=True, stop=True), reads=[dld[hh]], writes=[dp])
        pb, dpb = pp.next()
        if bias_ap is not None:
            t, dt_ = tt.next()
            S.op("dve", lambda e: e.scalar_tensor_tensor(out=t[:, :nch * 64], in0=p[:, :nch * 64], scalar=scale, in1=bias_ap,
                                                         op0=ALU.mult, op1=ALU.add), reads=[dp, dld[hh]], writes=[dt_])
            S.op("act", lambda e: e.activation(out=pb[:, :nch * 64], in_=t[:, :nch * 64], func=AF.Exp), reads=[dt_], writes=[dpb])
        else:
            S.op("act", lambda e: e.activation(out=pb[:, :nch * 64], in_=p[:, :nch * 64], func=AF.Exp, scale=scale), reads=[dp], writes=[dpb])
        return pb, dpb

    def stage2(hh, chunks, pb, dpb, ostg, dostg, slot, flush):
        nch = len(chunks)
        po, dpo = psO.next(); pm, dpm = psM.next()
        for i, kc in enumerate(chunks):
            S.op("pe", lambda e: e.matmul(po[:, 0:64], lhsT=vs[hh][:, kc, :], rhs=pb[:, i * 64:(i + 1) * 64],
                                          start=(i == 0), stop=(i == nch - 1)), reads=[dpb, dld[hh]], writes=[dpo])
        for i, kc in enumerate(chunks):
            S.op("pe", lambda e: e.matmul(pm[:, 0:64], lhsT=G.onesb[:, :], rhs=pb[:, i * 64:(i + 1) * 64],
                                          start=(i == 0), stop=(i == nch - 1)), reads=[dpb, G.dident], writes=[dpm])
        r_, dr_ = rc.next()
        S.op("dve", lambda e: e.reciprocal(out=r_[:, :], in_=pm[:, 0:64]), reads=[dpm], writes=[dr_])
        S.op("dve", lambda e: e.tensor_tensor(out=ostg[:, slot * 64:(slot + 1) * 64], in0=po[:, 0:64], in1=r_[:, :], op=ALU.mult),
             reads=[dpo, dr_], writes=[dostg])
        if flush is not None:
            S.dma("sp", flush[0], ostg[:, 0:flush[1]], reads=[dostg], writes=[dout])

    blocks = []
    for hh in range(2):
        ov = oT_out(hh)
        for r in range(ROWS):
            rs, cb, key = na_row_info(r)
            pi = pats.index(key)
            flush = (ov[:, (r - 7) * 64:(r + 1) * 64], 512) if r % 8 == 7 else None
            blocks.append((hh, r * 64, list(range(cb, cb + 5)) + [64, 65], bms[hh][:, pi, :], r % 8, flush))
        for qb in range(4):
            flush = (ov[:, SEQ:SEQ + 256], 256) if qb == 3 else None
            blocks.append((hh, SEQ + qb * 64, [64, 65], None, qb, flush))
    prev = None
    cur_stg = None
    for (hh, q0, chunks, bias_ap, slot, flush) in blocks:
        pb, dpb = stage1(hh, q0, chunks, bias_ap)
        if prev is not None:
            stage2(*prev)
        if slot == 0:
            cur_stg = og.next()
        prev = (hh, chunks, pb, dpb, cur_stg[0], cur_stg[1], slot, flush)
    stage2(*prev)
    return dout


def emit_gqa_core(S, G, src_q, src_kv, oT_out):
    scale = 128.0 ** -0.5
    dld = Dep(); dvt = Dep()
    qs = [S.sb([128, NTOK], BF16) for _ in range(2)]
    k_ = S.sb([128, NTOK], BF16); vT = S.sb([128, NTOK], BF16); v_ = S.sb([128, NKC, 128], BF16)
    pst = S.ps([128, 4, 128], BF16); dpst = Dep()
    for hh in range(2):
        sl, sc_ = src_q("pool", hh)
        load_feat(S, "pool", qs[hh], sl, sc_, dld)
    sl, sc_ = src_kv("pool", 0)
    load_feat(S, "pool", k_, sl, sc_, dld)
    sl, sc_ = src_kv("pool", 1)
    load_feat(S, "pool", vT, sl, sc_, dvt)
    transpose_v(S, G, vT, v_, dvt, dld, pst, dpst)
    psS = RR([(S.ps([128, 512]), Dep()) for _ in range(3)])
    psO = RR([(S.ps([128, 512]), Dep()) for _ in range(2)])
    psM = RR([(S.ps([128, 512]), Dep()) for _ in range(2)])
    pp = RR([(S.sb([128, 512], BF16), Dep()) for _ in range(3)])
    rc = RR([(S.sb([128, 512], F32), Dep()) for _ in range(2)])
    og = RR([(S.sb([128, 512], F32), Dep()) for _ in range(2)])
    accs = RR([(S.sb([128, 512], F32), Dep()) for _ in range(2)])
    dout = Dep()
    for hh in range(2):
        ov = oT_out(hh)
        blocks = [(qb * 512, 512, list(range(NKC))) for qb in range(SEQ // 512)] + [(SEQ, 256, [64, 65])]
        for q0, nq, chunks in blocks:
            po, dpo = psO.next(); pm, dpm = psM.next()
            nchk = len(chunks)
            ac, dac = accs.next()

            def pv(i, kc, pb, dpb):
                S.op("pe", lambda e: e.matmul(po[:, :nq], lhsT=v_[:, kc, :], rhs=pb[:, :nq], start=(i == 0), stop=(i == nchk - 1)),
                     reads=[dpb, dld], writes=[dpo])
                if i == 0:
                    S.op("dve", lambda e: e.tensor_copy(out=ac[:, :nq], in_=pb[:, :nq]), reads=[dpb], writes=[dac])
                else:
                    S.op("dve", lambda e: e.tensor_tensor(out=ac[:, :nq], in0=ac[:, :nq], in1=pb[:, :nq], op=ALU.add), reads=[dpb], writes=[dac])
            prev = None
            for i, kc in enumerate(chunks):
                p, dp = psS.next()
                S.op("pe", lambda e: e.matmul(p[:, :nq], lhsT=k_[:, kc * 128:(kc + 1) * 128], rhs=qs[hh][:, q0:q0 + nq], start=True, stop=True),
                     reads=[dld], writes=[dp])
                pb, dpb = pp.next()
                S.op("act", lambda e: e.activation(out=pb[:, :nq], in_=p[:, :nq], func=AF.Exp, scale=scale), reads=[dp], writes=[dpb])
                if prev is not None:
                    pv(*prev)
                prev = (i, kc, pb, dpb)
            pv(*prev)
            S.op("pe", lambda e: e.matmul(pm[:, :nq], lhsT=G.ones[:, :], rhs=ac[:, :nq], start=True, stop=True), reads=[dac, G.done], writes=[dpm])
            r_, dr_ = rc.next(); ostg, dostg = og.next()
            S.op("dve", lambda e: e.reciprocal(out=r_[:, :nq], in_=pm[:, :nq]), reads=[dpm], writes=[dr_])
            S.op("dve", lambda e: e.tensor_tensor(out=ostg[:, :nq], in0=po[:, :nq], in1=r_[:, :nq], op=ALU.mult), reads=[dpo, dr_], writes=[dostg])
            S.dma("sp", ov[:, q0:q0 + nq], ostg[:, :nq], reads=[dostg], writes=[dout])
    return dout


SSD_ORDER = [[64, 65] + list(range(64)), [65, 64] + list(range(63, -1, -1))]


def ssd_consts():
    k = np.arange(128)
    Uf = (k[:, None] <= k[None, :]).astype(np.float32)
    Ub = (k[:, None] >= k[None, :]).astype(np.float32)
    ident = np.eye(128, dtype=np.float32)
    NEGf = np.where(k[None, :] < k[:, None], np.float32(NEG), np.float32(0)).astype(np.float32)
    NEGb = np.where(k[None, :] > k[:, None], np.float32(NEG), np.float32(0)).astype(np.float32)
    return np.ascontiguousarray(np.stack([Uf, Ub, -Uf, -Ub, ident, NEGf, NEGb], axis=1))


def emit_ssd_core(S, G, src, yT_out):
    ones, done = G.ones, G.done
    cst, dcst = G.cst, G.dident
    identb = G.identb
    neg4 = S.sb([128, 2, 4, 128], F32); dn4 = Dep()
    for d in range(2):
        S.op("dve", lambda e: e.tensor_copy(out=neg4[:, d], in_=cst[:, 5 + d, :].unsqueeze(1).to_broadcast([128, 4, 128])), reads=[dcst], writes=[dn4])
    cws = S.sb([128, 6, 5], F32); cbs = S.sb([128, 6], F32); dcw = Dep()
    S.dma("sp", cws[:], G.ssm_cw[:, :, :], writes=[dcw])
    S.dma("sp", cbs[:], G.ssm_cb[:, :], writes=[dcw])
    psSm = S.ps([128, 512]); dSm = Dep(); dCB = Dep()
    sp_src = src("sp")
    dts = S.sb([128, NKC, 16], F32); das = S.sb([128, NKC, 16], F32); vs_ = S.sb([128, 3, 16], F32)
    ddt = Dep(); dvs = Dep()
    S.dma("sp", vs_[:], G.ssm_vecs[:, :, :], writes=[dvs])
    dtTs = RR([(S.sb([16, TL], F32), Dep()) for _ in range(2)])
    for r in range(9):
        dtT, ddtT = dtTs.next()
        if r < 8:
            S.dma("sp", dtT[:, :], sp_src[r, 768:784, 0:TL], writes=[ddtT])
            c0, ncl = r * 8, 8
        else:
            S.dma("sp", dtT[:, 0:CTX].rearrange("d (r t) -> d r t", t=TC), sp_src[:, 768:784, TL:T].rearrange("r d t -> d r t"), writes=[ddtT])
            c0, ncl = 64, 2
        for i in range(ncl):
            S.op("pe", lambda e: e.transpose(out=psSm[:, i * 16:(i + 1) * 16], in_=dtT[:, i * 128:(i + 1) * 128], identity=cst[0:16, 4, 0:16]),
                 reads=[ddtT, dcst], writes=[dSm])
        S.op("dve", lambda e: e.tensor_copy(out=dts[:, c0:c0 + ncl, :], in_=psSm[:, 0:ncl * 16].rearrange("p (c j) -> p c j", j=16)),
             reads=[dSm], writes=[ddt])
    S.op("dve", lambda e: e.tensor_tensor(out=dts[:], in0=dts[:], in1=vs_[:, 0, :].unsqueeze(1).to_broadcast([128, NKC, 16]), op=ALU.add),
         reads=[dvs], writes=[ddt])
    S.op("act", lambda e: e.activation(out=dts[:], in_=dts[:], func=AF.Exp), writes=[ddt])
    S.op("act", lambda e: e.activation(out=dts[:], in_=dts[:], func=AF.Ln, bias=ones[:, 0:1]), reads=[done], writes=[ddt])
    ea = S.sb([128, 16], F32); dsum = S.sb([128, 8], F32)
    S.op("act", lambda e: e.activation(out=ea[:], in_=vs_[:, 1, :], func=AF.Exp), reads=[dvs], writes=[dvs])
    S.op("dve", lambda e: e.scalar_tensor_tensor(out=das[:], in0=dts[:], scalar=-1.0, in1=ea[:, :].unsqueeze(1).to_broadcast([128, NKC, 16]),
                                                 op0=ALU.mult, op1=ALU.mult), reads=[ddt, dvs], writes=[ddt])
    S.op("dve", lambda e: e.tensor_tensor(out=dsum[:], in0=vs_[:, 2, 0:8], in1=vs_[:, 2, 8:16], op=ALU.add), reads=[dvs], writes=[dvs])
    xtok = S.sb([128, NKC, 512], BF16, "xtok"); btok = S.sb([128, NKC, 128], BF16, "btok")
    BT = S.sb([128, NTOK], BF16, "BT"); CT = S.sb([128, NTOK], BF16, "CT")
    dxt = [Dep() for _ in range(NKC)]
    PL = 512
    pieces = [(a, PL) for a in range(0, SEQ, PL)] + [(SEQ, CTX)]
    xin = RR([(S.sb([128, PL + 4], F32), Dep()) for _ in range(2)])
    acc = RR([(S.sb([128, PL], F32), Dep()) for _ in range(2)])
    cob = RR([(S.sb([128, PL], BF16), Dep()) for _ in range(2)])
    pst = S.ps([128, 4, 128], BF16); dpst = Dep()
    qi = 0
    for cch in range(6):
        for (a, L) in pieces:
            xi, dxi = xin.next()
            eng = "sp" if qi % 2 == 0 else "act"
            qi += 1
            sv = src(eng)[:, cch * 128:(cch + 1) * 128, :]
            if a >= SEQ:
                S.op("pool", lambda e: e.memset(xi[:, 0:2], 0.0), writes=[dxi])
                S.op("pool", lambda e: e.memset(xi[:, L + 2:L + 4], 0.0), writes=[dxi])
                S.dma(eng, xi[:, 2:2 + L].rearrange("p (r t) -> p r t", t=TC), sv[:, :, TL:T].rearrange("r p t -> p r t"), writes=[dxi])
            else:
                r, o = a // TL, a % TL
                lo = o - 2 if o >= 2 else o
                hi = o + L + 2 if o + L + 2 <= TL else o + L
                S.dma(eng, xi[:, 2 + (lo - o):2 + (hi - o)], sv[r, :, lo:hi], writes=[dxi])
                if lo == o:
                    if a == 0:
                        S.op("pool", lambda e: e.memset(xi[:, 0:2], 0.0), writes=[dxi])
                    else:
                        S.dma(eng, xi[:, 0:2], sv[r - 1, :, TL - 2:TL], writes=[dxi])
                if hi == o + L:
                    if a + L == SEQ:
                        S.op("pool", lambda e: e.memset(xi[:, L + 2:L + 4], 0.0), writes=[dxi])
                    else:
                        S.dma(eng, xi[:, L + 2:L + 4], sv[r + 1, :, 0:2], writes=[dxi])
            ac, dac_ = acc.next()
            S.op("dve", lambda e: e.tensor_scalar(out=ac[:, :L], in0=xi[:, 2:2 + L], scalar1=cws[:, cch, 2:3], scalar2=None, op0=ALU.mult),
                 reads=[dxi, dcw], writes=[dac_])
            for k in (0, 1, 3, 4):
                S.op("dve", lambda e: e.scalar_tensor_tensor(out=ac[:, :L], in0=xi[:, k:k + L], scalar=cws[:, cch, k:k + 1], in1=ac[:, :L],
                                                             op0=ALU.mult, op1=ALU.add), reads=[dxi, dcw], writes=[dac_])
            tcs = list(range(a // 128, (a + L) // 128))
            if cch < 5:
                co, dco = cob.next()
                S.op("act", lambda e: e.activation(out=co[:, :L], in_=ac[:, :L], func=AF.Silu, bias=cbs[:, cch:cch + 1]), reads=[dac_, dcw], writes=[dco])
                if cch == 4:
                    S.op("pool", lambda e: e.tensor_copy(out=BT[:, a:a + L], in_=co[:, :L]), reads=[dco], writes=[dxt[t] for t in tcs])
                for g0 in range(0, len(tcs), 4):
                    grp = tcs[g0:g0 + 4]
                    for i, tc_ in enumerate(grp):
                        S.op("pe", lambda e: e.transpose(out=pst[:, i, :], in_=co[:, (g0 + i) * 128:(g0 + i + 1) * 128], identity=identb[:, :]),
                             reads=[dco, dcst], writes=[dpst])
                    dst = xtok[:, grp[0]:grp[0] + len(grp), cch * 128:(cch + 1) * 128] if cch < 4 else btok[:, grp[0]:grp[0] + len(grp), :]
                    if (g0 // 4) % 2 == 0:
                        S.op("act", lambda e: e.copy(out=dst, in_=pst[:, 0:len(grp), :]), reads=[dpst], writes=[dxt[t] for t in grp])
                    else:
                        S.op("dve", lambda e: e.tensor_copy(out=dst, in_=pst[:, 0:len(grp), :]), reads=[dpst], writes=[dxt[t] for t in grp])
            else:
                S.op("act", lambda e: e.activation(out=CT[:, a:a + L], in_=ac[:, :L], func=AF.Silu, bias=cbs[:, cch:cch + 1]),
                     reads=[dac_, dcw], writes=[dxt[t] for t in tcs])
    Sst = [S.sb([128, 8, 64], F32) for _ in range(2)]
    Sbf = [S.sb([128, 512], BF16) for _ in range(2)]
    dS = [Dep(), Dep()]; dSbf = [Dep(), Dep()]
    for d in range(2):
        S.op("pool", lambda e: e.memset(Sst[d][:], 0.0), writes=[dS[d]])
        S.op("pool", lambda e: e.memset(Sbf[d][:], 0.0), writes=[dSbf[d]])
    rhsA = RR([(S.sb([128, 8, 128], F32), Dep()) for _ in range(2)])
    rhsB = RR([(S.sb([128, 8, 128], F32), Dep()) for _ in range(1)])
    expD = RR([(S.sb([128, 4, 128], F32), Dep()) for _ in range(4)])
    ecs_ = RR([(S.sb([128, 16], F32), Dep()) for _ in range(2)])
    MT = RR([(S.sb([128, 8, 128], BF16), Dep()) for _ in range(2)])
    xdt = RR([(S.sb([128, 8, 64], BF16), Dep()) for _ in range(2)])
    xw = RR([(S.sb([128, 8, 64], BF16), Dep()) for _ in range(2)])
    ysb = RR([(S.sb([128, 8, 64], F32), Dep()) for _ in range(2)])
    tb_ = RR([(S.sb([128, 8, 64], F32), Dep()) for _ in range(2)])
    ub_ = RR([(S.sb([128, 8, 64], F32), Dep()) for _ in range(1)])
    yts = RR([(S.sb([128, 4, 128], F32), Dep()) for _ in range(2)])
    pv = RR([(S.sb([128, 4, 128], F32), Dep()) for _ in range(2)])
    psD = RR([(S.ps([128, 4, 128]), Dep()) for _ in range(2)])
    psY = S.ps([128, 8, 64]); dY = Dep()
    psYo = S.ps([128, 8, 64]); dYo = Dep()
    psSp = S.ps([128, 8, 64]); dSp = Dep()
    psT = S.ps([128, 4, 128]); dT = Dep()
    dyg = [Dep() for _ in range(NKC)]
    yov = yT_out.rearrange("(cc p) t -> p cc t", p=128)
    visited = set()
    for step in range(NKC):
        for d in range(2):
            c = SSD_ORDER[d][step]
            j0 = d * 8
            U = cst[:, d, :]; negU = cst[:, 2 + d, :]; idn = cst[:, 4, :]
            col = 127 if d == 0 else 0
            dac = das[:, c, j0:j0 + 8]
            ra, dra = rhsA.next(); rb, drb = rhsB.next()
            S.op("dve", lambda e: e.tensor_tensor(out=ra[:], in0=U.unsqueeze(1).to_broadcast([128, 8, 128]),
                                                  in1=dac.unsqueeze(2).to_broadcast([128, 8, 128]), op=ALU.mult), reads=[dcst, ddt], writes=[dra])
            S.op("pool", lambda e: e.tensor_copy(out=rb[:], in_=dac.unsqueeze(2).to_broadcast([128, 8, 128])), reads=[ddt], writes=[drb])
            S.op("pe", lambda e: e.matmul(psSm[:, 0:8], lhsT=U, rhs=dac, start=True, stop=True), reads=[dcst, ddt], writes=[dSm])
            S.op("pe", lambda e: e.matmul(psSm[:, 8:16], lhsT=ones[:, :], rhs=dac, start=True, stop=True), reads=[done, ddt], writes=[dSm])
            ec, dec = ecs_.next()
            S.op("act", lambda e: e.activation(out=ec[:], in_=psSm[:, 0:16], func=AF.Exp), reads=[dSm], writes=[dec])
            eds = []
            for half in range(2):
                h4 = half * 4
                pD, dpD = psD.next()
                S.op("pe", lambda e: e.matmul(pD[:], lhsT=ones[:, :], rhs=ra[:, h4:h4 + 4, :], start=True, stop=False), reads=[dra, done], writes=[dpD])
                S.op("pe", lambda e: e.matmul(pD[:], lhsT=negU, rhs=rb[:, h4:h4 + 4, :], start=False, stop=False), reads=[drb, dcst], writes=[dpD])
                S.op("pe", lambda e: e.matmul(pD[:], lhsT=idn, rhs=neg4[:, d], start=False, stop=True), reads=[dcst, dn4], writes=[dpD])
                ed, ded = expD.next()
                S.op("act", lambda e: e.activation(out=ed[:], in_=pD[:], func=AF.Exp), reads=[dpD], writes=[ded])
                eds.append((ed, ded))
            S.op("pe", lambda e: e.matmul(psSm[:, 128:256], lhsT=BT[:, c * 128:(c + 1) * 128], rhs=CT[:, c * 128:(c + 1) * 128], start=True, stop=True),
                 reads=[dxt[c]], writes=[dCB])
            mt, dmt = MT.next()
            for half in range(2):
                h4 = half * 4
                ed, ded = eds[half]
                S.op("dve", lambda e: e.tensor_tensor(out=mt[:, h4:h4 + 4, :], in0=ed[:], in1=psSm[:, 128:256].unsqueeze(1).to_broadcast([128, 4, 128]),
                                                      op=ALU.mult), reads=[ded, dCB], writes=[dmt])
            xd, dxd = xdt.next(); xw_, dxw = xw.next()
            xt3 = xtok[:, c, :].rearrange("p (h q) -> p h q", h=8)
            S.op("pool", lambda e: e.tensor_tensor(out=xd[:], in0=xt3, in1=dts[:, c, j0:j0 + 8].unsqueeze(2).to_broadcast([128, 8, 64]), op=ALU.mult),
                 reads=[dxt[c], ddt], writes=[dxd])
            for half in range(2):
                h4 = half * 4
                ed, ded = eds[half]
                S.op("pool", lambda e: e.tensor_tensor(out=xw_[:, h4:h4 + 4, :], in0=xd[:, h4:h4 + 4, :],
                                                       in1=ed[:, :, col:col + 1].to_broadcast([128, 4, 64]), op=ALU.mult), reads=[dxd, ded], writes=[dxw])
            for hh in range(8):
                S.op("pe", lambda e: e.matmul(psY[:, hh, :], lhsT=mt[:, hh, :], rhs=xd[:, hh, :], start=True, stop=True), reads=[dmt, dxd], writes=[dY])
            S.op("pe", lambda e: e.matmul(psYo[:], lhsT=CT[:, c * 128:(c + 1) * 128], rhs=Sbf[d][:], start=True, stop=True),
                 reads=[dxt[c], dSbf[d]], writes=[dYo])
            S.op("pe", lambda e: e.matmul(psSp[:], lhsT=btok[:, c, :], rhs=xw_[:], start=True, stop=True), reads=[dxt[c], dxw], writes=[dSp])
            ys, dys = ysb.next(); t, dt_ = tb_.next()
            S.op("act", lambda e: e.copy(out=ys[:], in_=psY[:]), reads=[dY], writes=[dys])
            S.op("dve", lambda e: e.tensor_tensor(out=t[:], in0=psYo[:], in1=ec[:, 0:8].unsqueeze(2).to_broadcast([128, 8, 64]), op=ALU.mult),
                 reads=[dYo, dec], writes=[dt_])
            S.op("pool", lambda e: e.tensor_tensor(out=t[:], in0=t[:], in1=ys[:], op=ALU.add), reads=[dys], writes=[dt_])
            if d == 0:
                u, du = ub_.next()
                S.op("pool", lambda e: e.tensor_tensor(out=u[:], in0=xt3, in1=dsum[:, :].unsqueeze(2).to_broadcast([128, 8, 64]), op=ALU.mult),
                     reads=[dxt[c], dvs], writes=[du])
                S.op("pool", lambda e: e.tensor_tensor(out=t[:], in0=t[:], in1=u[:], op=ALU.add), reads=[du], writes=[dt_])
            t2 = t[:].rearrange("p h q -> p (h q)")
            for cc in range(4):
                S.op("pe", lambda e: e.transpose(out=psT[:, cc, :], in_=t2[:, cc * 128:(cc + 1) * 128], identity=idn), reads=[dt_, dcst], writes=[dT])
            yt, dyt = yts.next()
            S.op("act", lambda e: e.copy(out=yt[:], in_=psT[:]), reads=[dT], writes=[dyt])
            if c in visited:
                pvb, dpv = pv.next()
                S.dma("sp", pvb[:], yov[:, :, c * 128:(c + 1) * 128], reads=[dyg[c]], writes=[dpv])
                S.op("pool", lambda e: e.tensor_tensor(out=yt[:], in0=yt[:], in1=pvb[:], op=ALU.add), reads=[dpv], writes=[dyt])
            visited.add(c)
            S.dma("sp", yov[:, :, c * 128:(c + 1) * 128], yt[:], reads=[dyt], writes=[dyg[c]])
            S.op("dve", lambda e: e.tensor_tensor(out=Sst[d][:], in0=Sst[d][:], in1=ec[:, 8:16].unsqueeze(2).to_broadcast([128, 8, 64]), op=ALU.mult),
                 reads=[dec], writes=[dS[d]])
            S.op("dve", lambda e: e.tensor_tensor(out=Sst[d][:], in0=Sst[d][:], in1=psSp[:], op=ALU.add), reads=[dSp], writes=[dS[d]])
            S.op("act", lambda e: e.copy(out=Sbf[d][:], in_=Sst[d][:].rearrange("p h q -> p (h q)")), reads=[dS[d]], writes=[dSbf[d]])
    return dyg


def build_fused(n_layers=4):
    nc = new_nc()
    S = Sched(nc)
    G = G_()
    npat = len(na_patterns())

    def ext(name, shape, dt=F32):
        return nc.dram_tensor(name, list(shape), dt, kind="ExternalInput").ap()

    def internal(name, shape, dt=F32, shared=False):
        if shared:
            return nc.dram_tensor(name, list(shape), dt, addr_space="Shared").ap()
        return nc.dram_tensor(name, list(shape), dt).ap()

    xT = ext("xT", [D, T]); cid = ext("cid", [1, 2], I32)
    G.cc = ext("cc", [128, 16, 2]); G.aw = ext("aw", [4 * D, MODC]); G.abT = ext("abT", [128, 72])
    G.ng_all = ext("ng_all", [128, 13, 16])
    n_na = (n_layers + 2) // 3
    ffn_w_in = ext("ffn_w_in", [2 * n_layers * D, 2 * DFF]); ffn_w_out = ext("ffn_w_out", [2 * n_layers * DFF, D])
    na_w_qkv = ext("na_w_qkv", [n_na * D, 6144]); na_w_o = ext("na_w_o", [n_na * D, D])
    ssm_w_in = ext("ssm_w_in", [D, 10368] if n_layers > 1 else [128, 128]); ssm_w_out = ext("ssm_w_out", [4096, D] if n_layers > 1 else [128, 128])
    gqa_w_qkv = ext("gqa_w_qkv", [D, 3072] if n_layers > 2 else [128, 128]); gqa_w_o = ext("gqa_w_o", [D, D] if n_layers > 2 else [128, 128])
    na_bm = ext("na_bm", [2 * 2 * npat * 128, 448])
    G.ssm_cw = ext("ssm_cw", [128, 6, 5]); G.ssm_cb = ext("ssm_cb", [128, 6]); G.ssm_vecs = ext("ssm_vecs", [128, 3, 16])
    G.ssm_ng = ext("ssm_ng", [128, 32])
    consts = ext("consts", [128, 7, 128])
    G.qkn = ext("qkn", [128, 2]); G.cosF = ext("cosF", [128, TL]); G.sinF = ext("sinF", [128, TL]); G.rotm = ext("rotm", [128, 128])
    outT = nc.dram_tensor("outT", [D, T], F32, kind="ExternalOutput").ap()

    hbuf = [internal("hA", [D, T]), internal("hB", [D, T])]
    G.modT_local = internal("modT_local", [128, 144]); G.modT_all = internal("modT_all", [NCORES * 128, 144], shared=True)
    qkv_send = internal("qkv_send", [6144, T], BF16); qkv_all = internal("qkv_all", [NCORES * 6144, T], BF16, shared=True)
    o_send = internal("o_send", [OPAD, NTOK]); o_all = internal("o_all", [NCORES * OPAD, NTOK], shared=True)
    ssm_send = internal("ssm_send", [6272, T]); ssm_all = internal("ssm_all", [NCORES * 6272, T], shared=True)
    zT = internal("zT", [4096, T])
    y_send = internal("y_send", [512, NTOK]); y_all = internal("y_all", [NCORES * 512, NTOK], shared=True)
    gqa_send = internal("gqa_send", [3072, T]); gqa_all = internal("gqa_all", [NCORES * 3072, T], shared=True)

    G.ones, G.done = make_consts(S)
    G.cst = S.sb([128, 7, 128], F32); G.dident = Dep()
    S.dma("sp", G.cst[:], consts[:, :, :], writes=[G.dident])
    G.identb = S.sb([128, 128], BF16); G.onesb = S.sb([128, 128], BF16)
    S.op("dve", lambda e: e.tensor_copy(out=G.identb[:], in_=G.cst[:, 4, :]), reads=[G.dident], writes=[G.dident])
    S.op("dve", lambda e: e.tensor_copy(out=G.onesb[:], in_=G.ones[:, :]), reads=[G.done], writes=[G.dident])
    jreg, kvreg = {}, {}
    for e in ("sp", "act", "pool"):
        E = S.engs[e]
        r1 = E.alloc_register("cid_" + e); r2 = E.alloc_register("kv_" + e)
        E.reg_load(r1, cid[0:1, 0:1]); E.reg_load(r2, cid[0:1, 1:2])
        jreg[e] = E.snap(r1, min_val=0, max_val=NCORES - 1)
        kvreg[e] = E.snap(r2, min_val=0, max_val=3)

    qkv_mine = internal("qkv_mine", [NCORES * 768, T], BF16)
    o_mine = internal("o_mine", [D, T])
    ssm_mine = internal("ssm_mine", [NCORES * 784, T])
    y_mine = internal("y_mine", [4096, T])
    q_mine = internal("q_mine", [NCORES * 256, T])
    kv_mine = internal("kv_mine", [NCORES * 256, T])
    J = jreg["sp"]

    def pull_tokens(dst, src_all, nrow_groups, h_all, h_use, reads, dep):
        sv = src_all.rearrange("(r h p) t -> r h p t", r=nrow_groups, h=h_all, p=128)[:, 0:h_use]
        dv = dst.rearrange("(r h p) t -> r h p t", r=nrow_groups, h=h_use, p=128)
        S.dma("sp", dv[:, :, :, 0:TL], sv[:, :, :, 0:SEQ].rearrange("r h p (j t) -> j r h p t", t=TL)[J], reads=reads, writes=[dep])
        S.dma("sp", dv[:, :, :, TL:T], sv[:, :, :, SEQ:NTOK].rearrange("r h p (j t) -> j r h p t", t=TC)[J], reads=reads, writes=[dep])

    def o_src(eng):
        v = o_mine.rearrange("(c p) t -> p c t", p=128)
        return [(0, 16, v[:, :, 0:TL], v[:, :, TL:T])]

    def y_src(eng):
        v = y_mine.rearrange("(c p) t -> p c t", p=128)
        return [(0, 32, v[:, :, 0:TL], v[:, :, TL:T])]

    emit_mod(S, G)
    cur = xT
    hi = 0

    def nxt_buf():
        nonlocal hi
        b = hbuf[hi]
        hi ^= 1
        return b

    for i in range(n_layers):
        last = (i == n_layers - 1)
        nb = nxt_buf()
        emit_ffn(S, G, cur, nb, ffn_w_in[(2 * i) * D:(2 * i + 1) * D, :], ffn_w_out[(2 * i) * DFF:(2 * i + 1) * DFF, :], i, 0)
        cur = nb
        kind, jj = i % 3, i // 3
        dsend = Dep(); dall = Dep()
        if kind == 0:
            def oc_na(oc, st, dst):
                which, head = oc // 16, oc % 16
                row = (head // 2) * 768 + (which * 2 + head % 2) * 128
                S.dma("sp", qkv_send[row:row + 128, :], st[:], reads=[dst], writes=[dsend])
            emit_proj(S, G, cur, na_w_qkv[jj * D:(jj + 1) * D, :], 6144, "na", i, oc_na, st_dt=BF16)
            S.cc("AllGather", [qkv_send], [qkv_all], reads=[dsend], writes=[dall])
            S.dma("sp", qkv_mine.rearrange("(r x) t -> r x t", x=768),
                  qkv_all.rearrange("(r j x) t -> j r x t", r=NCORES, j=NCORES, x=768)[J], reads=[dall], writes=[Dep()])
            S.phase_end()
            S.phase_begin()

            def na_src(eng, hh, w):
                vv = qkv_mine.rearrange("(r w d) t -> r w d t", r=NCORES, w=6, d=128)[:, w * 2 + hh]
                return vv[:, :, 0:TL], vv[:, :, TL:T]
            bm0 = (jj * 2) * npat * 128
            dout = emit_na_core(S, G, na_src, lambda hh: na_bm[bm0 + hh * npat * 128: bm0 + (hh + 1) * npat * 128, :],
                                lambda hh: o_send[hh * 128:(hh + 1) * 128, :])
            S.cc("AllGather", [o_send], [o_all], reads=[dout], writes=[dall])
            pull_tokens(o_mine, o_all, NCORES, 3, 2, [dall], Dep())
            S.phase_end()
            nb = nxt_buf()
            emit_oproj(S, G, o_src, na_w_o[jj * D:(jj + 1) * D, :], cur, nb, i, 16, False)
            cur = nb
        elif kind == 1:
            def oc_ssm(oc, st, dst):
                if oc < 32:
                    S.dma("sp", zT[oc * 128:(oc + 1) * 128, :], st[:], reads=[dst], writes=[dsend])
                elif oc < 64:
                    xch = oc - 32
                    row = (xch // 4) * 784 + (xch % 4) * 128
                    S.dma("sp", ssm_send[row:row + 128, :], st[:], reads=[dst], writes=[dsend])
                elif oc < 80:
                    j = (oc - 64) % 8
                    row = j * 784 + (512 if oc < 72 else 640)
                    S.dma("sp", ssm_send[row:row + 128, :], st[:], reads=[dst], writes=[dsend])
                else:
                    for j in range(NCORES):
                        for d in range(2):
                            row = j * 784 + 768 + d * 8
                            S.dma("sp", ssm_send[row:row + 8, :], st[d * 64 + 8 * j:d * 64 + 8 * j + 8, :], reads=[dst], writes=[dsend])
            emit_proj(S, G, cur, ssm_w_in, 10368, "ssm", i, oc_ssm)
            S.cc("AllGather", [ssm_send], [ssm_all], reads=[dsend], writes=[dall])
            S.dma("sp", ssm_mine.rearrange("(r x) t -> r x t", x=784),
                  ssm_all.rearrange("(r j x) t -> j r x t", r=NCORES, j=NCORES, x=784)[J], reads=[dall], writes=[Dep()])
            S.phase_end()
            S.phase_begin()
            dyg = emit_ssd_core(S, G, lambda eng: ssm_mine.rearrange("(r x) t -> r x t", x=784), y_send)
            S.cc("AllGather", [y_send], [y_all], reads=dyg, writes=[dall])
            pull_tokens(y_mine, y_all, NCORES * 4, 1, 1, [dall], Dep())
            S.phase_end()
            nb = nxt_buf()
            emit_oproj(S, G, y_src, ssm_w_out, cur, nb, i, 32, True, zT=zT)
            cur = nb
        else:
            def oc_gqa(oc, st, dst):
                S.dma("sp", gqa_send[oc * 128:(oc + 1) * 128, :], st[:], reads=[dst], writes=[dsend])
            emit_proj(S, G, cur, gqa_w_qkv, 3072, "gqa", i, oc_gqa)
            S.cc("AllGather", [gqa_send], [gqa_all], reads=[dsend], writes=[dall])
            g3 = gqa_all.rearrange("(r x) t -> r x t", x=3072)
            S.dma("sp", q_mine.rearrange("(r x) t -> r x t", x=256),
                  g3[:, 0:2048, :].rearrange("r (j x) t -> j r x t", j=NCORES)[J], reads=[dall], writes=[Dep()])
            KV = kvreg["sp"]
            for w_ in range(2):
                S.dma("sp", kv_mine.rearrange("(r w d) t -> r w d t", w=2, d=128)[:, w_],
                      g3[:, 2048 + w_ * 512:2048 + (w_ + 1) * 512, :].rearrange("r (k d) t -> k r d t", k=4)[KV], reads=[dall], writes=[Dep()])
            S.phase_end()
            S.phase_begin()

            def src_q(eng, hh):
                v = q_mine.rearrange("(r h d) t -> r h d t", h=2, d=128)[:, hh]
                return v[:, :, 0:TL], v[:, :, TL:T]

            def src_kv(eng, w):
                v = kv_mine.rearrange("(r w d) t -> r w d t", w=2, d=128)[:, w]
                return v[:, :, 0:TL], v[:, :, TL:T]
            dout = emit_gqa_core(S, G, src_q, src_kv, lambda hh: o_send[hh * 128:(hh + 1) * 128, :])
            S.cc("AllGather", [o_send], [o_all], reads=[dout], writes=[dall])
            pull_tokens(o_mine, o_all, NCORES, 3, 2, [dall], Dep())
            S.phase_end()
            nb = nxt_buf()
            emit_oproj(S, G, o_src, gqa_w_o, cur, nb, i, 16, False)
            cur = nb
        nb = outT if last else nxt_buf()
        emit_ffn(S, G, cur, nb, ffn_w_in[(2 * i + 1) * D:(2 * i + 2) * D, :], ffn_w_out[(2 * i + 1) * DFF:(2 * i + 2) * DFF, :], i, 1,
                 final_norm=(i == 3))
        cur = nb
    S.finish()
    print("fused program: inst", S.n_inst, "waits", S.n_wait)
    return nc


def rope_tables():
    t = np.arange(SEQ)
    row = (t // 64).astype(np.float32)
    col = (t % 64).astype(np.float32)
    inv = (np.float32(10000.0) ** (-np.arange(32, dtype=np.float32) / np.float32(32))).astype(np.float32)
    ang = np.concatenate([row[:, None] * inv, col[:, None] * inv], axis=-1).astype(np.float32)
    cos = np.cos(ang).astype(np.float32); sin = np.sin(ang).astype(np.float32)
    cosF = np.ascontiguousarray(np.concatenate([cos, cos], axis=1).T)
    sinF = np.ascontiguousarray(np.concatenate([sin, sin], axis=1).T)
    return cosF, sinF


def rot_matrix():
    m = np.zeros((128, 128), np.float32)
    for i in range(64):
        m[i + 64, i] = -1.0
        m[i, i + 64] = 1.0
    return m


def fm(v):
    v = np.asarray(v, np.float32).reshape(-1, 128)
    return np.ascontiguousarray(v.T)


def to_cores(g):
    return [np.ascontiguousarray(np.concatenate([g[:, j * TL:(j + 1) * TL], g[:, SEQ + j * TC:SEQ + (j + 1) * TC]], axis=1))
            for j in range(NCORES)]


_NC = {}
N_LAYERS = 4


def make_inputs(x, c, ctx, c_ctx, ada_w, ada_b, norm_g, ffn_w_in, ffn_w_out, na_w_qkv, na_rpb, na_w_o,
                ssm_w_in, ssm_conv_w, ssm_conv_b, ssm_a_log, ssm_dt_bias, ssm_d, ssm_norm_g, ssm_w_out,
                gqa_w_qkv, gqa_q_norm, gqa_k_norm, gqa_w_o, final_norm_g):
    f32 = np.float32
    A = lambda v: np.asarray(v, f32)
    x = A(x); ctx = A(ctx); ada_w = A(ada_w); ada_b = A(ada_b); norm_g = A(norm_g)
    xg = np.concatenate([x[0].T, ctx[0].T], axis=1)
    xT = to_cores(xg)
    cc = np.ascontiguousarray(np.stack([A(c).reshape(D), A(c_ctx).reshape(D)]).reshape(2, 16, 128).transpose(2, 1, 0))
    ng_all = np.ascontiguousarray(np.stack([fm(norm_g[i, k]) for i in range(4) for k in range(3)] + [fm(A(final_norm_g))], axis=1))
    shared = {
        "cc": cc, "ng_all": ng_all,
        "ffn_w_in": A(ffn_w_in).reshape(8 * D, 2 * DFF)[:2 * N_LAYERS * D], "ffn_w_out": A(ffn_w_out).reshape(8 * DFF, D)[:2 * N_LAYERS * DFF],
        "na_w_qkv": A(na_w_qkv).reshape(2 * D, 6144)[:((N_LAYERS + 2) // 3) * D], "na_w_o": A(na_w_o).reshape(2 * D, D)[:((N_LAYERS + 2) // 3) * D],
        "ssm_w_in": A(ssm_w_in).reshape(D, 10368) if N_LAYERS > 1 else np.zeros((128, 128), f32),
        "ssm_w_out": A(ssm_w_out).reshape(4096, D) if N_LAYERS > 1 else np.zeros((128, 128), f32),
        "gqa_w_qkv": A(gqa_w_qkv).reshape(D, 3072) if N_LAYERS > 2 else np.zeros((128, 128), f32),
        "gqa_w_o": A(gqa_w_o).reshape(D, D) if N_LAYERS > 2 else np.zeros((128, 128), f32),
        "ssm_ng": fm(A(ssm_norm_g).reshape(4096)), "consts": ssd_consts(),
        "qkn": np.ascontiguousarray(np.stack([A(gqa_q_norm).reshape(128), A(gqa_k_norm).reshape(128)], axis=1)),
        "rotm": rot_matrix(),
    }
    cosF, sinF = rope_tables()
    bmt = [na_bias_tables(A(na_rpb)[jj]) for jj in range(2)]
    npat = bmt[0].shape[1]
    cwf = A(ssm_conv_w)[0]; cbf = A(ssm_conv_b)[0]
    ins = []
    for j in range(NCORES):
        sl = slice(j * MODC, (j + 1) * MODC)
        chans = np.concatenate([np.arange(512 * j, 512 * j + 512), 4096 + np.arange(128 * j, 128 * j + 128),
                                5120 + np.arange(128 * j, 128 * j + 128)])
        hd = np.concatenate([np.arange(8 * j, 8 * j + 8), 64 + np.arange(8 * j, 8 * j + 8)])
        vecs = np.stack([A(ssm_dt_bias)[0].reshape(128)[hd], A(ssm_a_log)[0].reshape(128)[hd], A(ssm_d)[0].reshape(128)[hd]])
        d = dict(shared)
        d.update({
            "xT": xT[j], "cid": np.array([[j, j // 2]], np.int32),
            "aw": np.ascontiguousarray(ada_w[:, :, sl]).reshape(4 * D, MODC),
            "abT": np.ascontiguousarray(ada_b[:, sl].reshape(4, 18, 128).transpose(2, 0, 1)).reshape(128, 72),
            "na_bm": np.ascontiguousarray(np.concatenate([bmt[jj][2 * j:2 * j + 2] for jj in range(2)], axis=0)).reshape(4 * npat * 128, 448),
            "ssm_cw": np.ascontiguousarray(cwf[:, chans].T.reshape(6, 128, 5).transpose(1, 0, 2)),
            "ssm_cb": np.ascontiguousarray(cbf[chans].reshape(6, 128).T),
            "ssm_vecs": np.ascontiguousarray(np.broadcast_to(vecs[None], (128, 3, 16))),
            "cosF": np.ascontiguousarray(cosF[:, j * TL:(j + 1) * TL]), "sinF": np.ascontiguousarray(sinF[:, j * TL:(j + 1) * TL]),
        })
        ins.append(d)
    return ins


def kernel(**inputs):
    ins = make_inputs(**inputs)
    if N_LAYERS not in _NC:
        _NC[N_LAYERS] = build_fused(N_LAYERS)
    res = run_bass_kernel_spmd(_NC[N_LAYERS], ins, core_ids=list(range(NCORES)))
    out = np.concatenate([res.results[j]["outT"][:, :TL].T for j in range(NCORES)], axis=0)
    return np.ascontiguousarray(out[None]).astype(np.float32)
```
